# Optimizing a Trainium2 kernel written in Bass

```python
import math
import jax, jax.numpy as jnp
from jax import lax
import numpy as np

D_MODEL = 2048
BATCH = 4
SEQ = 2048
DEPTH = 1

CHUNK = 64
Q_BLOCK = 128
N_MEM = 256
EPS = 1e-6
ROPE_THETA = 500000.0

FOX_HEADS = 8
FOX_DIM = 128
FOX_W = FOX_HEADS * FOX_DIM

DIFF_HEADS = 4
DIFF_QK_DIM = 64
DIFF_V_DIM = 2 * DIFF_QK_DIM
DIFF_QK_W = DIFF_HEADS * 2 * DIFF_QK_DIM
DIFF_V_W = DIFF_HEADS * DIFF_V_DIM
ROPE_DIM = DIFF_QK_DIM // 4

MEM_HEADS = 4
MEM_DIM = 128
MEM_W = MEM_HEADS * MEM_DIM

D_FF = 5632

IN_SIZES = (FOX_W, FOX_W, FOX_W, FOX_HEADS, DIFF_QK_W, DIFF_QK_W, DIFF_V_W, MEM_W)
D_IN = sum(IN_SIZES)

kernel_name = 'hybrid_gated_fox_diffattn_macaron_block'


def rmsnorm(x, g):
    xf = x.astype(jnp.float32)
    y = xf * lax.rsqrt(jnp.mean(xf * xf, axis=-1, keepdims=True) + EPS)
    return (y * g.astype(jnp.float32)).astype(x.dtype)


def swiglu(h, w_gate, w_up, w_down):
    return (jax.nn.silu(h @ w_gate) * (h @ w_up)) @ w_down


def rope_tables(seq):
    pos = jnp.arange(seq, dtype=jnp.float32)
    inv_freq = ROPE_THETA ** (-jnp.arange(0, ROPE_DIM, 2, dtype=jnp.float32) / ROPE_DIM)
    ang = pos[:, None] * inv_freq[None, :]
    return jnp.cos(ang), jnp.sin(ang)


def partial_rope(x, cos, sin):
    half = ROPE_DIM // 2
    c = cos[None, :, None, :].astype(x.dtype)
    s = sin[None, :, None, :].astype(x.dtype)
    x1 = x[..., :half]
    x2 = x[..., half:ROPE_DIM]
    return jnp.concatenate([x1 * c - x2 * s, x2 * c + x1 * s, x[..., ROPE_DIM:]], axis=-1)


def forgetting_attention(q, k, v, log_f):
    seq = q.shape[1]
    F = jnp.cumsum(log_f, axis=1).transpose(0, 2, 1)
    scale = FOX_DIM ** -0.5
    outs = []
    for i in range(seq // Q_BLOCK):
        lo, hi = i * Q_BLOCK, (i + 1) * Q_BLOCK
        logits = jnp.einsum('bqhd,bkhd->bhqk', q[:, lo:hi], k[:, :hi],
                            preferred_element_type=jnp.float32) * scale
        logits = logits + F[:, :, lo:hi, None] - F[:, :, None, :hi]
        t_idx = jnp.arange(lo, hi)[:, None]
        s_idx = jnp.arange(hi)[None, :]
        logits = jnp.where(s_idx <= t_idx, logits, -jnp.inf)
        p = jax.nn.softmax(logits, axis=-1).astype(v.dtype)
        outs.append(jnp.einsum('bhqk,bkhd->bqhd', p, v[:, :hi]))
    return jnp.concatenate(outs, axis=1)


def differential_attention(q, k, v, lam):
    seq = q.shape[1]
    scale = DIFF_QK_DIM ** -0.5
    outs = []
    for i in range(seq // Q_BLOCK):
        lo, hi = i * Q_BLOCK, (i + 1) * Q_BLOCK
        logits = jnp.einsum('bqhmd,bkhmd->bhmqk', q[:, lo:hi], k[:, :hi],
                            preferred_element_type=jnp.float32) * scale
        q_chunk = jnp.arange(lo, hi)[:, None] // CHUNK
        k_chunk = jnp.arange(hi)[None, :] // CHUNK
        logits = jnp.where(k_chunk <= q_chunk, logits, -jnp.inf)
        p = jax.nn.softmax(logits, axis=-1)
        p_diff = (p[:, :, 0] - lam * p[:, :, 1]).astype(v.dtype)
        outs.append(jnp.einsum('bhqk,bkhe->bqhe', p_diff, v[:, :hi]))
    return jnp.concatenate(outs, axis=1)


def memory_attention(q, k, v):
    logits = jnp.einsum('bqhd,bkhd->bhqk', q, k,
                        preferred_element_type=jnp.float32) * (MEM_DIM ** -0.5)
    p = jax.nn.softmax(logits, axis=-1).astype(v.dtype)
    return jnp.einsum('bhqk,bkhd->bqhd', p, v)


def setup_inputs(seed: int = 0) -> dict:
    key = jax.random.key(seed)
    ks = iter(jax.random.split(key, 48))
    L = DEPTH

    def nrm(shape, fan_in):
        return jax.random.normal(next(ks), shape, jnp.float32) * fan_in ** -0.5

    def gain(n):
        return 1.0 + 0.05 * jax.random.normal(next(ks), (L, n), jnp.float32)

    return {
        'x': jax.random.normal(next(ks), (BATCH, SEQ, D_MODEL), jnp.float32),
        'mem': jax.random.normal(next(ks), (BATCH, N_MEM, D_MODEL), jnp.float32),
        'ffn1_pre_g': gain(D_MODEL),
        'ffn1_w_gate': nrm((L, D_MODEL, D_FF), D_MODEL),
        'ffn1_w_up': nrm((L, D_MODEL, D_FF), D_MODEL),
        'ffn1_w_down': nrm((L, D_FF, D_MODEL), D_FF),
        'ffn1_post_g': gain(D_MODEL),
        'mix_pre_g': gain(D_MODEL),
        'w_in': nrm((L, D_MODEL, D_IN), D_MODEL),
        'fox_f_bias': jax.random.uniform(next(ks), (L, FOX_HEADS), jnp.float32, 1.0, 4.0),
        'diff_lambda_q1': 0.1 * jax.random.normal(next(ks), (L, DIFF_QK_DIM), jnp.float32),
        'diff_lambda_k1': 0.1 * jax.random.normal(next(ks), (L, DIFF_QK_DIM), jnp.float32),
        'diff_lambda_q2': 0.1 * jax.random.normal(next(ks), (L, DIFF_QK_DIM), jnp.float32),
        'diff_lambda_k2': 0.1 * jax.random.normal(next(ks), (L, DIFF_QK_DIM), jnp.float32),
        'diff_head_g': gain(DIFF_V_DIM),
        'mem_norm_g': gain(D_MODEL),
        'w_mem_kv': nrm((L, D_MODEL, 2 * MEM_W), D_MODEL),
        'w_branch_fox': nrm((L, FOX_W, D_MODEL), FOX_W),
        'w_branch_diff': nrm((L, DIFF_V_W, D_MODEL), DIFF_V_W),
        'w_branch_mem': nrm((L, MEM_W, D_MODEL), MEM_W),
        'w_merge_gate': nrm((L, D_MODEL, 3 * D_MODEL), D_MODEL),
        'b_merge_gate': 0.02 * jax.random.normal(next(ks), (L, 3 * D_MODEL), jnp.float32),
        'w_out': nrm((L, D_MODEL, D_MODEL), D_MODEL),
        'mix_post_g': gain(D_MODEL),
        'ffn2_pre_g': gain(D_MODEL),
        'ffn2_w_gate': nrm((L, D_MODEL, D_FF), D_MODEL),
        'ffn2_w_up': nrm((L, D_MODEL, D_FF), D_MODEL),
        'ffn2_w_down': nrm((L, D_FF, D_MODEL), D_FF),
        'ffn2_post_g': gain(D_MODEL),
    }


def reference(x, mem, ffn1_pre_g, ffn1_w_gate, ffn1_w_up, ffn1_w_down, ffn1_post_g,
              mix_pre_g, w_in, fox_f_bias, diff_lambda_q1, diff_lambda_k1, diff_lambda_q2,
              diff_lambda_k2, diff_head_g, mem_norm_g, w_mem_kv, w_branch_fox, w_branch_diff,
              w_branch_mem, w_merge_gate, b_merge_gate, w_out, mix_post_g,
              ffn2_pre_g, ffn2_w_gate, ffn2_w_up, ffn2_w_down, ffn2_post_g):
    b, seq, _ = x.shape
    n_mem = mem.shape[1]
    cos, sin = rope_tables(seq)
    split_points = [int(p) for p in np.cumsum(IN_SIZES)[:-1]]

    for l in range(DEPTH):
        h = rmsnorm(x, ffn1_pre_g[l])
        x = x + 0.5 * rmsnorm(swiglu(h, ffn1_w_gate[l], ffn1_w_up[l], ffn1_w_down[l]), ffn1_post_g[l])

        h = rmsnorm(x, mix_pre_g[l])
        proj = h @ w_in[l]
        fq, fk, fv, ff, dq, dk, dv, mq = jnp.split(proj, split_points, axis=-1)

        fq = fq.reshape(b, seq, FOX_HEADS, FOX_DIM)
        fk = fk.reshape(b, seq, FOX_HEADS, FOX_DIM)
        fv = fv.reshape(b, seq, FOX_HEADS, FOX_DIM)
        log_f = jax.nn.log_sigmoid(ff.astype(jnp.float32) + fox_f_bias[l].astype(jnp.float32))
        y_fox = forgetting_attention(fq, fk, fv, log_f).reshape(b, seq, FOX_W)

        dq = partial_rope(dq.reshape(b, seq, 2 * DIFF_HEADS, DIFF_QK_DIM), cos, sin)
        dk = partial_rope(dk.reshape(b, seq, 2 * DIFF_HEADS, DIFF_QK_DIM), cos, sin)
        dq = dq.reshape(b, seq, DIFF_HEADS, 2, DIFF_QK_DIM)
        dk = dk.reshape(b, seq, DIFF_HEADS, 2, DIFF_QK_DIM)
        dv = dv.reshape(b, seq, DIFF_HEADS, DIFF_V_DIM)
        lam_init = 0.8 - 0.6 * math.exp(-0.3 * l)
        lam = (jnp.exp(jnp.sum(diff_lambda_q1[l].astype(jnp.float32) * diff_lambda_k1[l].astype(jnp.float32)))
               - jnp.exp(jnp.sum(diff_lambda_q2[l].astype(jnp.float32) * diff_lambda_k2[l].astype(jnp.float32)))
               + lam_init)
        yd = differential_attention(dq, dk, dv, lam)
        yd = rmsnorm(yd, diff_head_g[l]) * (1.0 - lam_init)
        y_diff = yd.reshape(b, seq, DIFF_V_W)

        mh = rmsnorm(mem, mem_norm_g[l])
        mk, mv = jnp.split(mh @ w_mem_kv[l], 2, axis=-1)
        mk = mk.reshape(b, n_mem, MEM_HEADS, MEM_DIM)
        mv = mv.reshape(b, n_mem, MEM_HEADS, MEM_DIM)
        mq = mq.reshape(b, seq, MEM_HEADS, MEM_DIM)
        y_mem = memory_attention(mq, mk, mv).reshape(b, seq, MEM_W)

        gates = jax.nn.sigmoid(h @ w_merge_gate[l] + b_merge_gate[l])
        g_fox, g_diff, g_mem = jnp.split(gates, 3, axis=-1)
        merged = (g_fox * (y_fox @ w_branch_fox[l])
                  + g_diff * (y_diff @ w_branch_diff[l])
                  + g_mem * (y_mem @ w_branch_mem[l]))
        x = x + rmsnorm(merged @ w_out[l], mix_post_g[l])

        h = rmsnorm(x, ffn2_pre_g[l])
        x = x + 0.5 * rmsnorm(swiglu(h, ffn2_w_gate[l], ffn2_w_up[l], ffn2_w_down[l]), ffn2_post_g[l])

    return x
```

```python
import numpy as np
import ml_dtypes
from contextlib import ExitStack
import concourse.bass as bass
import concourse.mybir as mybir
from concourse.bass_utils import run_bass_kernel_spmd

F32 = mybir.dt.float32
BF16 = mybir.dt.bfloat16
AF = mybir.ActivationFunctionType
ALU = mybir.AluOpType
AX = mybir.AxisListType

D = 2048
DFF = 5632
T = 1024
NTB = 8
KD = 16
NF = 44
EPS = 1e-6
SAME_ENG_SYNC = True

SB_BASE = 16512


class Buf:
    __slots__ = ("name", "w", "r")

    def __init__(self, name):
        self.name = name
        self.w = {}
        self.r = {}


class DSem:
    def __init__(self, handle, key):
        self.h = handle
        self.key = key
        self.count = 0


def inherit(new_bufs, old_bufs):
    acc = {}
    for b in old_bufs:
        _merge(acc, b.w)
        _merge(acc, b.r)
    for nb in new_bufs:
        _merge(nb.w, acc)


def _merge(dst, src):
    for k, v in src.items():
        if dst.get(k, 0) < v:
            dst[k] = v


class Prog:
    ENGS = ("pe", "act", "dve", "pool", "sp")

    def __init__(self, nc, stack):
        self.nc = nc
        self.stack = stack
        self.q = {e: [] for e in self.ENGS}
        self.cnt = {e: 0 for e in self.ENGS}
        self.handles = {}
        for e in ("pe", "act", "dve", "pool"):
            self.handles[e] = stack.enter_context(nc.semaphore("s_" + e))
        self.seen = {e: {} for e in self.ENGS}
        self.ndsem = 0
        self.pe_pending = False
        self.nwaits = 0

    def dsem(self, name):
        h = self.stack.enter_context(self.nc.semaphore("d_" + name))
        key = "d_%s_%d" % (name, self.ndsem)
        self.ndsem += 1
        self.handles[key] = h
        return DSem(h, key)

    def _deps(self, eng, reads, writes, ow=()):
        deps = {}
        for b in reads:
            _merge(deps, b.w)
        for b in writes:
            _merge(deps, b.w)
            _merge(deps, b.r)
        if ow:
            d2 = {}
            for b in ow:
                _merge(d2, b.w)
                _merge(d2, b.r)
            d2.pop(eng, None)
            _merge(deps, d2)
        if eng == "pe" or not SAME_ENG_SYNC:
            deps.pop(eng, None)
        waits = []
        seen = self.seen[eng]
        for k, v in deps.items():
            if seen.get(k, 0) < v:
                seen[k] = v
                waits.append((self.handles[k], v))
                if k == "pe" and eng != "pe":
                    assert v <= self.cnt["pe"], "wait on unflagged PE op"
        return waits

    def op(self, eng, fn, reads=(), writes=(), ow=(), flag=True):
        waits = self._deps(eng, reads, writes, ow)
        if ow:
            writes = list(writes) + list(ow)
        if eng == "pe" and not flag:
            tokv = self.cnt[eng] + 1
            self.pe_pending = True
        else:
            self.cnt[eng] += 1
            tokv = self.cnt[eng]
            if eng == "pe":
                self.pe_pending = False
        sem = self.handles[eng]
        self.nwaits += len(waits)

        def run(e, waits=waits, fn=fn, flag=flag, sem=sem):
            for h, v in waits:
                e.wait_ge(h, v)
            ins = fn(e)
            if flag:
                ins.then_inc(sem, 1)

        self.q[eng].append(run)
        for b in reads:
            if b.r.get(eng, 0) < tokv:
                b.r[eng] = tokv
        for b in writes:
            if b.w.get(eng, 0) < tokv:
                b.w[eng] = tokv
            b.r = {}

    def dma(self, eng, out, in_, ds, reads=(), writes=(), **kw):
        waits = self._deps(eng, reads, writes)
        ds.count += 16
        tokv = ds.count
        self.nwaits += len(waits)

        def run(e, waits=waits, out=out, in_=in_, h=ds.h, kw=kw):
            for hh, v in waits:
                e.wait_ge(hh, v)
            e.dma_start(out=out, in_=in_, **kw).then_inc(h, 16)

        self.q[eng].append(run)
        for b in reads:
            if b.r.get(ds.key, 0) < tokv:
                b.r[ds.key] = tokv
        for b in writes:
            if b.w.get(ds.key, 0) < tokv:
                b.w[ds.key] = tokv
            b.r = {}

    def aop(self, eng, fn, ds, inc, reads=(), writes=()):
        waits = self._deps(eng, reads, writes)
        ds.count += inc
        tokv = ds.count

        def run(e, waits=waits, fn=fn, h=ds.h, inc=inc):
            for hh, v in waits:
                e.wait_ge(hh, v)
            fn(e).then_inc(h, inc)

        self.q[eng].append(run)
        for b in reads:
            if b.r.get(ds.key, 0) < tokv:
                b.r[ds.key] = tokv
        for b in writes:
            if b.w.get(ds.key, 0) < tokv:
                b.w[ds.key] = tokv
            b.r = {}

    def join(self, ds, bufs):
        for b in bufs:
            for dd in (b.r, b.w):
                if ds.key in dd:
                    dd[ds.key] = ds.count

    def raw(self, eng, fn, reads=(), writes=()):
        waits = self._deps(eng, reads, writes)

        def run(e, waits=waits, fn=fn):
            for h, v in waits:
                e.wait_ge(h, v)
            if fn is not None:
                fn(e)

        self.q[eng].append(run)

    def emit_all(self):
        nc = self.nc
        q = self.q
        with nc.Block() as block:
            @block.tensor
            def _(e):
                for f in q["pe"]:
                    f(e)

            @block.scalar
            def _(e):
                for f in q["act"]:
                    f(e)

            @block.vector
            def _(e):
                for f in q["dve"]:
                    f(e)

            @block.gpsimd
            def _(e):
                for f in q["pool"]:
                    f(e)

            @block.sync
            def _(e):
                for f in q["sp"]:
                    f(e)


class Slot:
    def __init__(self, P, name, off, nbytes):
        self.buf = Buf(name)
        self.ds = P.dsem(name)
        self.off = off
        self.nbytes = nbytes
        self.name = name


class WStream:
    def __init__(self, P):
        self.P = P
        self.steps = []

    def add(self, slot, dma_fn, compute_fn):
        self.steps.append((slot, dma_fn, compute_fn))

    def run(self):
        steps = self.steps
        n = len(steps)
        prev_user = [-1] * n
        last = {}
        for i, (slot, _, _) in enumerate(steps):
            if slot is not None:
                prev_user[i] = last.get(id(slot), -1)
                last[id(slot)] = i
        nd = 0
        for i in range(n):
            while nd < n and prev_user[nd] < i:
                slot, dma_fn, _ = steps[nd]
                if dma_fn is not None:
                    dma_fn()
                nd += 1
            assert nd > i
            steps[i][2]()
        self.steps = []


C_FQ, C_FK, C_FV, C_FF, C_DQ, C_DK, C_DV, C_MQ = 0, 1024, 2048, 3072, 3080, 3592, 4104, 4616
D_IN = 5128
LAM_INIT = 0.8 - 0.6
RG = [[0, 1], [2, 3], [4, 5], [6, 7]]


class Builder:
    def __init__(self, stage, dbg=()):
        self.stage = stage
        self.dbg = dbg
        self.nc = bass.Bass("TRN2", target_bir_lowering=False)
        self.stack = ExitStack()
        self.P = Prog(self.nc, self.stack)
        self.nsb = 0
        self.dbg_out = {}

    def sb(self, name, shape, dtype, off):
        self.nsb += 1
        h = self.nc.alloc_sbuf_tensor_at("%s_%d" % (name, self.nsb), list(shape), dtype, offset=SB_BASE + off)
        return h.ap()

    def din(self, name, shape, dtype=F32):
        return self.nc.dram_tensor(name, list(shape), dtype, kind="ExternalInput").ap()

    def dscr(self, name, shape, dtype):
        return self.nc.dram_tensor(name, list(shape), dtype).ap()

    def dump(self, name, ap, bufs, shape, dtype):
        if name not in self.dbg:
            return
        o = self.nc.dram_tensor("dbg_" + name, list(shape), dtype, kind="ExternalOutput").ap()
        self.dbg_out[name] = (list(shape), dtype)
        self.P.dma("sp", o, ap, self.dsout, reads=bufs)

    def build(self):
        nc = self.nc
        P = self.P
        self.x_in = self.din("x", [T, D])
        self.mem_in = self.din("mem", [256, D])
        self.out = nc.dram_tensor("out", [T, D], F32, kind="ExternalOutput").ap()
        self.w = {}
        for nm, shp in (("ffn1_pre_g", [1, D]), ("ffn1_w_gate", [D, DFF]), ("ffn1_w_up", [D, DFF]),
                        ("ffn1_w_down", [DFF, D]), ("ffn1_post_g", [1, D]),
                        ("mix_pre_g", [1, D]), ("w_in", [D, D_IN]), ("fox_f_bias", [1, 8]),
                        ("diff_lambda_q1", [1, 64]), ("diff_lambda_k1", [1, 64]),
                        ("diff_lambda_q2", [1, 64]), ("diff_lambda_k2", [1, 64]),
                        ("diff_head_g", [1, 128]), ("mem_norm_g", [1, D]), ("w_mem_kv", [D, 1024]),
                        ("w_branch_fox", [1024, D]), ("w_branch_diff", [512, D]), ("w_branch_mem", [512, D]),
                        ("w_merge_gate", [D, 3 * D]), ("b_merge_gate", [1, 3 * D]), ("w_out", [D, D]),
                        ("mix_post_g", [1, D]),
                        ("ffn2_pre_g", [1, D]), ("ffn2_w_gate", [D, DFF]), ("ffn2_w_up", [D, DFF]),
                        ("ffn2_w_down", [DFF, D]), ("ffn2_post_g", [1, D])):
            self.w[nm] = self.din(nm, shp)
        self.ident_in = self.din("c_ident", [128, 128], BF16)
        self.cmat_in = self.din("c_mat", [128, 3, 128], F32)
        self.masks_in = self.din("c_masks", [128, 4, 128], BF16)
        self.rope_in = self.din("c_rope", [128, 2, NTB, 8], F32)
        self.psel_in = self.din("c_psel", [128, 2], F32)
        self.xsp1 = self.dscr("xsp1", [T, D], F32)
        self.xsp2 = self.dscr("xsp2", [T, D], F32)
        self.XKs = self.dscr("XKs", [1024, 1024], BF16); self.XKd = self.dscr("XKd", [2048, 1024], BF16)
        self.XVs = self.dscr("XVs", [128, 8192], BF16); self.XVd = self.dscr("XVd", [256, 8192], BF16)
        self.XDKs = self.dscr("XDKs", [512, 1024], BF16); self.XDKd = self.dscr("XDKd", [1024, 1024], BF16)
        self.XDVs = self.dscr("XDVs", [128, 4096], BF16); self.XDVd = self.dscr("XDVd", [256, 4096], BF16)
        self.XLs = self.dscr("XLs", [128, 64], F32); self.XLd = self.dscr("XLd", [256, 64], F32)

        o = 0
        self.offA = o
        self.A = self.sb("A", [128, NTB, D], F32, o); o += NTB * D * 4
        self.hT = self.sb("hT", [128, KD, T], BF16, o); o += KD * T * 2
        self.offB = o
        self.actT = self.sb("actT", [128, 22, T], BF16, o); o += 22 * T * 2
        self.offWA = o
        o += 2 * 16384
        self.offWB = o
        o += 4 * 2048
        self.GB = self.sb("GB", [128, D], F32, o); o += D * 4
        self.offSIL = o
        self.SIL = [self.sb("SIL%d" % i, [128, T], F32, o + i * 4096) for i in range(2)]; o += 8192
        self.ident = self.sb("ident", [128, 128], BF16, o); o += 256
        self.stat = self.sb("stat", [128, 64], F32, o); o += 256
        self.cst = self.sb("cst", [128, 8], F32, o); o += 32
        self.offX = o
        self.biasT = self.sb("biasT", [128, 16, 8, 8], F32, o); o += 4096
        self.masks = self.sb("masks", [128, 4, 128], BF16, o); o += 1024
        self.rope = self.sb("rope", [128, 2, NTB, 8], F32, o); o += 512
        self.onesb = self.sb("onesb", [128, 128], BF16, o); o += 256
        self.bmT = self.sb("bmT", [128, 48], F32, o); o += 192
        self.psel = self.sb("psel", [128, 2], F32, o); o += 32
        self.lamv = self.sb("lamv", [128, 8], F32, o); o += 32
        self.fbias = self.sb("fbias", [128, 8], F32, o); o += 32
        self.Dtab = self.sb("Dtab", [128, 128], F32, o); o += 512
        self.Dend = self.sb("Dend", [128, 64], F32, o); o += 256
        self.wff = self.sb("wff", [128, KD, 8], BF16, o); o += 256
        assert SB_BASE + o <= 229344, o
        self.hb = [self.sb("hb%d" % i, [128, D], BF16, self.offB + i * 4096) for i in range(2)]
        self.xr = [self.sb("xr%d" % i, [128, D], F32, self.offB + 8192 + i * 8192) for i in range(2)]
        self.junk = self.sb("junk", [128, D], BF16, self.offB + 8192 + 16384)

        self.ps = nc.alloc_psum_tensor("ps", [128, 8, 512], F32).ap()
        self.bank = [Buf("bank%d" % i) for i in range(8)]

        self.bA = [Buf("A%d" % i) for i in range(NTB)]
        self.bhT = Buf("hT")
        self.bact = [Buf("act%d" % i) for i in range(22)]
        self.rB = self.bact
        self.bGB = Buf("GB")
        self.bSIL = [Buf("SIL0"), Buf("SIL1")]
        self.bhb = [Buf("hb0"), Buf("hb1")]
        self.bxr = [Buf("xr0"), Buf("xr1")]
        self.bjunk = Buf("junk")
        self.bstat = [Buf("stat%d" % i) for i in range(64)]
        self.bconst = Buf("const")
        self.dsmisc = P.dsem("misc")
        self.dsx = P.dsem("x")
        self.dsxr = [P.dsem("xr0"), P.dsem("xr1")]
        self.dsgb = P.dsem("gb")
        self.dsout = P.dsem("out")
        self.ringA = [Slot(P, "ra%d" % i, self.offWA + i * 16384, 16384) for i in range(2)]
        self.ringB = [Slot(P, "rb%d" % i, self.offWB + i * 2048, 2048) for i in range(4)]
        self.ra_i = 0
        self.rb_i = 0
        self.ws = WStream(P)
        self.chunk_ctr = 0

        P.dma("sp", self.ident, self.ident_in, self.dsmisc, writes=[self.bconst])
        P.op("dve", lambda e: e.memset(self.cst[:, 0:1], EPS), writes=[self.bconst])
        P.op("dve", lambda e: e.memset(self.cst[:, 1:2], 1.0), writes=[self.bconst])
        P.op("dve", lambda e: e.memset(self.cst[:, 2:3], 0.0), writes=[self.bconst])
        P.op("dve", lambda e: e.memset(self.cst[:, 3:4], 4.0 * EPS), writes=[self.bconst])
        P.op("dve", lambda e: e.memset(self.onesb, 1.0), writes=[self.bconst])
        self.epsb = self.cst[:, 0:1]
        self.oneb = self.cst[:, 1:2]
        if self.stage >= 2:
            P.dma("sp", self.masks, self.masks_in, self.dsmisc, writes=[self.bconst])
            P.dma("sp", self.rope, self.rope_in, self.dsmisc, writes=[self.bconst])
            P.dma("sp", self.psel, self.psel_in, self.dsmisc, writes=[self.bconst])
            P.dma("sp", self.fbias, self.w["fox_f_bias"][0:1, :].partition_broadcast(128)[:, 0, :], self.dsmisc,
                  writes=[self.bconst])
            P.dma("sp", self.bmT, self.w["b_merge_gate"].rearrange("o (c p) -> p (o c)", p=128), self.dsmisc,
                  writes=[self.bconst], allow_slow_non_contiguous=True)
            P.dma("sp", self.lamv[:, 6:7], self.w["diff_head_g"].rearrange("o p -> p o"), self.dsmisc,
                  writes=[self.bconst], allow_slow_non_contiguous=True)
            self.dswff = P.dsem("wff")
            self.bwff = Buf("wff")
            P.dma("pool", self.wff, self.w["w_in"].rearrange("(k p) n -> p k n", p=128)[:, :, C_FF:C_FF + 8],
                  self.dswff, writes=[self.bwff], allow_slow_non_contiguous=True)
        P.join(self.dsmisc, [self.bconst])

        for tb in range(NTB):
            P.dma("sp", self.A[:, tb, :], self.x_in[tb * 128:(tb + 1) * 128, :], self.dsx, writes=[self.bA[tb]])
        P.join(self.dsx, self.bA)
        self.prenorm_A("ffn1_pre_g")
        self.ffn_core("ffn1_")
        self.postnorm("ffn1_post_g", 0.5, self.x_in, None, final=False)
        if self.stage >= 2:
            self.mixer()
        if self.stage >= 3:
            self.prenorm_A("ffn2_pre_g")
            bsp = self.spill(self.xsp2, "sp2")
            self.ffn_core("ffn2_")
            self.postnorm("ffn2_post_g", 0.5, self.xsp2, bsp, final=True)
        else:
            for tb in range(NTB):
                P.dma("sp", self.out[tb * 128:(tb + 1) * 128, :], self.A[:, tb, :], self.dsout, reads=[self.bA[tb]])
        dso = self.dsout
        P.q["sp"].append(lambda e: e.wait_ge(dso.h, dso.count))
        P.emit_all()
        return nc

    def next_ra(self):
        s = self.ringA[self.ra_i % 2]
        self.ra_i += 1
        return s

    def next_rb(self):
        s = self.ringB[self.rb_i % 4]
        self.rb_i += 1
        return s

    def set_rB(self, new_bufs):
        inherit(new_bufs, list(self.rB) + self.bhb + self.bxr + [self.bjunk])
        self.rB = new_bufs

    def spill(self, dst, name):
        P = self.P
        ds = P.dsem(name)
        bsp = Buf(name)
        for tb in range(NTB):
            P.dma("sp", dst[tb * 128:(tb + 1) * 128, :], self.A[:, tb, :], ds, reads=[self.bA[tb]], writes=[bsp])
        P.join(ds, self.bA + [bsp])
        return bsp

    def load_gain(self, gname):
        src = self.w[gname][0:1, :].partition_broadcast(128)
        self.P.dma("sp", self.GB, src[:, 0, :], self.dsgb, writes=[self.bGB])

    def rstd_col(self, col, dim, coef=1.0):
        P = self.P
        st = self.stat[:, col:col + 1]
        bias = self.epsb if coef == 1.0 else self.cst[:, 3:4]
        assert coef in (1.0, 0.5)
        P.op("act", lambda e: e.activation(out=st, in_=st, func=AF.Sqrt, bias=bias, scale=1.0 / (dim * coef * coef)),
             reads=[self.bconst, self.bstat[col]], writes=[self.bstat[col]])
        P.op("dve", lambda e: e.reciprocal(out=st, in_=st), reads=[self.bstat[col]], writes=[self.bstat[col]])

    def prenorm(self, nblk, src_fn, gname, dst_fn, dstbuf):
        P = self.P
        self.load_gain(gname)
        P.op("dve", lambda e: e.memset(self.stat[:, 0:nblk], 0.0), writes=self.bstat[0:nblk])

        def stage_a(tb):
            sap, sbuf = src_fn(tb)
            P.op("act", lambda e: e.activation(out=self.junk, in_=sap, func=AF.Square,
                                               accum_out=self.stat[:, tb:tb + 1]),
                 reads=[sbuf], writes=[self.bstat[tb]], ow=[self.bjunk] + list(self.rB))
            self.rstd_col(tb, D)
            hb = self.hb[tb % 2]
            bhb = self.bhb[tb % 2]
            P.op("dve", lambda e: e.scalar_tensor_tensor(
                out=hb, in0=sap, scalar=self.stat[:, tb:tb + 1], in1=self.GB,
                op0=ALU.mult, op1=ALU.mult),
                reads=[sbuf, self.bstat[tb], self.bGB], ow=[bhb] + list(self.rB))

        def stage_b(tb):
            hb = self.hb[tb % 2]
            bhb = self.bhb[tb % 2]
            for half in range(2):
                bk = (2 * tb + half) % 8
                pst = self.ps[:, bk, :].bitcast(BF16)
                for kk in range(8):
                    k = half * 8 + kk
                    P.op("pe", lambda e, pst=pst, kk=kk, k=k: e.transpose(
                        out=pst[:, kk * 128:(kk + 1) * 128], in_=hb[:, k * 128:(k + 1) * 128], identity=self.ident),
                        reads=[bhb, self.bconst], writes=[self.bank[bk]], flag=(kk == 7))
                dst = dst_fn(tb, half)
                src = pst.rearrange("p (k n) -> p k n", k=8)
                if half == 0:
                    P.op("act", lambda e, dst=dst, src=src: e.copy(out=dst, in_=src),
                         reads=[self.bank[bk]], ow=[dstbuf])
                else:
                    P.op("dve", lambda e, dst=dst, src=src: e.tensor_copy(out=dst, in_=src),
                         reads=[self.bank[bk]], ow=[dstbuf])

        stage_a(0)
        for tb in range(nblk):
            if tb + 1 < nblk:
                stage_a(tb + 1)
            stage_b(tb)

    def prenorm_A(self, gname):
        self.prenorm(NTB, lambda tb: (self.A[:, tb, :], self.bA[tb]), gname,
                     lambda tb, half: self.hT[:, half * 8:(half + 1) * 8, tb * 128:(tb + 1) * 128], self.bhT)

    def postnorm(self, gname, coef, src, bsrc, final):
        P = self.P
        self.load_gain(gname)
        P.op("dve", lambda e: e.memset(self.stat[:, 8:16], 0.0), writes=self.bstat[8:16])
        for tb in range(NTB):
            c = 8 + tb
            xr = self.xr[tb % 2]
            bxr = self.bxr[tb % 2]
            P.dma("sp", xr, src[tb * 128:(tb + 1) * 128, :], self.dsxr[tb % 2],
                  reads=([bsrc] if bsrc is not None else []), writes=[bxr] + list(self.rB))
            P.op("act", lambda e, tb=tb, c=c: e.activation(out=self.junk, in_=self.A[:, tb, :], func=AF.Square,
                                                        accum_out=self.stat[:, c:c + 1]),
                 reads=[self.bA[tb]], writes=[self.bstat[c]], ow=[self.bjunk] + list(self.rB))
            self.rstd_col(c, D, coef)
            P.op("dve", lambda e, tb=tb: e.tensor_tensor(out=self.A[:, tb, :], in0=self.A[:, tb, :], in1=self.GB,
                                                          op=ALU.mult),
                 reads=[self.bGB, self.bA[tb]], writes=[self.bA[tb]])
            P.op("dve", lambda e, tb=tb, xr=xr, c=c: e.scalar_tensor_tensor(
                out=self.A[:, tb, :], in0=self.A[:, tb, :], scalar=self.stat[:, c:c + 1], in1=xr,
                op0=ALU.mult, op1=ALU.add),
                reads=[bxr, self.bstat[c], self.bA[tb]], writes=[self.bA[tb]])
            if final:
                P.dma("sp", self.out[tb * 128:(tb + 1) * 128, :], self.A[:, tb, :], self.dsout, reads=[self.bA[tb]])

    def tm_steps(self, wview, c0, lhs_fn, nk, evac_fn, kbase=0):
        P = self.P
        assert nk % 2 == 0
        for kb in range(nk // 2):
            slot = self.next_rb()
            tile = self.sb("wtm", [128, 2, 512], BF16, slot.off)
            kg = kbase + kb * 2

            def dma_fn(slot=slot, tile=tile, kg=kg):
                P.dma("pool", tile, wview[:, kg:kg + 2, c0:c0 + 512], slot.ds, writes=[slot.buf])

            def comp_fn(slot=slot, tile=tile, kb=kb):
                for kk in range(2):
                    kl = kb * 2 + kk
                    for tb in range(NTB):
                        lap, lbuf = lhs_fn(kl, tb)
                        last = (kl == nk - 1)
                        P.op("pe", lambda e, kk=kk, kl=kl, tb=tb, lap=lap, last=last: e.matmul(
                            self.ps[:, tb, :], lhsT=lap, rhs=tile[:, kk, :], start=(kl == 0), stop=last),
                            reads=[slot.buf, lbuf], writes=[self.bank[tb]], flag=last or (kk == 1 and tb == NTB - 1))
                if kb == nk // 2 - 1:
                    for tb in range(NTB):
                        evac_fn(tb)

            self.ws.add(slot, dma_fn, comp_fn)

    def fm_steps(self, wview, c0, nch, rhs, rhs_buf, evac_fn, nk=KD, ntg=2, wtile_cols=None):
        P = self.P
        slot = self.next_ra()
        tile = self.sb("wfm", [128, nk, nch * 128], BF16, slot.off)

        def dma_fn(slot=slot, tile=tile):
            P.dma("pool", tile, wview[:, 0:nk, c0:c0 + nch * 128], slot.ds, writes=[slot.buf])

        def comp_fn(slot=slot, tile=tile):
            for c in range(nch):
                pair = (self.chunk_ctr % 4) * 2
                self.chunk_ctr += 1
                for k in range(nk):
                    for tg in range(ntg):
                        bk = pair + tg
                        last = (k == nk - 1 and tg == ntg - 1)
                        ncols = 512 if ntg == 2 else rhs.shape[2]
                        P.op("pe", lambda e, bk=bk, k=k, tg=tg, c=c, ncols=ncols: e.matmul(
                            self.ps[:, bk, 0:ncols],
                            lhsT=tile[:, k, c * 128:(c + 1) * 128],
                            rhs=rhs[:, k, tg * 512:(tg + 1) * 512] if ntg == 2 else rhs[:, k, :],
                            start=(k == 0), stop=(k == nk - 1)),
                            reads=[slot.buf, rhs_buf], writes=[self.bank[bk]], flag=last)
                evac_fn(c, pair)

        self.ws.add(slot, dma_fn, comp_fn)

    def ffn_core(self, pre):
        P = self.P
        wgv = self.w[pre + "w_gate"].rearrange("(k p) n -> p k n", p=128)
        wuv = self.w[pre + "w_up"].rearrange("(k p) n -> p k n", p=128)
        wdv = self.w[pre + "w_down"].rearrange("(k p) n -> p k n", p=128)
        ws = self.ws
        ctr = [0]
        for half in range(2):
            for fg in range(11):
                slot = self.next_ra()
                tile = self.sb("wgu", [128, 2, KD, 256], BF16, slot.off)
                c0 = (half * 22 + fg * 2) * 128

                def dma_fn(slot=slot, tile=tile, c0=c0):
                    P.dma("pool", tile[:, 0], wgv[:, :, c0:c0 + 256], slot.ds, writes=[slot.buf])
                    P.dma("pool", tile[:, 1], wuv[:, :, c0:c0 + 256], slot.ds, writes=[slot.buf])

                def comp_fn(slot=slot, tile=tile, fg=fg):
                    for fi in range(2):
                        fl = fg * 2 + fi
                        par = ctr[0] % 2
                        ctr[0] += 1
                        b0 = par * 4
                        for gu in range(2):
                            for k in range(KD):
                                for tg in range(2):
                                    bk = b0 + gu * 2 + tg
                                    last = (k == KD - 1 and tg == 1)
                                    P.op("pe", lambda e, bk=bk, gu=gu, k=k, tg=tg, fi=fi: e.matmul(
                                        self.ps[:, bk, :], lhsT=tile[:, gu, k, fi * 128:(fi + 1) * 128],
                                        rhs=self.hT[:, k, tg * 512:(tg + 1) * 512],
                                        start=(k == 0), stop=(k == KD - 1)),
                                        reads=[slot.buf, self.bhT], writes=[self.bank[bk]], flag=last)
                        sil = self.SIL[par]
                        bsil = self.bSIL[par]
                        pg = self.ps[:, b0:b0 + 2, :].rearrange("p a n -> p (a n)")
                        pu = self.ps[:, b0 + 2:b0 + 4, :].rearrange("p a n -> p (a n)")
                        P.op("act", lambda e, sil=sil, pg=pg: e.activation(out=sil, in_=pg, func=AF.Silu),
                             reads=[self.bank[b0], self.bank[b0 + 1]], ow=[bsil])
                        P.op("dve", lambda e, sil=sil, pu=pu, fl=fl: e.tensor_tensor(
                            out=self.actT[:, fl, :], in0=pu, in1=sil, op=ALU.mult),
                            reads=[bsil, self.bank[b0 + 2], self.bank[b0 + 3]], ow=[self.bact[fl]])

                ws.add(slot, dma_fn, comp_fn)
            for ng in range(4):
                def evac_fn(tb, ng=ng, half=half):
                    ydst = self.A[:, tb, ng * 512:(ng + 1) * 512]
                    if half == 0:
                        if tb % 2 == 0:
                            P.op("act", lambda e: e.copy(out=ydst, in_=self.ps[:, tb, :]),
                                 reads=[self.bank[tb]], ow=[self.bA[tb]])
                        else:
                            P.op("dve", lambda e: e.tensor_copy(out=ydst, in_=self.ps[:, tb, :]),
                                 reads=[self.bank[tb]], ow=[self.bA[tb]])
                    else:
                        P.op("dve", lambda e: e.tensor_tensor(out=ydst, in0=self.ps[:, tb, :], in1=ydst, op=ALU.add),
                             reads=[self.bank[tb], self.bA[tb]], writes=[self.bA[tb]])

                self.tm_steps(wdv, ng * 512, lambda kl, tb: (self.actT[:, kl, tb * 128:(tb + 1) * 128], self.bact[kl]),
                              22, evac_fn, kbase=half * 22)
        ws.run()

    def mixer(self):
        P = self.P
        nc = self.nc
        A0 = self.offA
        B0 = self.offB
        self.prenorm_A("mix_pre_g")
        bsp1 = self.spill(self.xsp1, "sp1")

        memx = self.sb("memx", [128, 2, D], F32, A0)
        fkst = [self.sb("fkst%d" % i, [128, T], BF16, A0 + 16384 + i * 2048) for i in range(2)]
        fvst = self.sb("fvst", [128, 8, NTB, 128], BF16, A0 + 20480)
        dkTst = self.sb("dkTst", [128, 4, T], BF16, A0 + 36864)
        dvst = self.sb("dvst", [128, 4, NTB, 128], BF16, A0 + 45056)
        ropeb = [self.sb("ropeb%d" % i, [128, 512], BF16, A0 + 53248 + i * 1024) for i in range(2)]
        ropet = self.sb("ropet", [128, 4, 8, 8], F32, A0 + 55296)
        spst = self.sb("spst", [128, 64], F32, A0 + 56320)
        b_memx = [Buf("memx0"), Buf("memx1")]
        b_fkst = [Buf("fkst0"), Buf("fkst1")]
        b_fvst = Buf("fvst"); b_dkTst = Buf("dkTst"); b_dvst = Buf("dvst")
        b_ropeb = [Buf("ropeb0"), Buf("ropeb1")]
        b_ropet = Buf("ropet"); b_spst = Buf("spst")
        projA = b_memx + b_fkst + [b_fvst, b_dkTst, b_dvst] + b_ropeb + [b_ropet, b_spst]
        inherit(projA, self.bA)
        ds_mem = P.dsem("mem")
        ds_fk = [P.dsem("fk0"), P.dsem("fk1")]
        ds_fv = P.dsem("fv"); ds_dk = P.dsem("dk"); ds_dv = P.dsem("dv"); ds_sp = P.dsem("spx")

        for blk in range(2):
            P.dma("sp", memx[:, blk, :], self.mem_in[blk * 128:(blk + 1) * 128, :], ds_mem, writes=[b_memx[blk]])
        P.join(ds_mem, b_memx)
        mhT = self.sb("mhT", [128, KD, 256], BF16, B0 + 36864)
        b_mhT = Buf("mhT")
        inherit([b_mhT], list(self.rB))
        self.prenorm(2, lambda tb: (memx[:, tb, :], b_memx[tb]), "mem_norm_g",
                     lambda tb, half: mhT[:, half * 8:(half + 1) * 8, tb * 128:(tb + 1) * 128], b_mhT)
        fqT = self.sb("fqT", [128, 8, T], BF16, B0)
        dqT = self.sb("dqT", [128, 4, T], BF16, B0 + 16384)
        mqT = self.sb("mqT", [128, 4, T], BF16, B0 + 24576)
        mkT = self.sb("mkT", [128, 4, 256], BF16, B0 + 32768)
        mv = self.sb("mv", [128, 2, 512], BF16, B0 + 34816)
        Y1 = self.sb("Y1", [128, T], F32, B0 + 36864)
        Y2 = self.sb("Y2", [128, T], F32, B0 + 40960)
        b_fqT = [Buf("fqT%d" % i) for i in range(8)]
        b_dqT = Buf("dqT"); b_mqT = [Buf("mqT%d" % i) for i in range(4)]
        b_mkT = Buf("mkT"); b_mv = Buf("mv"); b_Y1 = Buf("Y1"); b_Y2 = Buf("Y2")
        mixB = b_fqT + [b_dqT] + b_mqT + [b_mkT, b_mv, b_mhT]
        self.set_rB(mixB)
        inherit([b_Y1, b_Y2], [b_mhT])
        self.rB = mixB + [b_Y1, b_Y2]

        wkv = self.w["w_mem_kv"].rearrange("(k p) n -> p k n", p=128)
        win = self.w["w_in"].rearrange("(k p) n -> p k n", p=128)

        def evac_mk(c, pair):
            P.op("act", lambda e: e.copy(out=mkT[:, c, :], in_=self.ps[:, pair, 0:256]),
                 reads=[self.bank[pair]], ow=[b_mkT])
        self.fm_steps(wkv, 0, 4, mhT, b_mhT, evac_mk, ntg=1)
        for kb in range(KD // 2):
            slot = self.next_rb()
            tile = self.sb("wmv", [128, 2, 512], BF16, slot.off)

            def dma_fn(slot=slot, tile=tile, kb=kb):
                P.dma("pool", tile, wkv[:, kb * 2:kb * 2 + 2, 512:1024], slot.ds, writes=[slot.buf])

            def comp_fn(slot=slot, tile=tile, kb=kb):
                for kk in range(2):
                    kl = kb * 2 + kk
                    for blk in range(2):
                        last = (kl == KD - 1)
                        P.op("pe", lambda e, kk=kk, kl=kl, blk=blk, last=last: e.matmul(
                            self.ps[:, blk, :], lhsT=mhT[:, kl, blk * 128:(blk + 1) * 128], rhs=tile[:, kk, :],
                            start=(kl == 0), stop=last),
                            reads=[slot.buf, b_mhT], writes=[self.bank[blk]], flag=last or (kk == 1 and blk == 1))
                if kb == KD // 2 - 1:
                    for blk in range(2):
                        P.op("dve", lambda e, blk=blk: e.tensor_copy(out=mv[:, blk, :], in_=self.ps[:, blk, :]),
                             reads=[self.bank[blk]], ow=[b_mv])
            self.ws.add(slot, dma_fn, comp_fn)

        b_XKs = Buf("XKs"); b_XVs = Buf("XVs"); b_XDKs = Buf("XDKs"); b_XDVs = Buf("XDVs"); b_XLs = Buf("XLs")
        b_XKd = Buf("XKd"); b_XVd = Buf("XVd"); b_XDKd = Buf("XDKd"); b_XDVd = Buf("XDVd"); b_XLd = Buf("XLd")
        ds_cc = [P.dsem("cc%d" % i) for i in range(5)]

        def gather(i, src, bsrc, dst, bdst):
            def step():
                P.aop("pool", lambda e: e.collective_compute(
                    "AllGather", ALU.bypass, replica_groups=RG, ins=[src], outs=[dst]),
                    ds_cc[i], 1, reads=[bsrc], writes=[bdst])
            self.ws.add(None, None, step)

        for grp in range(2):
            def evac_fk(c, pair, grp=grp):
                h = grp * 4 + c
                st = fkst[h % 2]; bst = b_fkst[h % 2]
                src = self.ps[:, pair:pair + 2, :].rearrange("p a n -> p (a n)")
                if h % 2 == 0:
                    P.op("act", lambda e: e.copy(out=st, in_=src), reads=[self.bank[pair], self.bank[pair + 1]], ow=[bst])
                else:
                    P.op("dve", lambda e: e.tensor_copy(out=st, in_=src),
                         reads=[self.bank[pair], self.bank[pair + 1]], ow=[bst])
                P.dma("sp", self.XKs[h * 128:(h + 1) * 128, :], st, ds_fk[h % 2], reads=[bst], writes=[b_XKs])
            self.fm_steps(win, C_FK + grp * 512, 4, self.hT, self.bhT, evac_fk)
        gather(0, self.XKs, b_XKs, self.XKd, b_XKd)

        lhs_h = lambda kl, tb: (self.hT[:, kl, tb * 128:(tb + 1) * 128], self.bhT)
        for grp in range(2):
            def evac_fv(tb, grp=grp):
                dst = fvst[:, grp * 4:(grp + 1) * 4, tb, :]
                src = self.ps[:, tb, :].rearrange("p (h d) -> p h d", h=4)
                if tb % 2 == 0:
                    P.op("act", lambda e: e.copy(out=dst, in_=src), reads=[self.bank[tb]], ow=[b_fvst])
                else:
                    P.op("dve", lambda e: e.tensor_copy(out=dst, in_=src), reads=[self.bank[tb]], ow=[b_fvst])
            self.tm_steps(win, C_FV + grp * 512, lhs_h, KD, evac_fv)

        def st_fv():
            P.dma("sp", self.XVs, fvst.rearrange("p h t d -> p (h t d)"), ds_fv, reads=[b_fvst], writes=[b_XVs])
        self.ws.add(None, None, st_fv)
        gather(1, self.XVs, b_XVs, self.XVd, b_XVd)

        def rope_evac(tb, dstT, bdstT, scale):
            rb = ropeb[tb % 2]; brb = b_ropeb[tb % 2]
            pv = self.ps[:, tb, :].rearrange("p (g c) -> p g c", c=64)
            rbv = rb.rearrange("p (g c) -> p g c", c=64)
            cos = self.rope[:, 0, tb, :].unsqueeze(1).broadcast_to([128, 8, 8])
            sin = self.rope[:, 1, tb, :].unsqueeze(1).broadcast_to([128, 8, 8])
            x1 = pv[:, :, 0:8]; x2 = pv[:, :, 8:16]
            t = [ropet[:, i] for i in range(4)]
            rd = [self.bconst, self.bank[tb]]
            P.op("dve", lambda e: e.tensor_tensor(out=t[0], in0=x1, in1=cos, op=ALU.mult), reads=rd, ow=[b_ropet])
            P.op("dve", lambda e: e.tensor_tensor(out=t[1], in0=x2, in1=sin, op=ALU.mult), reads=rd, ow=[b_ropet])
            P.op("dve", lambda e: e.tensor_tensor(out=t[2], in0=x2, in1=cos, op=ALU.mult), reads=rd, ow=[b_ropet])
            P.op("dve", lambda e: e.tensor_tensor(out=t[3], in0=x1, in1=sin, op=ALU.mult), reads=rd, ow=[b_ropet])
            P.op("dve", lambda e: e.tensor_tensor(out=rbv[:, :, 0:8], in0=t[0], in1=t[1], op=ALU.subtract),
                 reads=[b_ropet], ow=[brb])
            P.op("dve", lambda e: e.tensor_tensor(out=rbv[:, :, 8:16], in0=t[2], in1=t[3], op=ALU.add),
                 reads=[b_ropet], ow=[brb])
            P.op("act", lambda e: e.copy(out=rbv[:, :, 16:64], in_=pv[:, :, 16:64]), reads=[self.bank[tb]], ow=[brb])
            pst = self.ps[:, tb, :].bitcast(BF16)
            for h in range(4):
                P.op("pe", lambda e, h=h: e.transpose(out=pst[:, h * 128:(h + 1) * 128],
                                                      in_=rb[:, h * 128:(h + 1) * 128], identity=self.ident),
                     reads=[brb, self.bconst], writes=[self.bank[tb]], flag=(h == 3))
            src = pst[:, 0:512].rearrange("p (h n) -> p h n", h=4)
            dst = dstT[:, :, tb * 128:(tb + 1) * 128]
            P.op("act", lambda e: e.mul(out=dst, in_=src, mul=scale), reads=[self.bank[tb]], ow=[bdstT])

        self.tm_steps(win, C_DK, lhs_h, KD, lambda tb: rope_evac(tb, dkTst, b_dkTst, 1.0))

        def st_dk():
            P.dma("sp", self.XDKs.rearrange("(h p) t -> p h t", p=128), dkTst, ds_dk, reads=[b_dkTst], writes=[b_XDKs])
        self.ws.add(None, None, st_dk)
        gather(2, self.XDKs, b_XDKs, self.XDKd, b_XDKd)

        def evac_dv(tb):
            dst = dvst[:, :, tb, :]
            src = self.ps[:, tb, :].rearrange("p (h d) -> p h d", h=4)
            if tb % 2 == 0:
                P.op("act", lambda e: e.copy(out=dst, in_=src), reads=[self.bank[tb]], ow=[b_dvst])
            else:
                P.op("dve", lambda e: e.tensor_copy(out=dst, in_=src), reads=[self.bank[tb]], ow=[b_dvst])
        self.tm_steps(win, C_DV, lhs_h, KD, evac_dv)

        def st_dv():
            P.dma("sp", self.XDVs, dvst.rearrange("p h t d -> p (h t d)"), ds_dv, reads=[b_dvst], writes=[b_XDVs])
        self.ws.add(None, None, st_dv)
        gather(3, self.XDVs, b_XDVs, self.XDVd, b_XDVd)

        def ff_step():
            for tb in range(NTB):
                for k in range(KD):
                    P.op("pe", lambda e, tb=tb, k=k: e.matmul(
                        self.ps[:, 0, tb * 8:(tb + 1) * 8], lhsT=self.hT[:, k, tb * 128:(tb + 1) * 128],
                        rhs=self.wff[:, k, :], start=(k == 0), stop=(k == KD - 1)),
                        reads=[self.bhT, self.bwff], writes=[self.bank[0]], flag=(k == KD - 1 and tb == NTB - 1))
            fb = self.fbias.unsqueeze(1).broadcast_to([128, NTB, 8])
            P.op("dve", lambda e: e.tensor_tensor(out=spst.rearrange("p (t h) -> p t h", h=8),
                                                  in0=self.ps[:, 0, 0:64].rearrange("p (t h) -> p t h", h=8),
                                                  in1=fb, op=ALU.add),
                 reads=[self.bconst, self.bank[0]], writes=[b_spst])
            P.op("act", lambda e: e.activation(out=spst, in_=spst, func=AF.Exp, scale=-1.0), reads=[b_spst], writes=[b_spst])
            P.op("act", lambda e: e.activation(out=spst, in_=spst, func=AF.Ln, bias=self.oneb, scale=1.0),
                 reads=[self.bconst, b_spst], writes=[b_spst])
            P.dma("sp", self.XLs, spst, ds_sp, reads=[b_spst], writes=[b_XLs])
        self.ws.add(None, None, ff_step)
        gather(4, self.XLs, b_XLs, self.XLd, b_XLd)

        sc_f = 128.0 ** -0.5
        for grp in range(2):
            def evac_fq(c, pair, grp=grp):
                h = grp * 4 + c
                src = self.ps[:, pair:pair + 2, :].rearrange("p a n -> p (a n)")
                P.op("act", lambda e: e.mul(out=fqT[:, h, :], in_=src, mul=sc_f),
                     reads=[self.bank[pair], self.bank[pair + 1]], ow=[b_fqT[h]])
            self.fm_steps(win, C_FQ + grp * 512, 4, self.hT, self.bhT, evac_fq)

        def evac_mq(c, pair):
            src = self.ps[:, pair:pair + 2, :].rearrange("p a n -> p (a n)")
            P.op("dve", lambda e: e.tensor_scalar(out=mqT[:, c, :], in0=src, scalar1=sc_f, scalar2=None, op0=ALU.mult),
                 reads=[self.bank[pair], self.bank[pair + 1]], ow=[b_mqT[c]])
        self.fm_steps(win, C_MQ, 4, self.hT, self.bhT, evac_mq)
        self.tm_steps(win, C_DQ, lhs_h, KD, lambda tb: rope_evac(tb, dqT, b_dqT, 64.0 ** -0.5))
        self.ws.run()
        self.dump("fqT", fqT, b_fqT, [128, 8, T], BF16)
        self.dump("dqT", dqT, [b_dqT], [128, 4, T], BF16)
        self.dump("mkT", mkT, [b_mkT], [128, 4, 256], BF16)
        self.dump("mv", mv, [b_mv], [128, 2, 512], BF16)

        S0 = self.offSIL
        lamt = self.sb("lamt", [128, 4, 64], F32, S0)
        Lt = self.sb("Lt", [128, 2, 64], F32, S0 + 1024)
        Lpre = self.sb("Lpre", [128, 2, 64], F32, S0 + 1536)
        cmat = self.sb("cmat", [128, 3, 128], F32, S0 + 2048)
        b_S = Buf("silscratch")
        inherit([b_S], self.bSIL)
        ds_l = P.dsem("lam")
        for i, nm in enumerate(("diff_lambda_q1", "diff_lambda_k1", "diff_lambda_q2", "diff_lambda_k2")):
            P.dma("sp", lamt[:, i, :], self.w[nm][0:1, :].partition_broadcast(128)[:, 0, :], ds_l, writes=[b_S])
        P.dma("sp", cmat, self.cmat_in, ds_l, writes=[b_S])
        P.join(ds_l, [b_S])
        lv = self.lamv
        b_lam = Buf("lam")
        inherit([b_lam], [self.bconst])
        P.op("dve", lambda e: e.tensor_tensor(out=lamt[:, 0, :], in0=lamt[:, 0, :], in1=lamt[:, 1, :], op=ALU.mult),
             writes=[b_S])
        P.op("dve", lambda e: e.tensor_tensor(out=lamt[:, 2, :], in0=lamt[:, 2, :], in1=lamt[:, 3, :], op=ALU.mult),
             writes=[b_S])
        P.op("dve", lambda e: e.tensor_reduce(out=lv[:, 0:1], in_=lamt[:, 0, :], axis=AX.X, op=ALU.add),
             reads=[b_S], writes=[b_lam])
        P.op("dve", lambda e: e.tensor_reduce(out=lv[:, 1:2], in_=lamt[:, 2, :], axis=AX.X, op=ALU.add),
             reads=[b_S], writes=[b_lam])
        P.op("act", lambda e: e.activation(out=lv[:, 2:4], in_=lv[:, 0:2], func=AF.Exp), writes=[b_lam])
        P.op("dve", lambda e: e.tensor_tensor(out=lv[:, 4:5], in0=lv[:, 2:3], in1=lv[:, 3:4], op=ALU.subtract),
             writes=[b_lam])
        P.op("dve", lambda e: e.tensor_scalar(out=lv[:, 5:6], in0=lv[:, 4:5], scalar1=-1.0, scalar2=-LAM_INIT,
                                              op0=ALU.mult, op1=ALU.add), writes=[b_lam])
        P.op("dve", lambda e: e.tensor_scalar(out=lv[:, 6:7], in0=lv[:, 6:7], scalar1=1.0 - LAM_INIT, scalar2=None,
                                              op0=ALU.mult), writes=[b_lam])
        neglam = lv[:, 5:6]
        gsc = lv[:, 6:7]

        ds_lt = P.dsem("lt")
        b_Lt = Buf("Lt")
        inherit([b_Lt], [b_S])
        P.dma("sp", Lt, self.XLd.rearrange("(r s) c -> s r c", r=2), ds_lt, reads=[b_XLd], writes=[b_Lt])
        Ltv = Lt.rearrange("p r (j h) -> p r j h", h=8)
        Lpv = Lpre.rearrange("p r (j h) -> p r j h", h=8)
        P.op("dve", lambda e: e.memset(Lpv[:, 0, 0, :], 0.0), writes=[b_S])
        for g in range(1, 16):
            r, j = g % 2, g // 2
            rp, jp = (g - 1) % 2, (g - 1) // 2
            P.op("dve", lambda e, r=r, j=j, rp=rp, jp=jp: e.tensor_tensor(
                out=Lpv[:, r, j, :], in0=Lpv[:, rp, jp, :], in1=Ltv[:, rp, jp, :], op=ALU.add), reads=[b_Lt], writes=[b_S])
        b_D = Buf("Dtab")
        inherit([b_D], [self.bconst])
        P.op("pe", lambda e: e.matmul(self.ps[:, 0, 0:128], lhsT=cmat[:, 0, :], rhs=Lt.rearrange("p r c -> p (r c)"),
                                      start=True, stop=False), reads=[b_S, b_Lt], writes=[self.bank[0]], flag=False)
        P.op("pe", lambda e: e.matmul(self.ps[:, 0, 0:128], lhsT=cmat[:, 1, :], rhs=Lpre.rearrange("p r c -> p (r c)"),
                                      start=False, stop=True), reads=[b_S], writes=[self.bank[0]])
        P.op("dve", lambda e: e.tensor_copy(out=self.Dtab, in_=self.ps[:, 0, 0:128]), reads=[self.bank[0]], writes=[b_D])
        P.op("pe", lambda e: e.matmul(self.ps[:, 1, 0:128], lhsT=cmat[:, 2, :], rhs=self.Dtab, start=True, stop=True),
             reads=[b_S, b_D], writes=[self.bank[1]])
        P.op("dve", lambda e: e.tensor_scalar(out=self.Dend, in0=self.ps[:, 1, 0:64], scalar1=self.psel[:, 0:1],
                                              scalar2=None, op0=ALU.mult),
             reads=[self.bconst, self.bank[1]], writes=[b_D])
        P.op("dve", lambda e: e.scalar_tensor_tensor(out=self.Dend, in0=self.ps[:, 1, 64:128], scalar=self.psel[:, 1:2],
                                                     in1=self.Dend, op0=ALU.mult, op1=ALU.add),
             reads=[self.bconst, self.bank[1], b_D], writes=[b_D])
        in0 = self.Dtab.rearrange("p (g h) -> p g h", h=8).unsqueeze(3).broadcast_to([128, 16, 8, 8])
        in1 = self.Dend.rearrange("p (j h) -> p h j", h=8).unsqueeze(1).broadcast_to([128, 16, 8, 8])
        b_bias = Buf("biasT")
        inherit([b_bias], [self.bconst])
        P.op("dve", lambda e: e.tensor_tensor(out=self.biasT, in0=in0, in1=in1, op=ALU.subtract),
             reads=[b_D], writes=[b_bias])
        bflat = self.biasT.rearrange("p a b c -> p (a b c)")
        P.op("dve", lambda e: e.tensor_scalar_min(out=bflat, in0=bflat, scalar1=0.0), writes=[b_bias])
        self.dump("Dtab", self.Dtab, [b_D], [128, 128], F32)
        self.dump("biasT", bflat, [b_bias], [128, 1024], F32)

        yfoxT = self.sb("yfoxT", [128, 8, T], BF16, A0)
        ydiffT = self.sb("ydiffT", [128, 4, T], BF16, A0 + 16384)
        ymemT = self.sb("ymemT", [128, 4, T], BF16, A0 + 24576)
        KTb = [self.sb("KT%d" % i, [128, 2, T], BF16, A0 + 32768 + i * 4096) for i in range(2)]
        Vb = [self.sb("V%d" % i, [128, 2, NTB, 128], BF16, A0 + 40960 + i * 4096) for i in range(2)]
        PT = [self.sb("PT%d" % i, [128, T], BF16, A0 + 49152 + i * 2048) for i in range(3)]
        recip = self.sb("recip", [128, T], F32, A0 + 55296)
        sq = self.sb("sq", [128, T], BF16, A0 + 59392)
        accS = self.sb("accS", [128, T], F32, A0 + 61440)
        b_yfox = [Buf("yfox%d" % i) for i in range(8)]
        b_ydiff = [Buf("ydiff%d" % i) for i in range(4)]
        b_ymem = [Buf("ymem%d" % i) for i in range(4)]
        b_KT = [Buf("KT0"), Buf("KT1")]; b_V = [Buf("V0"), Buf("V1")]
        b_PT = [[Buf("PT%d_%d" % (i, j)) for j in range(8)] for i in range(3)]
        b_recip = Buf("recip"); b_sq = Buf("sq"); b_accS = Buf("accS")
        b_PTall = b_PT[0] + b_PT[1] + b_PT[2]
        attnA = b_yfox + b_ydiff + b_ymem + b_KT + b_V + b_PTall + [b_recip, b_sq, b_accS]
        inherit(attnA, projA)
        ds_kv = [P.dsem("kv0"), P.dsem("kv1")]
        self.pt_i = 0
        self.sp_i = 0
        accv = self.ps[:, 4:6, :].rearrange("p a n -> p (a n)")
        rsv = self.ps[:, 6:8, :].rearrange("p a n -> p (a n)")

        passes = []
        deferred = []

        def add_pass(nkb, q_ap, q_bufs, k_fn, v_fn, kv_bufs, causal, bias_fn, mask_fn, pre_fn, post_fn):
            passes.append(dict(nkb=nkb, q_ap=q_ap, q_bufs=q_bufs, k_fn=k_fn, v_fn=v_fn, kv_bufs=kv_bufs,
                               causal=causal, bias_fn=bias_fn, mask_fn=mask_fn, pre_fn=pre_fn, post_fn=post_fn))

        def item_geom(p, g):
            jmin = g // 2 if p["causal"] else 0
            c0 = jmin * 128
            parts = []
            if c0 < 512:
                parts.append((0, c0, 512))
            parts.append((1, max(c0, 512), 1024))
            return jmin, c0, parts

        pre_done = set()

        def pre(k):
            if 0 <= k < len(passes) and k not in pre_done:
                pre_done.add(k)
                if passes[k]["pre_fn"] is not None:
                    passes[k]["pre_fn"]()

        def emit_qk(p, g, st):
            jmin, c0, parts = item_geom(p, g)
            sp = self.sp_i % 2; self.sp_i += 1
            pi = self.pt_i % 3; self.pt_i += 1
            st["sp"] = sp; st["pi"] = pi
            k_fn = p["k_fn"]; q_ap = p["q_ap"]
            for ip, (hb_, lo, hi) in enumerate(parts):
                bk = 2 * sp + hb_
                l0 = lo - hb_ * 512
                P.op("pe", lambda e, bk=bk, lo=lo, hi=hi, g=g, l0=l0: e.matmul(
                    self.ps[:, bk, l0:l0 + (hi - lo)],
                    lhsT=k_fn(g), rhs=q_ap[:, lo:hi], start=True, stop=True),
                    reads=list(p["kv_bufs"]) + list(p["q_bufs"]), writes=[self.bank[bk]], flag=(ip == len(parts) - 1))

        def emit_rest(p, g, st):
            jmin, c0, parts = item_geom(p, g)
            sp = st["sp"]; pi = st["pi"]
            nkb = p["nkb"]
            pt = PT[pi]; bpt = b_PT[pi]
            bias_fn = p["bias_fn"]; mask_fn = p["mask_fn"]; v_fn = p["v_fn"]
            sbanks = [self.bank[2 * sp + hb_] for (hb_, _, _) in parts]
            if bias_fn is None:
                sv = self.ps[:, 2 * sp:2 * sp + 2, :].rearrange("p a n -> p (a n)")
                P.op("act", lambda e: e.activation(out=pt[:, c0:T], in_=sv[:, c0:T], func=AF.Exp),
                     reads=sbanks, ow=bpt[jmin:8])
            else:
                for j in range(jmin, 8):
                    bk = 2 * sp + j // 4
                    P.op("act", lambda e, bk=bk, j=j: e.activation(
                        out=pt[:, j * 128:(j + 1) * 128], in_=self.ps[:, bk, (j % 4) * 128:(j % 4 + 1) * 128],
                        func=AF.Exp, bias=bias_fn(g, j), scale=1.0),
                        reads=[b_bias, self.bank[bk]], ow=[bpt[j]])
            if mask_fn is not None:
                P.op("dve", lambda e: e.tensor_tensor(
                    out=pt[:, c0:c0 + 128], in0=pt[:, c0:c0 + 128], in1=mask_fn(g), op=ALU.mult),
                    reads=[self.bconst, bpt[jmin]], writes=[bpt[jmin]])
            for ip, (hb_, lo, hi) in enumerate(parts):
                l0 = lo - hb_ * 512
                P.op("pe", lambda e, hb_=hb_, l0=l0, lo=lo, hi=hi: e.matmul(
                    self.ps[:, 4 + hb_, l0:l0 + (hi - lo)], lhsT=v_fn(g), rhs=pt[:, lo:hi],
                    start=(g == 0), stop=(g == nkb - 1), skip_group_check=True),
                    reads=list(p["kv_bufs"]) + bpt[lo // 128:hi // 128], writes=[self.bank[4 + hb_]], flag=False)
                P.op("pe", lambda e, hb_=hb_, l0=l0, lo=lo, hi=hi: e.matmul(
                    self.ps[:, 6 + hb_, l0:l0 + (hi - lo)], lhsT=self.onesb, rhs=pt[:, lo:hi],
                    start=(g == 0), stop=(g == nkb - 1), skip_group_check=True),
                    reads=bpt[lo // 128:hi // 128] + [self.bconst], writes=[self.bank[6 + hb_]],
                    flag=(ip == len(parts) - 1))
            if g == nkb - 1:
                d = p["post_fn"](sp)
                if d is not None:
                    deferred.append([4, d])
            for dd in list(deferred):
                dd[0] -= 1
                if dd[0] <= 0:
                    deferred.remove(dd)
                    dd[1](sp)

        def run_passes():
            items = [(p, g, {}) for p in passes for g in range(p["nkb"])]
            pre(0); pre(1)
            emit_qk(*items[0])
            for i, it in enumerate(items):
                if i + 1 < len(items):
                    emit_qk(*items[i + 1])
                emit_rest(*it)
                if it[1] == it[0]["nkb"] - 1:
                    pre(passes.index(it[0]) + 2)
            for dd in deferred:
                dd[1](it[2]["sp"])

        def normalize(dst, bdst):
            P.op("act", lambda e: e.copy(out=accS, in_=accv), reads=[self.bank[4], self.bank[5]], ow=[b_accS])
            P.op("dve", lambda e: e.tensor_copy(out=recip, in_=rsv), reads=[self.bank[6], self.bank[7]], ow=[b_recip])
            P.op("dve", lambda e: e.reciprocal(out=recip, in_=recip), reads=[b_recip], writes=[b_recip])
            P.op("dve", lambda e: e.tensor_tensor(out=dst, in0=accS, in1=recip, op=ALU.mult),
                 reads=[b_recip, b_accS], ow=[bdst])

        for h in range(4):
            add_pass(2, mqT[:, h, :], [b_mqT[h]],
                     lambda g, h=h: mkT[:, h, g * 128:(g + 1) * 128],
                     lambda g, h=h: mv[:, g, h * 128:(h + 1) * 128],
                     [b_mkT, b_mv], False, None, None, None,
                     lambda sp, h=h: normalize(ymemT[:, h, :], b_ymem[h]))

        XKv = self.XKd.rearrange("(r h d) t -> h d r t", r=2, h=8)
        XVv = self.XVd.rearrange("(r s) (h x) -> h s r x", r=2, h=8)
        kvi = 0
        for h in range(8):
            i = kvi % 2; kvi += 1
            kt = KTb[i]; vv = Vb[i]

            def pre_fox(h=h, i=i):
                P.dma("sp", KTb[i], XKv[h], ds_kv[i], reads=[b_XKd], writes=[b_KT[i]])
                P.dma("sp", Vb[i].rearrange("p r t d -> p r (t d)"), XVv[h], ds_kv[i], reads=[b_XVd], writes=[b_V[i]])
                P.join(ds_kv[i], [b_KT[i], b_V[i]])
            add_pass(16, fqT[:, h, :], [b_fqT[h]],
                     lambda g, kt=kt: kt[:, g % 2, (g // 2) * 128:(g // 2 + 1) * 128],
                     lambda g, vv=vv: vv[:, g % 2, g // 2, :],
                     [b_KT[i], b_V[i]], True,
                     lambda g, j, h=h: self.biasT[:, (g % 2) * 8 + g // 2, h, j:j + 1],
                     lambda g: self.masks[:, g % 2, :], pre_fox,
                     lambda sp, h=h: normalize(yfoxT[:, h, :], b_yfox[h]))

        XDKv = self.XDKd.rearrange("(r h d) t -> h d r t", r=2, h=4)
        XDVv = self.XDVd.rearrange("(r s) (h x) -> h s r x", r=2, h=4)

        def diff_tail(sp, h):
            P.op("dve", lambda e: e.scalar_tensor_tensor(out=Y1, in0=Y2, scalar=neglam, in1=Y1, op0=ALU.mult, op1=ALU.add),
                 reads=[b_lam, b_Y2, b_Y1], writes=[b_Y1])
            P.op("act", lambda e: e.activation(out=sq, in_=Y1, func=AF.Square), reads=[b_Y1], ow=[b_sq])
            for tg in range(2):
                P.op("pe", lambda e, tg=tg: e.matmul(self.ps[:, 2 * sp + tg, :], lhsT=self.onesb,
                                                    rhs=sq[:, tg * 512:(tg + 1) * 512], start=True, stop=True),
                     reads=[b_sq, self.bconst], writes=[self.bank[2 * sp + tg]], flag=(tg == 1))
            ssv = self.ps[:, 2 * sp:2 * sp + 2, :].rearrange("p a n -> p (a n)")
            P.op("act", lambda e: e.activation(out=recip, in_=ssv, func=AF.Sqrt, bias=self.epsb, scale=1.0 / 128.0),
                 reads=[self.bconst, self.bank[2 * sp], self.bank[2 * sp + 1]], ow=[b_recip])
            P.op("dve", lambda e: e.reciprocal(out=recip, in_=recip), reads=[b_recip], writes=[b_recip])
            P.op("dve", lambda e: e.scalar_tensor_tensor(out=ydiffT[:, h, :], in0=Y1, scalar=gsc, in1=recip,
                                                         op0=ALU.mult, op1=ALU.mult),
                 reads=[b_Y1, b_recip, b_lam], ow=[b_ydiff[h]])

        for h in range(4):
            i = kvi % 2; kvi += 1
            kt = KTb[i]; vv = Vb[i]

            def pre_diff(h=h, i=i):
                P.dma("sp", KTb[i], XDKv[h], ds_kv[i], reads=[b_XDKd], writes=[b_KT[i]])
                P.dma("sp", Vb[i].rearrange("p r t d -> p r (t d)"), XDVv[h], ds_kv[i], reads=[b_XDVd], writes=[b_V[i]])
                P.join(ds_kv[i], [b_KT[i], b_V[i]])
            for m in range(2):
                def post_diff(sp, h=h, m=m):
                    if m == 0:
                        normalize(Y1, b_Y1)
                        return None
                    normalize(Y2, b_Y2)
                    return lambda sp2, h=h: diff_tail(sp2, h)
                add_pass(16, dqT[m * 64:(m + 1) * 64, h, :], [b_dqT],
                         lambda g, kt=kt, m=m: kt[m * 64:(m + 1) * 64, g % 2, (g // 2) * 128:(g // 2 + 1) * 128],
                         lambda g, vv=vv: vv[:, g % 2, g // 2, :],
                         [b_KT[i], b_V[i]], True, None,
                         lambda g: self.masks[:, 2 + g % 2, :],
                         pre_diff if m == 0 else None, post_diff)
        run_passes()
        self.dump("ymemT", ymemT, b_ymem, [128, 4, T], BF16)
        self.dump("yfoxT", yfoxT, b_yfox, [128, 8, T], BF16)
        self.dump("ydiffT", ydiffT, b_ydiff, [128, 4, T], BF16)

        mrgT = self.sb("mrgT", [128, KD, T], BF16, B0)
        b_mrg = [Buf("mrg%d" % i) for i in range(KD)]
        self.set_rB(b_mrg + [Buf("Btail")])
        sig = [self.sb("sig%d" % i, [128, T], BF16, A0 + 32768 + i * 2048) for i in range(3)]
        acc = self.sb("macc", [128, T], F32, A0 + 32768 + 6144)
        tmp = [self.sb("mtmp%d" % i, [128, T], F32, A0 + 32768 + 10240 + i * 4096) for i in range(2)]
        b_sig = [Buf("sig%d" % i) for i in range(3)]
        b_acc = Buf("macc"); b_tmp = [Buf("mtmp0"), Buf("mtmp1")]
        inherit(b_sig + [b_acc] + b_tmp, b_KT + b_V + b_PTall + [b_recip, b_sq, b_accS])
        wmg = self.w["w_merge_gate"].rearrange("(k p) (g c) -> p k g c", p=128, g=3)
        wbr = [self.w["w_branch_fox"].rearrange("(k p) n -> p k n", p=128),
               self.w["w_branch_diff"].rearrange("(k p) n -> p k n", p=128),
               self.w["w_branch_mem"].rearrange("(k p) n -> p k n", p=128)]
        ybr = [(yfoxT, b_yfox, 8), (ydiffT, b_ydiff, 4), (ymemT, b_ymem, 4)]
        for m in range(KD):
            slot = self.next_ra()
            gt = self.sb("wmg", [128, KD, 3, 128], BF16, slot.off)
            bt = self.sb("wbr", [128, KD, 128], BF16, slot.off + 12288)

            def dma_fn(slot=slot, gt=gt, bt=bt, m=m):
                for b in range(3):
                    P.dma("pool", gt[:, :, b, :], wmg[:, :, b, m * 128:(m + 1) * 128], slot.ds, writes=[slot.buf])
                P.dma("pool", bt[:, 0:8, :], wbr[0][:, :, m * 128:(m + 1) * 128], slot.ds, writes=[slot.buf])
                P.dma("pool", bt[:, 8:12, :], wbr[1][:, :, m * 128:(m + 1) * 128], slot.ds, writes=[slot.buf])
                P.dma("pool", bt[:, 12:16, :], wbr[2][:, :, m * 128:(m + 1) * 128], slot.ds, writes=[slot.buf])

            def comp_fn(slot=slot, gt=gt, bt=bt, m=m):
                kofs = 0
                for b in range(3):
                    yT, byT, nkb = ybr[b]
                    pg = (self.chunk_ctr % 4) * 2; self.chunk_ctr += 1
                    for k in range(KD):
                        for tg in range(2):
                            P.op("pe", lambda e, k=k, tg=tg, b=b, pg=pg: e.matmul(
                                self.ps[:, pg + tg, :], lhsT=gt[:, k, b, :], rhs=self.hT[:, k, tg * 512:(tg + 1) * 512],
                                start=(k == 0), stop=(k == KD - 1)),
                                reads=[slot.buf, self.bhT], writes=[self.bank[pg + tg]], flag=(k == KD - 1 and tg == 1))
                    gv = self.ps[:, pg:pg + 2, :].rearrange("p a n -> p (a n)")
                    P.op("act", lambda e, b=b, gv=gv: e.activation(out=sig[b], in_=gv, func=AF.Sigmoid,
                                                                 bias=self.bmT[:, b * 16 + m:b * 16 + m + 1], scale=1.0),
                         reads=[self.bconst, self.bank[pg], self.bank[pg + 1]], ow=[b_sig[b]])
                    py = (self.chunk_ctr % 4) * 2; self.chunk_ctr += 1
                    for k in range(nkb):
                        for tg in range(2):
                            P.op("pe", lambda e, k=k, tg=tg, py=py, kofs=kofs, yT=yT: e.matmul(
                                self.ps[:, py + tg, :], lhsT=bt[:, kofs + k, :], rhs=yT[:, k, tg * 512:(tg + 1) * 512],
                                start=(k == 0), stop=(k == nkb - 1)),
                                reads=[slot.buf] + list(byT), writes=[self.bank[py + tg]],
                                flag=(k == nkb - 1 and tg == 1))
                    yv = self.ps[:, py:py + 2, :].rearrange("p a n -> p (a n)")
                    ybanks = [self.bank[py], self.bank[py + 1]]
                    if b == 0:
                        P.op("dve", lambda e, yv=yv: e.tensor_tensor(out=acc, in0=yv, in1=sig[0], op=ALU.mult),
                             reads=[b_sig[0]] + ybanks, ow=[b_acc])
                    elif b == 1:
                        P.op("dve", lambda e, yv=yv: e.tensor_tensor(out=tmp[0], in0=yv, in1=sig[1], op=ALU.mult),
                             reads=[b_sig[1]] + ybanks, ow=[b_tmp[0]])
                        P.op("dve", lambda e: e.tensor_tensor(out=acc, in0=acc, in1=tmp[0], op=ALU.add),
                             reads=[b_tmp[0], b_acc], writes=[b_acc])
                    else:
                        P.op("dve", lambda e, yv=yv: e.tensor_tensor(out=tmp[1], in0=yv, in1=sig[2], op=ALU.mult),
                             reads=[b_sig[2]] + ybanks, ow=[b_tmp[1]])
                        P.op("dve", lambda e, m=m: e.tensor_tensor(out=mrgT[:, m, :], in0=acc, in1=tmp[1], op=ALU.add),
                             reads=[b_tmp[1], b_acc], ow=[b_mrg[m]])
                    kofs += nkb

            self.ws.add(slot, dma_fn, comp_fn)
        wov = self.w["w_out"].rearrange("(k p) n -> p k n", p=128)
        self.ws.add(None, None, lambda: inherit(self.bA, attnA + b_sig + [b_acc] + b_tmp))
        for ng in range(4):
            def evac_o(tb, ng=ng):
                ydst = self.A[:, tb, ng * 512:(ng + 1) * 512]
                if tb % 2 == 0:
                    P.op("act", lambda e: e.copy(out=ydst, in_=self.ps[:, tb, :]), reads=[self.bank[tb]], ow=[self.bA[tb]])
                else:
                    P.op("dve", lambda e: e.tensor_copy(out=ydst, in_=self.ps[:, tb, :]),
                         reads=[self.bank[tb]], ow=[self.bA[tb]])
            self.tm_steps(wov, ng * 512, lambda kl, tb: (mrgT[:, kl, tb * 128:(tb + 1) * 128], b_mrg[kl]), KD, evac_o)
        self.ws.run()
        self.dump("mrgT", mrgT, b_mrg, [128, KD, T], BF16)
        self.postnorm("mix_post_g", 1.0, self.xsp1, bsp1, final=False)
        inherit(self.bSIL, [b_S])
        self.set_rB(self.bact)


_NC_CACHE = {}


def _get_builder(stage, dbg=()):
    key = (stage, tuple(dbg))
    if key not in _NC_CACHE:
        b = Builder(stage, dbg)
        b.build()
        _NC_CACHE[key] = b
    return _NC_CACHE[key]


def _shard_x(x, c):
    b, p = c // 2, c % 2
    return np.ascontiguousarray(x[b].reshape(8, 2, 128, D)[:, p].reshape(T, D))


def _consts(p):
    s = np.arange(128)[:, None]
    t = np.arange(128)[None, :]
    tri = (s <= t).astype(np.float32)
    ones = np.ones((128, 128), np.float32)
    zeros = np.zeros((128, 128), np.float32)
    sel = np.zeros((128, 128), np.float32); sel[127, :] = 1.0
    cmat = np.stack([tri, ones, sel], axis=1)
    chunk = ((s // 64) <= (t // 64)).astype(np.float32)
    if p == 0:
        masks = [tri, zeros, chunk, zeros]
    else:
        masks = [ones, tri, ones, chunk]
    masks = np.stack(masks, axis=1).astype(ml_dtypes.bfloat16)
    pos = ((2 * np.arange(8)[None, :] + p) * 128 + np.arange(128)[:, None]).astype(np.float32)
    inv = (500000.0 ** (-np.arange(0, 16, 2, dtype=np.float32) / 16)).astype(np.float32)
    ang = pos[:, :, None] * inv[None, None, :]
    rope = np.stack([np.cos(ang), np.sin(ang)], axis=1).astype(np.float32)
    psel = np.zeros((128, 2), np.float32); psel[:, p] = 1.0
    return {"c_ident": np.eye(128, dtype=ml_dtypes.bfloat16), "c_mat": cmat, "c_masks": masks,
            "c_rope": rope, "c_psel": psel}


_W2D = ("ffn1_pre_g", "ffn1_w_gate", "ffn1_w_up", "ffn1_w_down", "ffn1_post_g", "mix_pre_g", "w_in", "fox_f_bias",
        "diff_lambda_q1", "diff_lambda_k1", "diff_lambda_q2", "diff_lambda_k2", "diff_head_g", "mem_norm_g",
        "w_mem_kv", "w_branch_fox", "w_branch_diff", "w_branch_mem", "w_merge_gate", "b_merge_gate", "w_out",
        "mix_post_g", "ffn2_pre_g", "ffn2_w_gate", "ffn2_w_up", "ffn2_w_down", "ffn2_post_g")


def kernel(stage=3, dbg=(), **inputs):
    bld = _get_builder(stage, dbg)
    nc = bld.nc
    x = np.asarray(inputs["x"], dtype=np.float32)
    mem = np.asarray(inputs["mem"], dtype=np.float32)
    shared = {}
    for nm in _W2D:
        a = np.asarray(inputs[nm], dtype=np.float32)[0]
        shared[nm] = np.ascontiguousarray(a.reshape(1, -1) if a.ndim == 1 else a)
    cst = [_consts(0), _consts(1)]
    in_maps = []
    for c in range(8):
        m = dict(shared)
        m.update(cst[c % 2])
        m["x"] = _shard_x(x, c)
        m["mem"] = np.ascontiguousarray(mem[c // 2])
        in_maps.append(m)
    res = run_bass_kernel_spmd(nc, in_maps, core_ids=list(range(8)))
    out = np.empty((4, 2048, D), np.float32)
    for c in range(8):
        b, p = c // 2, c % 2
        out[b].reshape(8, 2, 128, D)[:, p] = res.results[c]["out"].reshape(8, 128, D)
    if dbg:
        kernel.last_dbg = [{k: res.results[c]["dbg_" + k] for k in bld.dbg_out} for c in range(8)]
    return out
```

```python
import numpy as np
import ml_dtypes
from contextlib import ExitStack
import concourse.bass as bass
import concourse.mybir as mybir
from concourse.bass_utils import run_bass_kernel_spmd

F32 = mybir.dt.float32
BF16 = mybir.dt.bfloat16
AF = mybir.ActivationFunctionType
ALU = mybir.AluOpType
AX = mybir.AxisListType

D = 2048
DFF = 5632
T = 1024
NTB = 8
KD = 16
NF = 44
EPS = 1e-6
SAME_ENG_SYNC = True

SB_BASE = 16512


class Buf:
    __slots__ = ("name", "w", "r")

    def __init__(self, name):
        self.name = name
        self.w = {}
        self.r = {}


class DSem:
    def __init__(self, handle, key):
        self.h = handle
        self.key = key
        self.count = 0


def inherit(new_bufs, old_bufs):
    acc = {}
    for b in old_bufs:
        _merge(acc, b.w)
        _merge(acc, b.r)
    for nb in new_bufs:
        _merge(nb.w, acc)


def _merge(dst, src):
    for k, v in src.items():
        if dst.get(k, 0) < v:
            dst[k] = v


class Prog:
    ENGS = ("pe", "act", "dve", "pool", "sp")

    def __init__(self, nc, stack):
        self.nc = nc
        self.stack = stack
        self.q = {e: [] for e in self.ENGS}
        self.cnt = {e: 0 for e in self.ENGS}
        self.handles = {}
        for e in ("pe", "act", "dve", "pool"):
            self.handles[e] = stack.enter_context(nc.semaphore("s_" + e))
        self.seen = {e: {} for e in self.ENGS}
        self.ndsem = 0
        self.pe_pending = False
        self.nwaits = 0

    def dsem(self, name):
        h = self.stack.enter_context(self.nc.semaphore("d_" + name))
        key = "d_%s_%d" % (name, self.ndsem)
        self.ndsem += 1
        self.handles[key] = h
        return DSem(h, key)

    def _deps(self, eng, reads, writes, ow=()):
        deps = {}
        for b in reads:
            _merge(deps, b.w)
        for b in writes:
            _merge(deps, b.w)
            _merge(deps, b.r)
        if ow:
            d2 = {}
            for b in ow:
                _merge(d2, b.w)
                _merge(d2, b.r)
            d2.pop(eng, None)
            _merge(deps, d2)
        if eng == "pe" or not SAME_ENG_SYNC:
            deps.pop(eng, None)
        waits = []
        seen = self.seen[eng]
        for k, v in deps.items():
            if seen.get(k, 0) < v:
                seen[k] = v
                waits.append((self.handles[k], v))
                if k == "pe" and eng != "pe":
                    assert v <= self.cnt["pe"], "wait on unflagged PE op"
        return waits

    def op(self, eng, fn, reads=(), writes=(), ow=(), flag=True):
        waits = self._deps(eng, reads, writes, ow)
        if ow:
            writes = list(writes) + list(ow)
        if eng == "pe" and not flag:
            tokv = self.cnt[eng] + 1
            self.pe_pending = True
        else:
            self.cnt[eng] += 1
            tokv = self.cnt[eng]
            if eng == "pe":
                self.pe_pending = False
        sem = self.handles[eng]
        self.nwaits += len(waits)

        def run(e, waits=waits, fn=fn, flag=flag, sem=sem):
            for h, v in waits:
                e.wait_ge(h, v)
            ins = fn(e)
            if flag:
                ins.then_inc(sem, 1)

        self.q[eng].append(run)
        for b in reads:
            if b.r.get(eng, 0) < tokv:
                b.r[eng] = tokv
        for b in writes:
            if b.w.get(eng, 0) < tokv:
                b.w[eng] = tokv
            b.r = {}

    def dma(self, eng, out, in_, ds, reads=(), writes=(), **kw):
        waits = self._deps(eng, reads, writes)
        ds.count += 16
        tokv = ds.count
        self.nwaits += len(waits)

        def run(e, waits=waits, out=out, in_=in_, h=ds.h, kw=kw):
            for hh, v in waits:
                e.wait_ge(hh, v)
            e.dma_start(out=out, in_=in_, **kw).then_inc(h, 16)

        self.q[eng].append(run)
        for b in reads:
            if b.r.get(ds.key, 0) < tokv:
                b.r[ds.key] = tokv
        for b in writes:
            if b.w.get(ds.key, 0) < tokv:
                b.w[ds.key] = tokv
            b.r = {}

    def aop(self, eng, fn, ds, inc, reads=(), writes=()):
        waits = self._deps(eng, reads, writes)
        ds.count += inc
        tokv = ds.count

        def run(e, waits=waits, fn=fn, h=ds.h, inc=inc):
            for hh, v in waits:
                e.wait_ge(hh, v)
            fn(e).then_inc(h, inc)

        self.q[eng].append(run)
        for b in reads:
            if b.r.get(ds.key, 0) < tokv:
                b.r[ds.key] = tokv
        for b in writes:
            if b.w.get(ds.key, 0) < tokv:
                b.w[ds.key] = tokv
            b.r = {}

    def join(self, ds, bufs):
        for b in bufs:
            for dd in (b.r, b.w):
                if ds.key in dd:
                    dd[ds.key] = ds.count

    def raw(self, eng, fn, reads=(), writes=()):
        waits = self._deps(eng, reads, writes)

        def run(e, waits=waits, fn=fn):
            for h, v in waits:
                e.wait_ge(h, v)
            if fn is not None:
                fn(e)

        self.q[eng].append(run)

    def emit_all(self):
        nc = self.nc
        q = self.q
        with nc.Block() as block:
            @block.tensor
            def _(e):
                for f in q["pe"]:
                    f(e)

            @block.scalar
            def _(e):
                for f in q["act"]:
                    f(e)

            @block.vector
            def _(e):
                for f in q["dve"]:
                    f(e)

            @block.gpsimd
            def _(e):
                for f in q["pool"]:
                    f(e)

            @block.sync
            def _(e):
                for f in q["sp"]:
                    f(e)


class Slot:
    def __init__(self, P, name, off, nbytes):
        self.buf = Buf(name)
        self.ds = P.dsem(name)
        self.off = off
        self.nbytes = nbytes
        self.name = name


class WStream:
    def __init__(self, P):
        self.P = P
        self.steps = []

    def add(self, slot, dma_fn, compute_fn):
        self.steps.append((slot, dma_fn, compute_fn))

    def run(self):
        steps = self.steps
        n = len(steps)
        prev_user = [-1] * n
        last = {}
        for i, (slot, _, _) in enumerate(steps):
            if slot is not None:
                prev_user[i] = last.get(id(slot), -1)
                last[id(slot)] = i
        nd = 0
        for i in range(n):
            while nd < n and prev_user[nd] < i:
                slot, dma_fn, _ = steps[nd]
                if dma_fn is not None:
                    dma_fn()
                nd += 1
            assert nd > i
            steps[i][2]()
        self.steps = []


C_FQ, C_FK, C_FV, C_FF, C_DQ, C_DK, C_DV, C_MQ = 0, 1024, 2048, 3072, 3080, 3592, 4104, 4616
D_IN = 5128
LAM_INIT = 0.8 - 0.6
RG = [[0, 1], [2, 3], [4, 5], [6, 7]]


class Builder:
    def __init__(self, stage, dbg=()):
        self.stage = stage
        self.dbg = dbg
        self.nc = bass.Bass("TRN2", target_bir_lowering=False)
        self.stack = ExitStack()
        self.P = Prog(self.nc, self.stack)
        self.nsb = 0
        self.dbg_out = {}

    def sb(self, name, shape, dtype, off):
        self.nsb += 1
        h = self.nc.alloc_sbuf_tensor_at("%s_%d" % (name, self.nsb), list(shape), dtype, offset=SB_BASE + off)
        return h.ap()

    def din(self, name, shape, dtype=F32):
        return self.nc.dram_tensor(name, list(shape), dtype, kind="ExternalInput").ap()

    def dscr(self, name, shape, dtype):
        return self.nc.dram_tensor(name, list(shape), dtype).ap()

    def dump(self, name, ap, bufs, shape, dtype):
        if name not in self.dbg:
            return
        o = self.nc.dram_tensor("dbg_" + name, list(shape), dtype, kind="ExternalOutput").ap()
        self.dbg_out[name] = (list(shape), dtype)
        self.P.dma("sp", o, ap, self.dsout, reads=bufs)

    def build(self):
        nc = self.nc
        P = self.P
        self.x_in = self.din("x", [T, D])
        self.mem_in = self.din("mem", [256, D])
        self.out = nc.dram_tensor("out", [T, D], F32, kind="ExternalOutput").ap()
        self.w = {}
        for nm, shp in (("ffn1_pre_g", [1, D]), ("ffn1_w_gate", [D, DFF]), ("ffn1_w_up", [D, DFF]),
                        ("ffn1_w_down", [DFF, D]), ("ffn1_post_g", [1, D]),
                        ("mix_pre_g", [1, D]), ("w_in", [D, D_IN]), ("fox_f_bias", [1, 8]),
                        ("diff_lambda_q1", [1, 64]), ("diff_lambda_k1", [1, 64]),
                        ("diff_lambda_q2", [1, 64]), ("diff_lambda_k2", [1, 64]),
                        ("diff_head_g", [1, 128]), ("mem_norm_g", [1, D]), ("w_mem_kv", [D, 1024]),
                        ("w_branch_fox", [1024, D]), ("w_branch_diff", [512, D]), ("w_branch_mem", [512, D]),
                        ("w_merge_gate", [D, 3 * D]), ("b_merge_gate", [1, 3 * D]), ("w_out", [D, D]),
                        ("mix_post_g", [1, D]),
                        ("ffn2_pre_g", [1, D]), ("ffn2_w_gate", [D, DFF]), ("ffn2_w_up", [D, DFF]),
                        ("ffn2_w_down", [DFF, D]), ("ffn2_post_g", [1, D])):
            self.w[nm] = self.din(nm, shp)
        self.ident_in = self.din("c_ident", [128, 128], BF16)
        self.cmat_in = self.din("c_mat", [128, 3, 128], F32)
        self.masks_in = self.din("c_masks", [128, 4, 128], BF16)
        self.rope_in = self.din("c_rope", [128, 2, NTB, 8], F32)
        self.psel_in = self.din("c_psel", [128, 2], F32)
        self.xsp1 = self.dscr("xsp1", [T, D], F32)
        self.xsp2 = self.dscr("xsp2", [T, D], F32)
        self.XKs = self.dscr("XKs", [1024, 1024], BF16); self.XKd = self.dscr("XKd", [2048, 1024], BF16)
        self.XVs = self.dscr("XVs", [128, 8192], BF16); self.XVd = self.dscr("XVd", [256, 8192], BF16)
        self.XDKs = self.dscr("XDKs", [512, 1024], BF16); self.XDKd = self.dscr("XDKd", [1024, 1024], BF16)
        self.XDVs = self.dscr("XDVs", [128, 4096], BF16); self.XDVd = self.dscr("XDVd", [256, 4096], BF16)
        self.XLs = self.dscr("XLs", [128, 64], F32); self.XLd = self.dscr("XLd", [256, 64], F32)

        o = 0
        self.offA = o
        self.A = self.sb("A", [128, NTB, D], F32, o); o += NTB * D * 4
        self.hT = self.sb("hT", [128, KD, T], BF16, o); o += KD * T * 2
        self.offB = o
        self.actT = self.sb("actT", [128, 22, T], BF16, o); o += 22 * T * 2
        self.offWA = o
        o += 2 * 16384
        self.offWB = o
        o += 4 * 2048
        self.GB = self.sb("GB", [128, D], F32, o); o += D * 4
        self.offSIL = o
        self.SIL = [self.sb("SIL%d" % i, [128, T], F32, o + i * 4096) for i in range(2)]; o += 8192
        self.ident = self.sb("ident", [128, 128], BF16, o); o += 256
        self.stat = self.sb("stat", [128, 64], F32, o); o += 256
        self.cst = self.sb("cst", [128, 8], F32, o); o += 32
        self.offX = o
        self.biasT = self.sb("biasT", [128, 16, 8, 8], F32, o); o += 4096
        self.masks = self.sb("masks", [128, 4, 128], BF16, o); o += 1024
        self.rope = self.sb("rope", [128, 2, NTB, 8], F32, o); o += 512
        self.onesb = self.sb("onesb", [128, 128], BF16, o); o += 256
        self.bmT = self.sb("bmT", [128, 48], F32, o); o += 192
        self.psel = self.sb("psel", [128, 2], F32, o); o += 32
        self.lamv = self.sb("lamv", [128, 8], F32, o); o += 32
        self.fbias = self.sb("fbias", [128, 8], F32, o); o += 32
        self.Dtab = self.sb("Dtab", [128, 128], F32, o); o += 512
        self.Dend = self.sb("Dend", [128, 64], F32, o); o += 256
        self.wff = self.sb("wff", [128, KD, 8], BF16, o); o += 256
        assert SB_BASE + o <= 229344, o
        self.hb = [self.sb("hb%d" % i, [128, D], BF16, self.offB + i * 4096) for i in range(2)]
        self.xr = [self.sb("xr%d" % i, [128, D], F32, self.offB + 8192 + i * 8192) for i in range(2)]
        self.junk = self.sb("junk", [128, D], BF16, self.offB + 8192 + 16384)

        self.ps = nc.alloc_psum_tensor("ps", [128, 8, 512], F32).ap()
        self.bank = [Buf("bank%d" % i) for i in range(8)]

        self.bA = [Buf("A%d" % i) for i in range(NTB)]
        self.bhT = Buf("hT")
        self.bact = [Buf("act%d" % i) for i in range(22)]
        self.rB = self.bact
        self.bGB = Buf("GB")
        self.bSIL = [Buf("SIL0"), Buf("SIL1")]
        self.bhb = [Buf("hb0"), Buf("hb1")]
        self.bxr = [Buf("xr0"), Buf("xr1")]
        self.bjunk = Buf("junk")
        self.bstat = [Buf("stat%d" % i) for i in range(64)]
        self.bconst = Buf("const")
        self.dsmisc = P.dsem("misc")
        self.dsx = P.dsem("x")
        self.dsxr = [P.dsem("xr0"), P.dsem("xr1")]
        self.dsgb = P.dsem("gb")
        self.dsgb2 = P.dsem("gb2")
        self.dsout = P.dsem("out")
        self.ringA = [Slot(P, "ra%d" % i, self.offWA + i * 16384, 16384) for i in range(2)]
        self.ringB = [Slot(P, "rb%d" % i, self.offWB + i * 2048, 2048) for i in range(4)]
        self.ra_i = 0
        self.rb_i = 0
        self.ws = WStream(P)
        self.chunk_ctr = 0

        P.dma("sp", self.ident, self.ident_in, self.dsmisc, writes=[self.bconst])
        P.op("dve", lambda e: e.memset(self.cst[:, 0:1], EPS), writes=[self.bconst])
        P.op("dve", lambda e: e.memset(self.cst[:, 1:2], 1.0), writes=[self.bconst])
        P.op("dve", lambda e: e.memset(self.cst[:, 2:3], 0.0), writes=[self.bconst])
        P.op("dve", lambda e: e.memset(self.cst[:, 3:4], 4.0 * EPS), writes=[self.bconst])
        P.op("dve", lambda e: e.memset(self.onesb, 1.0), writes=[self.bconst])
        self.epsb = self.cst[:, 0:1]
        self.oneb = self.cst[:, 1:2]
        if self.stage >= 2:
            P.dma("sp", self.masks, self.masks_in, self.dsmisc, writes=[self.bconst])
            P.dma("sp", self.rope, self.rope_in, self.dsmisc, writes=[self.bconst])
            P.dma("sp", self.psel, self.psel_in, self.dsmisc, writes=[self.bconst])
            P.dma("sp", self.fbias, self.w["fox_f_bias"][0:1, :].partition_broadcast(128)[:, 0, :], self.dsmisc,
                  writes=[self.bconst])
            P.dma("sp", self.bmT, self.w["b_merge_gate"].rearrange("o (c p) -> p (o c)", p=128), self.dsmisc,
                  writes=[self.bconst], allow_slow_non_contiguous=True)
            P.dma("sp", self.lamv[:, 6:7], self.w["diff_head_g"].rearrange("o p -> p o"), self.dsmisc,
                  writes=[self.bconst], allow_slow_non_contiguous=True)
            self.dswff = P.dsem("wff")
            self.bwff = Buf("wff")
            P.dma("pool", self.wff, self.w["w_in"].rearrange("(k p) n -> p k n", p=128)[:, :, C_FF:C_FF + 8],
                  self.dswff, writes=[self.bwff], allow_slow_non_contiguous=True)
        P.join(self.dsmisc, [self.bconst])

        for tb in range(NTB):
            P.dma("sp", self.A[:, tb, :], self.x_in[tb * 128:(tb + 1) * 128, :], self.dsx, writes=[self.bA[tb]])
        P.join(self.dsx, self.bA)
        self.prenorm_A("ffn1_pre_g")
        self.ffn_core("ffn1_")
        if self.stage >= 2:
            self.trans("ffn1_post_g", 0.5, self.x_in, None, "mix_pre_g")
            bsp1 = self.spill(self.xsp1, "sp1")
            self.mixer()
            if self.stage >= 3:
                self.trans("mix_post_g", 1.0, self.xsp1, bsp1, "ffn2_pre_g")
            else:
                self.postnorm("mix_post_g", 1.0, self.xsp1, bsp1, final=False)
        else:
            self.postnorm("ffn1_post_g", 0.5, self.x_in, None, final=False)
        if self.stage >= 3:
            bsp = self.spill(self.xsp2, "sp2")
            self.ffn_core("ffn2_")
            self.postnorm("ffn2_post_g", 0.5, self.xsp2, bsp, final=True)
        else:
            for tb in range(NTB):
                P.dma("sp", self.out[tb * 128:(tb + 1) * 128, :], self.A[:, tb, :], self.dsout, reads=[self.bA[tb]])
        dso = self.dsout
        P.q["sp"].append(lambda e: e.wait_ge(dso.h, dso.count))
        P.emit_all()
        return nc

    def next_ra(self):
        s = self.ringA[self.ra_i % 2]
        self.ra_i += 1
        return s

    def next_rb(self):
        s = self.ringB[self.rb_i % 4]
        self.rb_i += 1
        return s

    def set_rB(self, new_bufs):
        inherit(new_bufs, list(self.rB) + self.bhb + self.bxr + [self.bjunk])
        self.rB = new_bufs

    def spill(self, dst, name):
        P = self.P
        ds = P.dsem(name)
        bsp = Buf(name)
        for tb in range(NTB):
            P.dma("sp", dst[tb * 128:(tb + 1) * 128, :], self.A[:, tb, :], ds, reads=[self.bA[tb]], writes=[bsp])
        P.join(ds, self.bA + [bsp])
        return bsp

    def load_gain(self, gname):
        src = self.w[gname][0:1, :].partition_broadcast(128)
        self.P.dma("sp", self.GB, src[:, 0, :], self.dsgb, writes=[self.bGB])

    def rstd_col(self, col, dim, coef=1.0):
        P = self.P
        st = self.stat[:, col:col + 1]
        bias = self.epsb if coef == 1.0 else self.cst[:, 3:4]
        assert coef in (1.0, 0.5)
        P.op("act", lambda e: e.activation(out=st, in_=st, func=AF.Sqrt, bias=bias, scale=1.0 / (dim * coef * coef)),
             reads=[self.bconst, self.bstat[col]], writes=[self.bstat[col]])
        P.op("dve", lambda e: e.reciprocal(out=st, in_=st), reads=[self.bstat[col]], writes=[self.bstat[col]])

    def prenorm(self, nblk, src_fn, gname, dst_fn, dstbuf):
        P = self.P
        self.load_gain(gname)
        P.op("dve", lambda e: e.memset(self.stat[:, 0:nblk], 0.0), writes=self.bstat[0:nblk])

        def stage_a(tb):
            sap, sbuf = src_fn(tb)
            P.op("act", lambda e: e.activation(out=self.junk, in_=sap, func=AF.Square,
                                               accum_out=self.stat[:, tb:tb + 1]),
                 reads=[sbuf], writes=[self.bstat[tb]], ow=[self.bjunk] + list(self.rB))
            self.rstd_col(tb, D)
            hb = self.hb[tb % 2]
            bhb = self.bhb[tb % 2]
            P.op("dve", lambda e: e.scalar_tensor_tensor(
                out=hb, in0=sap, scalar=self.stat[:, tb:tb + 1], in1=self.GB,
                op0=ALU.mult, op1=ALU.mult),
                reads=[sbuf, self.bstat[tb], self.bGB], ow=[bhb] + list(self.rB))

        def stage_b(tb):
            hb = self.hb[tb % 2]
            bhb = self.bhb[tb % 2]
            for half in range(2):
                bk = (2 * tb + half) % 8
                pst = self.ps[:, bk, :].bitcast(BF16)
                for kk in range(8):
                    k = half * 8 + kk
                    P.op("pe", lambda e, pst=pst, kk=kk, k=k: e.transpose(
                        out=pst[:, kk * 128:(kk + 1) * 128], in_=hb[:, k * 128:(k + 1) * 128], identity=self.ident),
                        reads=[bhb, self.bconst], writes=[self.bank[bk]], flag=(kk == 7))
                dst = dst_fn(tb, half)
                src = pst.rearrange("p (k n) -> p k n", k=8)
                if half == 0:
                    P.op("act", lambda e, dst=dst, src=src: e.copy(out=dst, in_=src),
                         reads=[self.bank[bk]], ow=[dstbuf])
                else:
                    P.op("dve", lambda e, dst=dst, src=src: e.tensor_copy(out=dst, in_=src),
                         reads=[self.bank[bk]], ow=[dstbuf])

        stage_a(0)
        for tb in range(nblk):
            if tb + 1 < nblk:
                stage_a(tb + 1)
            stage_b(tb)

    def prenorm_A(self, gname):
        self.prenorm(NTB, lambda tb: (self.A[:, tb, :], self.bA[tb]), gname,
                     lambda tb, half: self.hT[:, half * 8:(half + 1) * 8, tb * 128:(tb + 1) * 128], self.bhT)

    def postnorm(self, gname, coef, src, bsrc, final):
        P = self.P
        self.load_gain(gname)
        P.op("dve", lambda e: e.memset(self.stat[:, 8:16], 0.0), writes=self.bstat[8:16])
        for tb in range(NTB):
            c = 8 + tb
            xr = self.xr[tb % 2]
            bxr = self.bxr[tb % 2]
            P.dma("sp", xr, src[tb * 128:(tb + 1) * 128, :], self.dsxr[tb % 2],
                  reads=([bsrc] if bsrc is not None else []), writes=[bxr] + list(self.rB))
            P.op("act", lambda e, tb=tb, c=c: e.activation(out=self.junk, in_=self.A[:, tb, :], func=AF.Square,
                                                        accum_out=self.stat[:, c:c + 1]),
                 reads=[self.bA[tb]], writes=[self.bstat[c]], ow=[self.bjunk] + list(self.rB))
            self.rstd_col(c, D, coef)
            P.op("dve", lambda e, tb=tb: e.tensor_tensor(out=self.A[:, tb, :], in0=self.A[:, tb, :], in1=self.GB,
                                                          op=ALU.mult),
                 reads=[self.bGB, self.bA[tb]], writes=[self.bA[tb]])
            P.op("dve", lambda e, tb=tb, xr=xr, c=c: e.scalar_tensor_tensor(
                out=self.A[:, tb, :], in0=self.A[:, tb, :], scalar=self.stat[:, c:c + 1], in1=xr,
                op0=ALU.mult, op1=ALU.add),
                reads=[bxr, self.bstat[c], self.bA[tb]], writes=[self.bA[tb]])
            if final:
                P.dma("sp", self.out[tb * 128:(tb + 1) * 128, :], self.A[:, tb, :], self.dsout, reads=[self.bA[tb]])

    def trans(self, g_post, coef, src, bsrc, g_pre):
        P = self.P
        GB2 = self.sb("GB2", [128, D], F32, self.offSIL)
        bGB2 = Buf("GB2")
        inherit([bGB2], self.bSIL)
        self.load_gain(g_post)
        P.dma("sp", GB2, self.w[g_pre][0:1, :].partition_broadcast(128)[:, 0, :], self.dsgb2, writes=[bGB2])
        P.op("dve", lambda e: e.memset(self.stat[:, 0:16], 0.0), writes=self.bstat[0:16])

        def stage_p(tb):
            c = 8 + tb
            xr = self.xr[tb % 2]
            bxr = self.bxr[tb % 2]
            P.dma("sp", xr, src[tb * 128:(tb + 1) * 128, :], self.dsxr[tb % 2],
                  reads=([bsrc] if bsrc is not None else []), writes=[bxr] + list(self.rB))
            P.op("act", lambda e: e.activation(out=self.junk, in_=self.A[:, tb, :], func=AF.Square,
                                               accum_out=self.stat[:, c:c + 1]),
                 reads=[self.bA[tb]], writes=[self.bstat[c]], ow=[self.bjunk] + list(self.rB))
            self.rstd_col(c, D, coef)
            P.op("dve", lambda e: e.tensor_tensor(out=self.A[:, tb, :], in0=self.A[:, tb, :], in1=self.GB, op=ALU.mult),
                 reads=[self.bGB, self.bA[tb]], writes=[self.bA[tb]])
            P.op("dve", lambda e: e.scalar_tensor_tensor(
                out=self.A[:, tb, :], in0=self.A[:, tb, :], scalar=self.stat[:, c:c + 1], in1=xr,
                op0=ALU.mult, op1=ALU.add),
                reads=[bxr, self.bstat[c], self.bA[tb]], writes=[self.bA[tb]])

        def stage_a(tb):
            P.op("act", lambda e: e.activation(out=self.junk, in_=self.A[:, tb, :], func=AF.Square,
                                               accum_out=self.stat[:, tb:tb + 1]),
                 reads=[self.bA[tb]], writes=[self.bstat[tb]], ow=[self.bjunk] + list(self.rB))
            self.rstd_col(tb, D)
            hb = self.hb[tb % 2]
            bhb = self.bhb[tb % 2]
            P.op("dve", lambda e: e.scalar_tensor_tensor(
                out=hb, in0=self.A[:, tb, :], scalar=self.stat[:, tb:tb + 1], in1=GB2,
                op0=ALU.mult, op1=ALU.mult),
                reads=[self.bA[tb], self.bstat[tb], bGB2], ow=[bhb] + list(self.rB))

        def stage_b(tb):
            hb = self.hb[tb % 2]
            bhb = self.bhb[tb % 2]
            for half in range(2):
                bk = (2 * tb + half) % 8
                pst = self.ps[:, bk, :].bitcast(BF16)
                for kk in range(8):
                    k = half * 8 + kk
                    P.op("pe", lambda e, pst=pst, kk=kk, k=k: e.transpose(
                        out=pst[:, kk * 128:(kk + 1) * 128], in_=hb[:, k * 128:(k + 1) * 128], identity=self.ident),
                        reads=[bhb, self.bconst], writes=[self.bank[bk]], flag=(kk == 7))
                dst = self.hT[:, half * 8:(half + 1) * 8, tb * 128:(tb + 1) * 128]
                src_ = pst.rearrange("p (k n) -> p k n", k=8)
                P.op("act", lambda e, dst=dst, src_=src_: e.copy(out=dst, in_=src_),
                     reads=[self.bank[bk]], ow=[self.bhT])

        for t in range(NTB + 2):
            if t < NTB:
                stage_p(t)
            if 1 <= t <= NTB:
                stage_a(t - 1)
            if t >= 2:
                stage_b(t - 2)
        inherit(self.bSIL, [bGB2])

    def tm_steps(self, wview, c0, lhs_fn, nk, evac_fn, kbase=0):
        P = self.P
        assert nk % 2 == 0
        for kb in range(nk // 2):
            slot = self.next_rb()
            tile = self.sb("wtm", [128, 2, 512], BF16, slot.off)
            kg = kbase + kb * 2

            def dma_fn(slot=slot, tile=tile, kg=kg):
                P.dma("pool", tile, wview[:, kg:kg + 2, c0:c0 + 512], slot.ds, writes=[slot.buf])

            def comp_fn(slot=slot, tile=tile, kb=kb):
                for kk in range(2):
                    kl = kb * 2 + kk
                    for tb in range(NTB):
                        lap, lbuf = lhs_fn(kl, tb)
                        last = (kl == nk - 1)
                        P.op("pe", lambda e, kk=kk, kl=kl, tb=tb, lap=lap, last=last: e.matmul(
                            self.ps[:, tb, :], lhsT=lap, rhs=tile[:, kk, :], start=(kl == 0), stop=last),
                            reads=[slot.buf, lbuf], writes=[self.bank[tb]], flag=last or (kk == 1 and tb == NTB - 1))
                if kb == nk // 2 - 1:
                    for tb in range(NTB):
                        evac_fn(tb)

            self.ws.add(slot, dma_fn, comp_fn)

    def fm_steps(self, wview, c0, nch, rhs, rhs_buf, evac_fn, nk=KD, ntg=2, wtile_cols=None):
        P = self.P
        slot = self.next_ra()
        tile = self.sb("wfm", [128, nk, nch * 128], BF16, slot.off)

        def dma_fn(slot=slot, tile=tile):
            P.dma("pool", tile, wview[:, 0:nk, c0:c0 + nch * 128], slot.ds, writes=[slot.buf])

        def comp_fn(slot=slot, tile=tile):
            for c in range(nch):
                pair = (self.chunk_ctr % 4) * 2
                self.chunk_ctr += 1
                for k in range(nk):
                    for tg in range(ntg):
                        bk = pair + tg
                        last = (k == nk - 1 and tg == ntg - 1)
                        ncols = 512 if ntg == 2 else rhs.shape[2]
                        P.op("pe", lambda e, bk=bk, k=k, tg=tg, c=c, ncols=ncols: e.matmul(
                            self.ps[:, bk, 0:ncols],
                            lhsT=tile[:, k, c * 128:(c + 1) * 128],
                            rhs=rhs[:, k, tg * 512:(tg + 1) * 512] if ntg == 2 else rhs[:, k, :],
                            start=(k == 0), stop=(k == nk - 1)),
                            reads=[slot.buf, rhs_buf], writes=[self.bank[bk]], flag=last)
                evac_fn(c, pair)

        self.ws.add(slot, dma_fn, comp_fn)

    def ffn_core(self, pre):
        P = self.P
        wgv = self.w[pre + "w_gate"].rearrange("(k p) n -> p k n", p=128)
        wuv = self.w[pre + "w_up"].rearrange("(k p) n -> p k n", p=128)
        wdv = self.w[pre + "w_down"].rearrange("(k p) n -> p k n", p=128)
        ws = self.ws
        ctr = [0]
        for half in range(2):
            for fg in range(11):
                slot = self.next_ra()
                tile = self.sb("wgu", [128, 2, KD, 256], BF16, slot.off)
                c0 = (half * 22 + fg * 2) * 128

                def dma_fn(slot=slot, tile=tile, c0=c0):
                    P.dma("pool", tile[:, 0], wgv[:, :, c0:c0 + 256], slot.ds, writes=[slot.buf])
                    P.dma("pool", tile[:, 1], wuv[:, :, c0:c0 + 256], slot.ds, writes=[slot.buf])

                def comp_fn(slot=slot, tile=tile, fg=fg):
                    for fi in range(2):
                        fl = fg * 2 + fi
                        par = ctr[0] % 2
                        ctr[0] += 1
                        b0 = par * 4
                        for gu in range(2):
                            for k in range(KD):
                                for tg in range(2):
                                    bk = b0 + gu * 2 + tg
                                    last = (k == KD - 1 and tg == 1)
                                    P.op("pe", lambda e, bk=bk, gu=gu, k=k, tg=tg, fi=fi: e.matmul(
                                        self.ps[:, bk, :], lhsT=tile[:, gu, k, fi * 128:(fi + 1) * 128],
                                        rhs=self.hT[:, k, tg * 512:(tg + 1) * 512],
                                        start=(k == 0), stop=(k == KD - 1)),
                                        reads=[slot.buf, self.bhT], writes=[self.bank[bk]], flag=last)
                        sil = self.SIL[par]
                        bsil = self.bSIL[par]
                        pg = self.ps[:, b0:b0 + 2, :].rearrange("p a n -> p (a n)")
                        pu = self.ps[:, b0 + 2:b0 + 4, :].rearrange("p a n -> p (a n)")
                        P.op("act", lambda e, sil=sil, pg=pg: e.activation(out=sil, in_=pg, func=AF.Silu),
                             reads=[self.bank[b0], self.bank[b0 + 1]], ow=[bsil])
                        P.op("dve", lambda e, sil=sil, pu=pu, fl=fl: e.tensor_tensor(
                            out=self.actT[:, fl, :], in0=pu, in1=sil, op=ALU.mult),
                            reads=[bsil, self.bank[b0 + 2], self.bank[b0 + 3]], ow=[self.bact[fl]])

                ws.add(slot, dma_fn, comp_fn)
            for ng in range(4):
                def evac_fn(tb, ng=ng, half=half):
                    ydst = self.A[:, tb, ng * 512:(ng + 1) * 512]
                    if half == 0:
                        if tb % 2 == 0:
                            P.op("act", lambda e: e.copy(out=ydst, in_=self.ps[:, tb, :]),
                                 reads=[self.bank[tb]], ow=[self.bA[tb]])
                        else:
                            P.op("dve", lambda e: e.tensor_copy(out=ydst, in_=self.ps[:, tb, :]),
                                 reads=[self.bank[tb]], ow=[self.bA[tb]])
                    else:
                        P.op("dve", lambda e: e.tensor_tensor(out=ydst, in0=self.ps[:, tb, :], in1=ydst, op=ALU.add),
                             reads=[self.bank[tb], self.bA[tb]], writes=[self.bA[tb]])

                self.tm_steps(wdv, ng * 512, lambda kl, tb: (self.actT[:, kl, tb * 128:(tb + 1) * 128], self.bact[kl]),
                              22, evac_fn, kbase=half * 22)
        ws.run()

    def mixer(self):
        P = self.P
        nc = self.nc
        A0 = self.offA
        B0 = self.offB

        memx = self.sb("memx", [128, 2, D], F32, A0)
        fkst = [self.sb("fkst%d" % i, [128, T], BF16, A0 + 16384 + i * 2048) for i in range(2)]
        fvst = self.sb("fvst", [128, 8, NTB, 128], BF16, A0 + 20480)
        dkTst = self.sb("dkTst", [128, 4, T], BF16, A0 + 36864)
        dvst = self.sb("dvst", [128, 4, NTB, 128], BF16, A0 + 45056)
        ropeb = [self.sb("ropeb%d" % i, [128, 512], BF16, A0 + 53248 + i * 1024) for i in range(2)]
        ropet = self.sb("ropet", [128, 4, 8, 8], F32, A0 + 55296)
        spst = self.sb("spst", [128, 64], F32, A0 + 56320)
        b_memx = [Buf("memx0"), Buf("memx1")]
        b_fkst = [Buf("fkst0"), Buf("fkst1")]
        b_fvst = Buf("fvst"); b_dkTst = Buf("dkTst"); b_dvst = Buf("dvst")
        b_ropeb = [Buf("ropeb0"), Buf("ropeb1")]
        b_ropet = Buf("ropet"); b_spst = Buf("spst")
        projA = b_memx + b_fkst + [b_fvst, b_dkTst, b_dvst] + b_ropeb + [b_ropet, b_spst]
        inherit(projA, self.bA)
        ds_mem = P.dsem("mem")
        ds_fk = [P.dsem("fk0"), P.dsem("fk1")]
        ds_fv = P.dsem("fv"); ds_dk = P.dsem("dk"); ds_dv = P.dsem("dv"); ds_sp = P.dsem("spx")

        for blk in range(2):
            P.dma("sp", memx[:, blk, :], self.mem_in[blk * 128:(blk + 1) * 128, :], ds_mem, writes=[b_memx[blk]])
        P.join(ds_mem, b_memx)
        mhT = self.sb("mhT", [128, KD, 256], BF16, B0 + 36864)
        b_mhT = Buf("mhT")
        inherit([b_mhT], list(self.rB))
        self.prenorm(2, lambda tb: (memx[:, tb, :], b_memx[tb]), "mem_norm_g",
                     lambda tb, half: mhT[:, half * 8:(half + 1) * 8, tb * 128:(tb + 1) * 128], b_mhT)
        fqT = self.sb("fqT", [128, 8, T], BF16, B0)
        dqT = self.sb("dqT", [128, 4, T], BF16, B0 + 16384)
        mqT = self.sb("mqT", [128, 4, T], BF16, B0 + 24576)
        mkT = self.sb("mkT", [128, 4, 256], BF16, B0 + 32768)
        mv = self.sb("mv", [128, 2, 512], BF16, B0 + 34816)
        Y1 = self.sb("Y1", [128, T], F32, B0 + 36864)
        Y2 = self.sb("Y2", [128, T], F32, B0 + 40960)
        b_fqT = [Buf("fqT%d" % i) for i in range(8)]
        b_dqT = Buf("dqT"); b_mqT = [Buf("mqT%d" % i) for i in range(4)]
        b_mkT = Buf("mkT"); b_mv = Buf("mv"); b_Y1 = Buf("Y1"); b_Y2 = Buf("Y2")
        mixB = b_fqT + [b_dqT] + b_mqT + [b_mkT, b_mv, b_mhT]
        self.set_rB(mixB)
        inherit([b_Y1, b_Y2], [b_mhT])
        self.rB = mixB + [b_Y1, b_Y2]

        wkv = self.w["w_mem_kv"].rearrange("(k p) n -> p k n", p=128)
        win = self.w["w_in"].rearrange("(k p) n -> p k n", p=128)

        b_XKs = Buf("XKs"); b_XVs = Buf("XVs"); b_XDKs = Buf("XDKs"); b_XDVs = Buf("XDVs"); b_XLs = Buf("XLs")
        b_XKd = Buf("XKd"); b_XVd = Buf("XVd"); b_XDKd = Buf("XDKd"); b_XDVd = Buf("XDVd"); b_XLd = Buf("XLd")
        ds_cc = [P.dsem("cc%d" % i) for i in range(5)]

        def gather(i, src, bsrc, dst, bdst):
            def step():
                P.aop("pool", lambda e: e.collective_compute(
                    "AllGather", ALU.bypass, replica_groups=RG, ins=[src], outs=[dst]),
                    ds_cc[i], 1, reads=[bsrc], writes=[bdst])
            self.ws.add(None, None, step)

        for grp in range(2):
            def evac_fk(c, pair, grp=grp):
                h = grp * 4 + c
                st = fkst[h % 2]; bst = b_fkst[h % 2]
                src = self.ps[:, pair:pair + 2, :].rearrange("p a n -> p (a n)")
                if h % 2 == 0:
                    P.op("act", lambda e: e.copy(out=st, in_=src), reads=[self.bank[pair], self.bank[pair + 1]], ow=[bst])
                else:
                    P.op("dve", lambda e: e.tensor_copy(out=st, in_=src),
                         reads=[self.bank[pair], self.bank[pair + 1]], ow=[bst])
                P.dma("sp", self.XKs[h * 128:(h + 1) * 128, :], st, ds_fk[h % 2], reads=[bst], writes=[b_XKs])
            self.fm_steps(win, C_FK + grp * 512, 4, self.hT, self.bhT, evac_fk)
        gather(0, self.XKs, b_XKs, self.XKd, b_XKd)

        lhs_h = lambda kl, tb: (self.hT[:, kl, tb * 128:(tb + 1) * 128], self.bhT)
        for grp in range(2):
            def evac_fv(tb, grp=grp):
                dst = fvst[:, grp * 4:(grp + 1) * 4, tb, :]
                src = self.ps[:, tb, :].rearrange("p (h d) -> p h d", h=4)
                if tb % 2 == 0:
                    P.op("act", lambda e: e.copy(out=dst, in_=src), reads=[self.bank[tb]], ow=[b_fvst])
                else:
                    P.op("dve", lambda e: e.tensor_copy(out=dst, in_=src), reads=[self.bank[tb]], ow=[b_fvst])
            self.tm_steps(win, C_FV + grp * 512, lhs_h, KD, evac_fv)

        def st_fv():
            P.dma("sp", self.XVs, fvst.rearrange("p h t d -> p (h t d)"), ds_fv, reads=[b_fvst], writes=[b_XVs])
        self.ws.add(None, None, st_fv)
        gather(1, self.XVs, b_XVs, self.XVd, b_XVd)

        def rope_evac(tb, dstT, bdstT, scale):
            rb = ropeb[tb % 2]; brb = b_ropeb[tb % 2]
            pv = self.ps[:, tb, :].rearrange("p (g c) -> p g c", c=64)
            rbv = rb.rearrange("p (g c) -> p g c", c=64)
            cos = self.rope[:, 0, tb, :].unsqueeze(1).broadcast_to([128, 8, 8])
            sin = self.rope[:, 1, tb, :].unsqueeze(1).broadcast_to([128, 8, 8])
            x1 = pv[:, :, 0:8]; x2 = pv[:, :, 8:16]
            t = [ropet[:, i] for i in range(4)]
            rd = [self.bconst, self.bank[tb]]
            P.op("dve", lambda e: e.tensor_tensor(out=t[0], in0=x1, in1=cos, op=ALU.mult), reads=rd, ow=[b_ropet])
            P.op("dve", lambda e: e.tensor_tensor(out=t[1], in0=x2, in1=sin, op=ALU.mult), reads=rd, ow=[b_ropet])
            P.op("dve", lambda e: e.tensor_tensor(out=t[2], in0=x2, in1=cos, op=ALU.mult), reads=rd, ow=[b_ropet])
            P.op("dve", lambda e: e.tensor_tensor(out=t[3], in0=x1, in1=sin, op=ALU.mult), reads=rd, ow=[b_ropet])
            P.op("dve", lambda e: e.tensor_tensor(out=rbv[:, :, 0:8], in0=t[0], in1=t[1], op=ALU.subtract),
                 reads=[b_ropet], ow=[brb])
            P.op("dve", lambda e: e.tensor_tensor(out=rbv[:, :, 8:16], in0=t[2], in1=t[3], op=ALU.add),
                 reads=[b_ropet], ow=[brb])
            P.op("act", lambda e: e.copy(out=rbv[:, :, 16:64], in_=pv[:, :, 16:64]), reads=[self.bank[tb]], ow=[brb])
            pst = self.ps[:, tb, :].bitcast(BF16)
            for h in range(4):
                P.op("pe", lambda e, h=h: e.transpose(out=pst[:, h * 128:(h + 1) * 128],
                                                      in_=rb[:, h * 128:(h + 1) * 128], identity=self.ident),
                     reads=[brb, self.bconst], writes=[self.bank[tb]], flag=(h == 3))
            src = pst[:, 0:512].rearrange("p (h n) -> p h n", h=4)
            dst = dstT[:, :, tb * 128:(tb + 1) * 128]
            P.op("act", lambda e: e.mul(out=dst, in_=src, mul=scale), reads=[self.bank[tb]], ow=[bdstT])

        self.tm_steps(win, C_DK, lhs_h, KD, lambda tb: rope_evac(tb, dkTst, b_dkTst, 1.0))

        def st_dk():
            P.dma("sp", self.XDKs.rearrange("(h p) t -> p h t", p=128), dkTst, ds_dk, reads=[b_dkTst], writes=[b_XDKs])
        self.ws.add(None, None, st_dk)
        gather(2, self.XDKs, b_XDKs, self.XDKd, b_XDKd)

        def evac_dv(tb):
            dst = dvst[:, :, tb, :]
            src = self.ps[:, tb, :].rearrange("p (h d) -> p h d", h=4)
            if tb % 2 == 0:
                P.op("act", lambda e: e.copy(out=dst, in_=src), reads=[self.bank[tb]], ow=[b_dvst])
            else:
                P.op("dve", lambda e: e.tensor_copy(out=dst, in_=src), reads=[self.bank[tb]], ow=[b_dvst])
        self.tm_steps(win, C_DV, lhs_h, KD, evac_dv)

        def st_dv():
            P.dma("sp", self.XDVs, dvst.rearrange("p h t d -> p (h t d)"), ds_dv, reads=[b_dvst], writes=[b_XDVs])
        self.ws.add(None, None, st_dv)
        gather(3, self.XDVs, b_XDVs, self.XDVd, b_XDVd)

        def ff_step():
            for tb in range(NTB):
                for k in range(KD):
                    P.op("pe", lambda e, tb=tb, k=k: e.matmul(
                        self.ps[:, 0, tb * 8:(tb + 1) * 8], lhsT=self.hT[:, k, tb * 128:(tb + 1) * 128],
                        rhs=self.wff[:, k, :], start=(k == 0), stop=(k == KD - 1)),
                        reads=[self.bhT, self.bwff], writes=[self.bank[0]], flag=(k == KD - 1 and tb == NTB - 1))
            fb = self.fbias.unsqueeze(1).broadcast_to([128, NTB, 8])
            P.op("dve", lambda e: e.tensor_tensor(out=spst.rearrange("p (t h) -> p t h", h=8),
                                                  in0=self.ps[:, 0, 0:64].rearrange("p (t h) -> p t h", h=8),
                                                  in1=fb, op=ALU.add),
                 reads=[self.bconst, self.bank[0]], writes=[b_spst])
            P.op("act", lambda e: e.activation(out=spst, in_=spst, func=AF.Exp, scale=-1.0), reads=[b_spst], writes=[b_spst])
            P.op("act", lambda e: e.activation(out=spst, in_=spst, func=AF.Ln, bias=self.oneb, scale=1.0),
                 reads=[self.bconst, b_spst], writes=[b_spst])
            P.dma("sp", self.XLs, spst, ds_sp, reads=[b_spst], writes=[b_XLs])
        self.ws.add(None, None, ff_step)
        gather(4, self.XLs, b_XLs, self.XLd, b_XLd)

        def evac_mk(c, pair):
            P.op("act", lambda e: e.copy(out=mkT[:, c, :], in_=self.ps[:, pair, 0:256]),
                 reads=[self.bank[pair]], ow=[b_mkT])
        self.fm_steps(wkv, 0, 4, mhT, b_mhT, evac_mk, ntg=1)
        for kb in range(KD // 2):
            slot = self.next_rb()
            tile = self.sb("wmv", [128, 2, 512], BF16, slot.off)

            def dma_fn(slot=slot, tile=tile, kb=kb):
                P.dma("pool", tile, wkv[:, kb * 2:kb * 2 + 2, 512:1024], slot.ds, writes=[slot.buf])

            def comp_fn(slot=slot, tile=tile, kb=kb):
                for kk in range(2):
                    kl = kb * 2 + kk
                    for blk in range(2):
                        last = (kl == KD - 1)
                        P.op("pe", lambda e, kk=kk, kl=kl, blk=blk, last=last: e.matmul(
                            self.ps[:, blk, :], lhsT=mhT[:, kl, blk * 128:(blk + 1) * 128], rhs=tile[:, kk, :],
                            start=(kl == 0), stop=last),
                            reads=[slot.buf, b_mhT], writes=[self.bank[blk]], flag=last or (kk == 1 and blk == 1))
                if kb == KD // 2 - 1:
                    for blk in range(2):
                        P.op("dve", lambda e, blk=blk: e.tensor_copy(out=mv[:, blk, :], in_=self.ps[:, blk, :]),
                             reads=[self.bank[blk]], ow=[b_mv])
            self.ws.add(slot, dma_fn, comp_fn)

        sc_f = 128.0 ** -0.5
        for grp in range(2):
            def evac_fq(c, pair, grp=grp):
                h = grp * 4 + c
                src = self.ps[:, pair:pair + 2, :].rearrange("p a n -> p (a n)")
                P.op("act", lambda e: e.mul(out=fqT[:, h, :], in_=src, mul=sc_f),
                     reads=[self.bank[pair], self.bank[pair + 1]], ow=[b_fqT[h]])
            self.fm_steps(win, C_FQ + grp * 512, 4, self.hT, self.bhT, evac_fq)

        def evac_mq(c, pair):
            src = self.ps[:, pair:pair + 2, :].rearrange("p a n -> p (a n)")
            P.op("dve", lambda e: e.tensor_scalar(out=mqT[:, c, :], in0=src, scalar1=sc_f, scalar2=None, op0=ALU.mult),
                 reads=[self.bank[pair], self.bank[pair + 1]], ow=[b_mqT[c]])
        self.fm_steps(win, C_MQ, 4, self.hT, self.bhT, evac_mq)
        self.tm_steps(win, C_DQ, lhs_h, KD, lambda tb: rope_evac(tb, dqT, b_dqT, 64.0 ** -0.5))
        self.ws.run()
        self.dump("fqT", fqT, b_fqT, [128, 8, T], BF16)
        self.dump("dqT", dqT, [b_dqT], [128, 4, T], BF16)
        self.dump("mkT", mkT, [b_mkT], [128, 4, 256], BF16)
        self.dump("mv", mv, [b_mv], [128, 2, 512], BF16)

        S0 = self.offSIL
        lamt = self.sb("lamt", [128, 4, 64], F32, S0)
        Lt = self.sb("Lt", [128, 2, 64], F32, S0 + 1024)
        Lpre = self.sb("Lpre", [128, 2, 64], F32, S0 + 1536)
        cmat = self.sb("cmat", [128, 3, 128], F32, S0 + 2048)
        b_S = Buf("silscratch")
        inherit([b_S], self.bSIL)
        ds_l = P.dsem("lam")
        for i, nm in enumerate(("diff_lambda_q1", "diff_lambda_k1", "diff_lambda_q2", "diff_lambda_k2")):
            P.dma("sp", lamt[:, i, :], self.w[nm][0:1, :].partition_broadcast(128)[:, 0, :], ds_l, writes=[b_S])
        P.dma("sp", cmat, self.cmat_in, ds_l, writes=[b_S])
        P.join(ds_l, [b_S])
        lv = self.lamv
        b_lam = Buf("lam")
        inherit([b_lam], [self.bconst])
        P.op("dve", lambda e: e.tensor_tensor(out=lamt[:, 0, :], in0=lamt[:, 0, :], in1=lamt[:, 1, :], op=ALU.mult),
             writes=[b_S])
        P.op("dve", lambda e: e.tensor_tensor(out=lamt[:, 2, :], in0=lamt[:, 2, :], in1=lamt[:, 3, :], op=ALU.mult),
             writes=[b_S])
        P.op("dve", lambda e: e.tensor_reduce(out=lv[:, 0:1], in_=lamt[:, 0, :], axis=AX.X, op=ALU.add),
             reads=[b_S], writes=[b_lam])
        P.op("dve", lambda e: e.tensor_reduce(out=lv[:, 1:2], in_=lamt[:, 2, :], axis=AX.X, op=ALU.add),
             reads=[b_S], writes=[b_lam])
        P.op("act", lambda e: e.activation(out=lv[:, 2:4], in_=lv[:, 0:2], func=AF.Exp), writes=[b_lam])
        P.op("dve", lambda e: e.tensor_tensor(out=lv[:, 4:5], in0=lv[:, 2:3], in1=lv[:, 3:4], op=ALU.subtract),
             writes=[b_lam])
        P.op("dve", lambda e: e.tensor_scalar(out=lv[:, 5:6], in0=lv[:, 4:5], scalar1=-1.0, scalar2=-LAM_INIT,
                                              op0=ALU.mult, op1=ALU.add), writes=[b_lam])
        P.op("dve", lambda e: e.tensor_scalar(out=lv[:, 6:7], in0=lv[:, 6:7], scalar1=1.0 - LAM_INIT, scalar2=None,
                                              op0=ALU.mult), writes=[b_lam])
        neglam = lv[:, 5:6]
        gsc = lv[:, 6:7]

        ds_lt = P.dsem("lt")
        b_Lt = Buf("Lt")
        inherit([b_Lt], [b_S])
        P.dma("sp", Lt, self.XLd.rearrange("(r s) c -> s r c", r=2), ds_lt, reads=[b_XLd], writes=[b_Lt])
        Ltv = Lt.rearrange("p r (j h) -> p r j h", h=8)
        Lpv = Lpre.rearrange("p r (j h) -> p r j h", h=8)
        P.op("dve", lambda e: e.memset(Lpv[:, 0, 0, :], 0.0), writes=[b_S])
        for g in range(1, 16):
            r, j = g % 2, g // 2
            rp, jp = (g - 1) % 2, (g - 1) // 2
            P.op("dve", lambda e, r=r, j=j, rp=rp, jp=jp: e.tensor_tensor(
                out=Lpv[:, r, j, :], in0=Lpv[:, rp, jp, :], in1=Ltv[:, rp, jp, :], op=ALU.add), reads=[b_Lt], writes=[b_S])
        b_D = Buf("Dtab")
        inherit([b_D], [self.bconst])
        P.op("pe", lambda e: e.matmul(self.ps[:, 0, 0:128], lhsT=cmat[:, 0, :], rhs=Lt.rearrange("p r c -> p (r c)"),
                                      start=True, stop=False), reads=[b_S, b_Lt], writes=[self.bank[0]], flag=False)
        P.op("pe", lambda e: e.matmul(self.ps[:, 0, 0:128], lhsT=cmat[:, 1, :], rhs=Lpre.rearrange("p r c -> p (r c)"),
                                      start=False, stop=True), reads=[b_S], writes=[self.bank[0]])
        P.op("dve", lambda e: e.tensor_copy(out=self.Dtab, in_=self.ps[:, 0, 0:128]), reads=[self.bank[0]], writes=[b_D])
        P.op("pe", lambda e: e.matmul(self.ps[:, 1, 0:128], lhsT=cmat[:, 2, :], rhs=self.Dtab, start=True, stop=True),
             reads=[b_S, b_D], writes=[self.bank[1]])
        P.op("dve", lambda e: e.tensor_scalar(out=self.Dend, in0=self.ps[:, 1, 0:64], scalar1=self.psel[:, 0:1],
                                              scalar2=None, op0=ALU.mult),
             reads=[self.bconst, self.bank[1]], writes=[b_D])
        P.op("dve", lambda e: e.scalar_tensor_tensor(out=self.Dend, in0=self.ps[:, 1, 64:128], scalar=self.psel[:, 1:2],
                                                     in1=self.Dend, op0=ALU.mult, op1=ALU.add),
             reads=[self.bconst, self.bank[1], b_D], writes=[b_D])
        in0 = self.Dtab.rearrange("p (g h) -> p g h", h=8).unsqueeze(3).broadcast_to([128, 16, 8, 8])
        in1 = self.Dend.rearrange("p (j h) -> p h j", h=8).unsqueeze(1).broadcast_to([128, 16, 8, 8])
        b_bias = Buf("biasT")
        inherit([b_bias], [self.bconst])
        P.op("dve", lambda e: e.tensor_tensor(out=self.biasT, in0=in0, in1=in1, op=ALU.subtract),
             reads=[b_D], writes=[b_bias])
        bflat = self.biasT.rearrange("p a b c -> p (a b c)")
        P.op("dve", lambda e: e.tensor_scalar_min(out=bflat, in0=bflat, scalar1=0.0), writes=[b_bias])
        self.dump("Dtab", self.Dtab, [b_D], [128, 128], F32)
        self.dump("biasT", bflat, [b_bias], [128, 1024], F32)

        yfoxT = self.sb("yfoxT", [128, 8, T], BF16, A0)
        ydiffT = self.sb("ydiffT", [128, 4, T], BF16, A0 + 16384)
        ymemT = self.sb("ymemT", [128, 4, T], BF16, A0 + 24576)
        KTb = [self.sb("KT%d" % i, [128, 2, T], BF16, A0 + 32768 + i * 4096) for i in range(2)]
        Vb = [self.sb("V%d" % i, [128, 2, NTB, 128], BF16, A0 + 40960 + i * 4096) for i in range(2)]
        PT = [self.sb("PT%d" % i, [128, T], BF16, A0 + 49152 + i * 2048) for i in range(3)]
        recip = self.sb("recip", [128, T], F32, A0 + 55296)
        sq = self.sb("sq", [128, T], BF16, A0 + 59392)
        accS = self.sb("accS", [128, T], F32, A0 + 61440)
        b_yfox = [Buf("yfox%d" % i) for i in range(8)]
        b_ydiff = [Buf("ydiff%d" % i) for i in range(4)]
        b_ymem = [Buf("ymem%d" % i) for i in range(4)]
        b_KT = [Buf("KT0"), Buf("KT1")]; b_V = [Buf("V0"), Buf("V1")]
        b_PT = [[Buf("PT%d_%d" % (i, j)) for j in range(8)] for i in range(3)]
        b_recip = Buf("recip"); b_sq = Buf("sq"); b_accS = Buf("accS")
        b_PTall = b_PT[0] + b_PT[1] + b_PT[2]
        attnA = b_yfox + b_ydiff + b_ymem + b_KT + b_V + b_PTall + [b_recip, b_sq, b_accS]
        inherit(attnA, projA)
        ds_kv = [P.dsem("kv0"), P.dsem("kv1")]
        self.pt_i = 0
        self.sp_i = 0
        accv = self.ps[:, 4:6, :].rearrange("p a n -> p (a n)")
        rsv = self.ps[:, 6:8, :].rearrange("p a n -> p (a n)")

        passes = []
        deferred = []

        def add_pass(nkb, q_ap, q_bufs, k_fn, v_fn, kv_bufs, causal, bias_fn, mask_fn, pre_fn, post_fn):
            passes.append(dict(nkb=nkb, q_ap=q_ap, q_bufs=q_bufs, k_fn=k_fn, v_fn=v_fn, kv_bufs=kv_bufs,
                               causal=causal, bias_fn=bias_fn, mask_fn=mask_fn, pre_fn=pre_fn, post_fn=post_fn))

        def item_geom(p, g):
            jmin = g // 2 if p["causal"] else 0
            c0 = jmin * 128
            parts = []
            if c0 < 512:
                parts.append((0, c0, 512))
            parts.append((1, max(c0, 512), 1024))
            return jmin, c0, parts

        pre_done = set()

        def pre(k):
            if 0 <= k < len(passes) and k not in pre_done:
                pre_done.add(k)
                if passes[k]["pre_fn"] is not None:
                    passes[k]["pre_fn"]()

        def emit_qk(p, g, st):
            jmin, c0, parts = item_geom(p, g)
            sp = self.sp_i % 2; self.sp_i += 1
            pi = self.pt_i % 3; self.pt_i += 1
            st["sp"] = sp; st["pi"] = pi
            k_fn = p["k_fn"]; q_ap = p["q_ap"]
            for ip, (hb_, lo, hi) in enumerate(parts):
                bk = 2 * sp + hb_
                l0 = lo - hb_ * 512
                P.op("pe", lambda e, bk=bk, lo=lo, hi=hi, g=g, l0=l0: e.matmul(
                    self.ps[:, bk, l0:l0 + (hi - lo)],
                    lhsT=k_fn(g), rhs=q_ap[:, lo:hi], start=True, stop=True),
                    reads=list(p["kv_bufs"]) + list(p["q_bufs"]), writes=[self.bank[bk]], flag=(ip == len(parts) - 1))

        def emit_rest(p, g, st):
            jmin, c0, parts = item_geom(p, g)
            sp = st["sp"]; pi = st["pi"]
            nkb = p["nkb"]
            pt = PT[pi]; bpt = b_PT[pi]
            bias_fn = p["bias_fn"]; mask_fn = p["mask_fn"]; v_fn = p["v_fn"]
            sbanks = [self.bank[2 * sp + hb_] for (hb_, _, _) in parts]
            if bias_fn is None:
                sv = self.ps[:, 2 * sp:2 * sp + 2, :].rearrange("p a n -> p (a n)")
                P.op("act", lambda e: e.activation(out=pt[:, c0:T], in_=sv[:, c0:T], func=AF.Exp),
                     reads=sbanks, ow=bpt[jmin:8])
            else:
                for j in range(jmin, 8):
                    bk = 2 * sp + j // 4
                    P.op("act", lambda e, bk=bk, j=j: e.activation(
                        out=pt[:, j * 128:(j + 1) * 128], in_=self.ps[:, bk, (j % 4) * 128:(j % 4 + 1) * 128],
                        func=AF.Exp, bias=bias_fn(g, j), scale=1.0),
                        reads=[b_bias, self.bank[bk]], ow=[bpt[j]])
            if mask_fn is not None:
                P.op("pool", lambda e: e.tensor_tensor(
                    out=pt[:, c0:c0 + 128], in0=pt[:, c0:c0 + 128], in1=mask_fn(g), op=ALU.mult),
                    reads=[self.bconst, bpt[jmin]], writes=[bpt[jmin]])
            for ip, (hb_, lo, hi) in enumerate(parts):
                l0 = lo - hb_ * 512
                P.op("pe", lambda e, hb_=hb_, l0=l0, lo=lo, hi=hi: e.matmul(
                    self.ps[:, 4 + hb_, l0:l0 + (hi - lo)], lhsT=v_fn(g), rhs=pt[:, lo:hi],
                    start=(g == 0), stop=(g == nkb - 1), skip_group_check=True),
                    reads=list(p["kv_bufs"]) + bpt[lo // 128:hi // 128], writes=[self.bank[4 + hb_]], flag=False)
                P.op("pe", lambda e, hb_=hb_, l0=l0, lo=lo, hi=hi: e.matmul(
                    self.ps[:, 6 + hb_, l0:l0 + (hi - lo)], lhsT=self.onesb, rhs=pt[:, lo:hi],
                    start=(g == 0), stop=(g == nkb - 1), skip_group_check=True),
                    reads=bpt[lo // 128:hi // 128] + [self.bconst], writes=[self.bank[6 + hb_]],
                    flag=(ip == len(parts) - 1))
            if g == nkb - 1:
                d = p["post_fn"](sp)
                if d is not None:
                    deferred.append([4, d])
            for dd in list(deferred):
                dd[0] -= 1
                if dd[0] <= 0:
                    deferred.remove(dd)
                    dd[1](sp)

        def run_passes():
            items = [(p, g, {}) for p in passes for g in range(p["nkb"])]
            pre(0); pre(1)
            emit_qk(*items[0])
            for i, it in enumerate(items):
                if i + 1 < len(items):
                    emit_qk(*items[i + 1])
                emit_rest(*it)
                if it[1] == it[0]["nkb"] - 1:
                    pre(passes.index(it[0]) + 2)
            for dd in deferred:
                dd[1](it[2]["sp"])

        def normalize(dst, bdst):
            P.op("act", lambda e: e.copy(out=accS, in_=accv), reads=[self.bank[4], self.bank[5]], ow=[b_accS])
            P.op("dve", lambda e: e.tensor_copy(out=recip, in_=rsv), reads=[self.bank[6], self.bank[7]], ow=[b_recip])
            P.op("dve", lambda e: e.reciprocal(out=recip, in_=recip), reads=[b_recip], writes=[b_recip])
            P.op("dve", lambda e: e.tensor_tensor(out=dst, in0=accS, in1=recip, op=ALU.mult),
                 reads=[b_recip, b_accS], ow=[bdst])

        for h in range(4):
            add_pass(2, mqT[:, h, :], [b_mqT[h]],
                     lambda g, h=h: mkT[:, h, g * 128:(g + 1) * 128],
                     lambda g, h=h: mv[:, g, h * 128:(h + 1) * 128],
                     [b_mkT, b_mv], False, None, None, None,
                     lambda sp, h=h: normalize(ymemT[:, h, :], b_ymem[h]))

        XKv = self.XKd.rearrange("(r h d) t -> h d r t", r=2, h=8)
        XVv = self.XVd.rearrange("(r s) (h x) -> h s r x", r=2, h=8)
        kvi = 0
        for h in range(8):
            i = kvi % 2; kvi += 1
            kt = KTb[i]; vv = Vb[i]

            def pre_fox(h=h, i=i):
                P.dma("sp", KTb[i], XKv[h], ds_kv[i], reads=[b_XKd], writes=[b_KT[i]])
                P.dma("sp", Vb[i].rearrange("p r t d -> p r (t d)"), XVv[h], ds_kv[i], reads=[b_XVd], writes=[b_V[i]])
                P.join(ds_kv[i], [b_KT[i], b_V[i]])
            add_pass(16, fqT[:, h, :], [b_fqT[h]],
                     lambda g, kt=kt: kt[:, g % 2, (g // 2) * 128:(g // 2 + 1) * 128],
                     lambda g, vv=vv: vv[:, g % 2, g // 2, :],
                     [b_KT[i], b_V[i]], True,
                     lambda g, j, h=h: self.biasT[:, (g % 2) * 8 + g // 2, h, j:j + 1],
                     lambda g: self.masks[:, g % 2, :], pre_fox,
                     lambda sp, h=h: normalize(yfoxT[:, h, :], b_yfox[h]))

        XDKv = self.XDKd.rearrange("(r h d) t -> h d r t", r=2, h=4)
        XDVv = self.XDVd.rearrange("(r s) (h x) -> h s r x", r=2, h=4)

        def diff_tail(sp, h):
            P.op("dve", lambda e: e.scalar_tensor_tensor(out=Y1, in0=Y2, scalar=neglam, in1=Y1, op0=ALU.mult, op1=ALU.add),
                 reads=[b_lam, b_Y2, b_Y1], writes=[b_Y1])
            P.op("act", lambda e: e.activation(out=sq, in_=Y1, func=AF.Square), reads=[b_Y1], ow=[b_sq])
            for tg in range(2):
                P.op("pe", lambda e, tg=tg: e.matmul(self.ps[:, 2 * sp + tg, :], lhsT=self.onesb,
                                                    rhs=sq[:, tg * 512:(tg + 1) * 512], start=True, stop=True),
                     reads=[b_sq, self.bconst], writes=[self.bank[2 * sp + tg]], flag=(tg == 1))
            ssv = self.ps[:, 2 * sp:2 * sp + 2, :].rearrange("p a n -> p (a n)")
            P.op("act", lambda e: e.activation(out=recip, in_=ssv, func=AF.Sqrt, bias=self.epsb, scale=1.0 / 128.0),
                 reads=[self.bconst, self.bank[2 * sp], self.bank[2 * sp + 1]], ow=[b_recip])
            P.op("dve", lambda e: e.reciprocal(out=recip, in_=recip), reads=[b_recip], writes=[b_recip])
            P.op("dve", lambda e: e.scalar_tensor_tensor(out=ydiffT[:, h, :], in0=Y1, scalar=gsc, in1=recip,
                                                         op0=ALU.mult, op1=ALU.mult),
                 reads=[b_Y1, b_recip, b_lam], ow=[b_ydiff[h]])

        for h in range(4):
            i = kvi % 2; kvi += 1
            kt = KTb[i]; vv = Vb[i]

            def pre_diff(h=h, i=i):
                P.dma("sp", KTb[i], XDKv[h], ds_kv[i], reads=[b_XDKd], writes=[b_KT[i]])
                P.dma("sp", Vb[i].rearrange("p r t d -> p r (t d)"), XDVv[h], ds_kv[i], reads=[b_XDVd], writes=[b_V[i]])
                P.join(ds_kv[i], [b_KT[i], b_V[i]])
            for m in range(2):
                def post_diff(sp, h=h, m=m):
                    if m == 0:
                        normalize(Y1, b_Y1)
                        return None
                    normalize(Y2, b_Y2)
                    return lambda sp2, h=h: diff_tail(sp2, h)
                add_pass(16, dqT[m * 64:(m + 1) * 64, h, :], [b_dqT],
                         lambda g, kt=kt, m=m: kt[m * 64:(m + 1) * 64, g % 2, (g // 2) * 128:(g // 2 + 1) * 128],
                         lambda g, vv=vv: vv[:, g % 2, g // 2, :],
                         [b_KT[i], b_V[i]], True, None,
                         lambda g: self.masks[:, 2 + g % 2, :],
                         pre_diff if m == 0 else None, post_diff)
        run_passes()
        self.dump("ymemT", ymemT, b_ymem, [128, 4, T], BF16)
        self.dump("yfoxT", yfoxT, b_yfox, [128, 8, T], BF16)
        self.dump("ydiffT", ydiffT, b_ydiff, [128, 4, T], BF16)

        mrgT = self.sb("mrgT", [128, KD, T], BF16, B0)
        b_mrg = [Buf("mrg%d" % i) for i in range(KD)]
        self.set_rB(b_mrg + [Buf("Btail")])
        sig = [self.sb("sig%d" % i, [128, T], BF16, A0 + 32768 + i * 2048) for i in range(3)]
        acc = self.sb("macc", [128, T], F32, A0 + 32768 + 6144)
        tmp = [self.sb("mtmp%d" % i, [128, T], F32, A0 + 32768 + 10240 + i * 4096) for i in range(2)]
        b_sig = [Buf("sig%d" % i) for i in range(3)]
        b_acc = Buf("macc"); b_tmp = [Buf("mtmp0"), Buf("mtmp1")]
        inherit(b_sig + [b_acc] + b_tmp, b_KT + b_V + b_PTall + [b_recip, b_sq, b_accS])
        wmg = self.w["w_merge_gate"].rearrange("(k p) (g c) -> p k g c", p=128, g=3)
        wbr = [self.w["w_branch_fox"].rearrange("(k p) n -> p k n", p=128),
               self.w["w_branch_diff"].rearrange("(k p) n -> p k n", p=128),
               self.w["w_branch_mem"].rearrange("(k p) n -> p k n", p=128)]
        ybr = [(yfoxT, b_yfox, 8), (ydiffT, b_ydiff, 4), (ymemT, b_ymem, 4)]
        for m in range(KD):
            slot = self.next_ra()
            gt = self.sb("wmg", [128, KD, 3, 128], BF16, slot.off)
            bt = self.sb("wbr", [128, KD, 128], BF16, slot.off + 12288)

            def dma_fn(slot=slot, gt=gt, bt=bt, m=m):
                for b in range(3):
                    P.dma("pool", gt[:, :, b, :], wmg[:, :, b, m * 128:(m + 1) * 128], slot.ds, writes=[slot.buf])
                P.dma("pool", bt[:, 0:8, :], wbr[0][:, :, m * 128:(m + 1) * 128], slot.ds, writes=[slot.buf])
                P.dma("pool", bt[:, 8:12, :], wbr[1][:, :, m * 128:(m + 1) * 128], slot.ds, writes=[slot.buf])
                P.dma("pool", bt[:, 12:16, :], wbr[2][:, :, m * 128:(m + 1) * 128], slot.ds, writes=[slot.buf])

            def comp_fn(slot=slot, gt=gt, bt=bt, m=m):
                kofs = 0
                for b in range(3):
                    yT, byT, nkb = ybr[b]
                    pg = (self.chunk_ctr % 4) * 2; self.chunk_ctr += 1
                    for k in range(KD):
                        for tg in range(2):
                            P.op("pe", lambda e, k=k, tg=tg, b=b, pg=pg: e.matmul(
                                self.ps[:, pg + tg, :], lhsT=gt[:, k, b, :], rhs=self.hT[:, k, tg * 512:(tg + 1) * 512],
                                start=(k == 0), stop=(k == KD - 1)),
                                reads=[slot.buf, self.bhT], writes=[self.bank[pg + tg]], flag=(k == KD - 1 and tg == 1))
                    gv = self.ps[:, pg:pg + 2, :].rearrange("p a n -> p (a n)")
                    P.op("act", lambda e, b=b, gv=gv: e.activation(out=sig[b], in_=gv, func=AF.Sigmoid,
                                                                 bias=self.bmT[:, b * 16 + m:b * 16 + m + 1], scale=1.0),
                         reads=[self.bconst, self.bank[pg], self.bank[pg + 1]], ow=[b_sig[b]])
                    py = (self.chunk_ctr % 4) * 2; self.chunk_ctr += 1
                    for k in range(nkb):
                        for tg in range(2):
                            P.op("pe", lambda e, k=k, tg=tg, py=py, kofs=kofs, yT=yT: e.matmul(
                                self.ps[:, py + tg, :], lhsT=bt[:, kofs + k, :], rhs=yT[:, k, tg * 512:(tg + 1) * 512],
                                start=(k == 0), stop=(k == nkb - 1)),
                                reads=[slot.buf] + list(byT), writes=[self.bank[py + tg]],
                                flag=(k == nkb - 1 and tg == 1))
                    yv = self.ps[:, py:py + 2, :].rearrange("p a n -> p (a n)")
                    ybanks = [self.bank[py], self.bank[py + 1]]
                    if b == 0:
                        P.op("dve", lambda e, yv=yv: e.tensor_tensor(out=acc, in0=yv, in1=sig[0], op=ALU.mult),
                             reads=[b_sig[0]] + ybanks, ow=[b_acc])
                    elif b == 1:
                        P.op("dve", lambda e, yv=yv: e.tensor_tensor(out=tmp[0], in0=yv, in1=sig[1], op=ALU.mult),
                             reads=[b_sig[1]] + ybanks, ow=[b_tmp[0]])
                        P.op("dve", lambda e: e.tensor_tensor(out=acc, in0=acc, in1=tmp[0], op=ALU.add),
                             reads=[b_tmp[0], b_acc], writes=[b_acc])
                    else:
                        P.op("dve", lambda e, yv=yv: e.tensor_tensor(out=tmp[1], in0=yv, in1=sig[2], op=ALU.mult),
                             reads=[b_sig[2]] + ybanks, ow=[b_tmp[1]])
                        P.op("dve", lambda e, m=m: e.tensor_tensor(out=mrgT[:, m, :], in0=acc, in1=tmp[1], op=ALU.add),
                             reads=[b_tmp[1], b_acc], ow=[b_mrg[m]])
                    kofs += nkb

            self.ws.add(slot, dma_fn, comp_fn)
        wov = self.w["w_out"].rearrange("(k p) n -> p k n", p=128)
        self.ws.add(None, None, lambda: inherit(self.bA, attnA + b_sig + [b_acc] + b_tmp))
        for ng in range(4):
            def evac_o(tb, ng=ng):
                ydst = self.A[:, tb, ng * 512:(ng + 1) * 512]
                if tb % 2 == 0:
                    P.op("act", lambda e: e.copy(out=ydst, in_=self.ps[:, tb, :]), reads=[self.bank[tb]], ow=[self.bA[tb]])
                else:
                    P.op("dve", lambda e: e.tensor_copy(out=ydst, in_=self.ps[:, tb, :]),
                         reads=[self.bank[tb]], ow=[self.bA[tb]])
            self.tm_steps(wov, ng * 512, lambda kl, tb: (mrgT[:, kl, tb * 128:(tb + 1) * 128], b_mrg[kl]), KD, evac_o)
        self.ws.run()
        self.dump("mrgT", mrgT, b_mrg, [128, KD, T], BF16)
        inherit(self.bSIL, [b_S])
        self.set_rB(self.bact)


_NC_CACHE = {}


def _get_builder(stage, dbg=()):
    key = (stage, tuple(dbg))
    if key not in _NC_CACHE:
        b = Builder(stage, dbg)
        b.build()
        _NC_CACHE[key] = b
    return _NC_CACHE[key]


def _shard_x(x, c):
    b, p = c // 2, c % 2
    return np.ascontiguousarray(x[b].reshape(8, 2, 128, D)[:, p].reshape(T, D))


def _consts(p):
    s = np.arange(128)[:, None]
    t = np.arange(128)[None, :]
    tri = (s <= t).astype(np.float32)
    ones = np.ones((128, 128), np.float32)
    zeros = np.zeros((128, 128), np.float32)
    sel = np.zeros((128, 128), np.float32); sel[127, :] = 1.0
    cmat = np.stack([tri, ones, sel], axis=1)
    chunk = ((s // 64) <= (t // 64)).astype(np.float32)
    if p == 0:
        masks = [tri, zeros, chunk, zeros]
    else:
        masks = [ones, tri, ones, chunk]
    masks = np.stack(masks, axis=1).astype(ml_dtypes.bfloat16)
    pos = ((2 * np.arange(8)[None, :] + p) * 128 + np.arange(128)[:, None]).astype(np.float32)
    inv = (500000.0 ** (-np.arange(0, 16, 2, dtype=np.float32) / 16)).astype(np.float32)
    ang = pos[:, :, None] * inv[None, None, :]
    rope = np.stack([np.cos(ang), np.sin(ang)], axis=1).astype(np.float32)
    psel = np.zeros((128, 2), np.float32); psel[:, p] = 1.0
    return {"c_ident": np.eye(128, dtype=ml_dtypes.bfloat16), "c_mat": cmat, "c_masks": masks,
            "c_rope": rope, "c_psel": psel}


_W2D = ("ffn1_pre_g", "ffn1_w_gate", "ffn1_w_up", "ffn1_w_down", "ffn1_post_g", "mix_pre_g", "w_in", "fox_f_bias",
        "diff_lambda_q1", "diff_lambda_k1", "diff_lambda_q2", "diff_lambda_k2", "diff_head_g", "mem_norm_g",
        "w_mem_kv", "w_branch_fox", "w_branch_diff", "w_branch_mem", "w_merge_gate", "b_merge_gate", "w_out",
        "mix_post_g", "ffn2_pre_g", "ffn2_w_gate", "ffn2_w_up", "ffn2_w_down", "ffn2_post_g")


def kernel(stage=3, dbg=(), **inputs):
    bld = _get_builder(stage, dbg)
    nc = bld.nc
    x = np.asarray(inputs["x"], dtype=np.float32)
    mem = np.asarray(inputs["mem"], dtype=np.float32)
    shared = {}
    for nm in _W2D:
        a = np.asarray(inputs[nm], dtype=np.float32)[0]
        shared[nm] = np.ascontiguousarray(a.reshape(1, -1) if a.ndim == 1 else a)
    cst = [_consts(0), _consts(1)]
    in_maps = []
    for c in range(8):
        m = dict(shared)
        m.update(cst[c % 2])
        m["x"] = _shard_x(x, c)
        m["mem"] = np.ascontiguousarray(mem[c // 2])
        in_maps.append(m)
    res = run_bass_kernel_spmd(nc, in_maps, core_ids=list(range(8)))
    out = np.empty((4, 2048, D), np.float32)
    for c in range(8):
        b, p = c // 2, c % 2
        out[b].reshape(8, 2, 128, D)[:, p] = res.results[c]["out"].reshape(8, 128, D)
    if dbg:
        kernel.last_dbg = [{k: res.results[c]["dbg_" + k] for k in bld.dbg_out} for c in range(8)]
    return out
```

```python
import numpy as np
import ml_dtypes
from contextlib import ExitStack
import concourse.bass as bass
import concourse.mybir as mybir
from concourse.bass_utils import run_bass_kernel_spmd

F32 = mybir.dt.float32
BF16 = mybir.dt.bfloat16
AF = mybir.ActivationFunctionType
ALU = mybir.AluOpType
AX = mybir.AxisListType

D = 2048
DFF = 5632
T = 1024
NTB = 8
KD = 16
NF = 44
EPS = 1e-6
SAME_ENG_SYNC = True

SB_BASE = 16512


class Buf:
    __slots__ = ("name", "w", "r")

    def __init__(self, name):
        self.name = name
        self.w = {}
        self.r = {}


class DSem:
    def __init__(self, handle, key):
        self.h = handle
        self.key = key
        self.count = 0


def inherit(new_bufs, old_bufs):
    acc = {}
    for b in old_bufs:
        _merge(acc, b.w)
        _merge(acc, b.r)
    for nb in new_bufs:
        _merge(nb.w, acc)


def _merge(dst, src):
    for k, v in src.items():
        if dst.get(k, 0) < v:
            dst[k] = v


class Prog:
    ENGS = ("pe", "act", "dve", "pool", "sp")

    def __init__(self, nc, stack):
        self.nc = nc
        self.stack = stack
        self.q = {e: [] for e in self.ENGS}
        self.cnt = {e: 0 for e in self.ENGS}
        self.handles = {}
        for e in ("pe", "act", "dve", "pool"):
            self.handles[e] = stack.enter_context(nc.semaphore("s_" + e))
        self.seen = {e: {} for e in self.ENGS}
        self.ndsem = 0
        self.pe_pending = False
        self.nwaits = 0

    def dsem(self, name):
        h = self.stack.enter_context(self.nc.semaphore("d_" + name))
        key = "d_%s_%d" % (name, self.ndsem)
        self.ndsem += 1
        self.handles[key] = h
        return DSem(h, key)

    def _deps(self, eng, reads, writes, ow=()):
        deps = {}
        for b in reads:
            _merge(deps, b.w)
        for b in writes:
            _merge(deps, b.w)
            _merge(deps, b.r)
        if ow:
            d2 = {}
            for b in ow:
                _merge(d2, b.w)
                _merge(d2, b.r)
            d2.pop(eng, None)
            _merge(deps, d2)
        if eng == "pe" or not SAME_ENG_SYNC:
            deps.pop(eng, None)
        waits = []
        seen = self.seen[eng]
        for k, v in deps.items():
            if seen.get(k, 0) < v:
                seen[k] = v
                waits.append((self.handles[k], v))
                if k == "pe" and eng != "pe":
                    assert v <= self.cnt["pe"], "wait on unflagged PE op"
        return waits

    def op(self, eng, fn, reads=(), writes=(), ow=(), flag=True):
        waits = self._deps(eng, reads, writes, ow)
        if ow:
            writes = list(writes) + list(ow)
        if eng == "pe" and not flag:
            tokv = self.cnt[eng] + 1
            self.pe_pending = True
        else:
            self.cnt[eng] += 1
            tokv = self.cnt[eng]
            if eng == "pe":
                self.pe_pending = False
        sem = self.handles[eng]
        self.nwaits += len(waits)

        def run(e, waits=waits, fn=fn, flag=flag, sem=sem):
            for h, v in waits:
                e.wait_ge(h, v)
            ins = fn(e)
            if flag:
                ins.then_inc(sem, 1)

        self.q[eng].append(run)
        for b in reads:
            if b.r.get(eng, 0) < tokv:
                b.r[eng] = tokv
        for b in writes:
            if b.w.get(eng, 0) < tokv:
                b.w[eng] = tokv
            b.r = {}

    def dma(self, eng, out, in_, ds, reads=(), writes=(), **kw):
        waits = self._deps(eng, reads, writes)
        ds.count += 16
        tokv = ds.count
        self.nwaits += len(waits)

        def run(e, waits=waits, out=out, in_=in_, h=ds.h, kw=kw):
            for hh, v in waits:
                e.wait_ge(hh, v)
            e.dma_start(out=out, in_=in_, **kw).then_inc(h, 16)

        self.q[eng].append(run)
        for b in reads:
            if b.r.get(ds.key, 0) < tokv:
                b.r[ds.key] = tokv
        for b in writes:
            if b.w.get(ds.key, 0) < tokv:
                b.w[ds.key] = tokv
            b.r = {}

    def aop(self, eng, fn, ds, inc, reads=(), writes=()):
        waits = self._deps(eng, reads, writes)
        ds.count += inc
        tokv = ds.count

        def run(e, waits=waits, fn=fn, h=ds.h, inc=inc):
            for hh, v in waits:
                e.wait_ge(hh, v)
            fn(e).then_inc(h, inc)

        self.q[eng].append(run)
        for b in reads:
            if b.r.get(ds.key, 0) < tokv:
                b.r[ds.key] = tokv
        for b in writes:
            if b.w.get(ds.key, 0) < tokv:
                b.w[ds.key] = tokv
            b.r = {}

    def join(self, ds, bufs):
        for b in bufs:
            for dd in (b.r, b.w):
                if ds.key in dd:
                    dd[ds.key] = ds.count

    def raw(self, eng, fn, reads=(), writes=()):
        waits = self._deps(eng, reads, writes)

        def run(e, waits=waits, fn=fn):
            for h, v in waits:
                e.wait_ge(h, v)
            if fn is not None:
                fn(e)

        self.q[eng].append(run)

    def emit_all(self):
        nc = self.nc
        q = self.q
        with nc.Block() as block:
            @block.tensor
            def _(e):
                for f in q["pe"]:
                    f(e)

            @block.scalar
            def _(e):
                for f in q["act"]:
                    f(e)

            @block.vector
            def _(e):
                for f in q["dve"]:
                    f(e)

            @block.gpsimd
            def _(e):
                for f in q["pool"]:
                    f(e)

            @block.sync
            def _(e):
                for f in q["sp"]:
                    f(e)


class Slot:
    def __init__(self, P, name, off, nbytes):
        self.buf = Buf(name)
        self.ds = P.dsem(name)
        self.off = off
        self.nbytes = nbytes
        self.name = name


class WStream:
    def __init__(self, P):
        self.P = P
        self.steps = []

    def add(self, slot, dma_fn, compute_fn):
        self.steps.append((slot, dma_fn, compute_fn))

    def run(self):
        steps = self.steps
        n = len(steps)
        prev_user = [-1] * n
        last = {}
        for i, (slot, _, _) in enumerate(steps):
            if slot is not None:
                prev_user[i] = last.get(id(slot), -1)
                last[id(slot)] = i
        nd = 0
        for i in range(n):
            while nd < n and prev_user[nd] < i:
                slot, dma_fn, _ = steps[nd]
                if dma_fn is not None:
                    dma_fn()
                nd += 1
            assert nd > i
            steps[i][2]()
        self.steps = []


C_FQ, C_FK, C_FV, C_FF, C_DQ, C_DK, C_DV, C_MQ = 0, 1024, 2048, 3072, 3080, 3592, 4104, 4616
D_IN = 5128
LAM_INIT = 0.8 - 0.6
RG = [[0, 1], [2, 3], [4, 5], [6, 7]]


class Builder:
    def __init__(self, stage, dbg=()):
        self.stage = stage
        self.dbg = dbg
        self.nc = bass.Bass("TRN2", target_bir_lowering=False)
        self.stack = ExitStack()
        self.P = Prog(self.nc, self.stack)
        self.nsb = 0
        self.dbg_out = {}

    def sb(self, name, shape, dtype, off):
        self.nsb += 1
        h = self.nc.alloc_sbuf_tensor_at("%s_%d" % (name, self.nsb), list(shape), dtype, offset=SB_BASE + off)
        return h.ap()

    def din(self, name, shape, dtype=F32):
        return self.nc.dram_tensor(name, list(shape), dtype, kind="ExternalInput").ap()

    def dscr(self, name, shape, dtype):
        return self.nc.dram_tensor(name, list(shape), dtype).ap()

    def dump(self, name, ap, bufs, shape, dtype):
        if name not in self.dbg:
            return
        o = self.nc.dram_tensor("dbg_" + name, list(shape), dtype, kind="ExternalOutput").ap()
        self.dbg_out[name] = (list(shape), dtype)
        self.P.dma("sp", o, ap, self.dsout, reads=bufs)

    def build(self):
        nc = self.nc
        P = self.P
        self.x_in = self.din("x", [T, D])
        self.mem_in = self.din("mem", [256, D])
        self.out = nc.dram_tensor("out", [T, D], F32, kind="ExternalOutput").ap()
        self.w = {}
        for nm, shp in (("ffn1_pre_g", [1, D]), ("ffn1_w_gate", [D, DFF]), ("ffn1_w_up", [D, DFF]),
                        ("ffn1_w_down", [DFF, D]), ("ffn1_post_g", [1, D]),
                        ("mix_pre_g", [1, D]), ("w_in", [D, D_IN]), ("fox_f_bias", [1, 8]),
                        ("diff_lambda_q1", [1, 64]), ("diff_lambda_k1", [1, 64]),
                        ("diff_lambda_q2", [1, 64]), ("diff_lambda_k2", [1, 64]),
                        ("diff_head_g", [1, 128]), ("mem_norm_g", [1, D]), ("w_mem_kv", [D, 1024]),
                        ("w_branch_fox", [1024, D]), ("w_branch_diff", [512, D]), ("w_branch_mem", [512, D]),
                        ("w_merge_gate", [D, 3 * D]), ("b_merge_gate", [1, 3 * D]), ("w_out", [D, D]),
                        ("mix_post_g", [1, D]),
                        ("ffn2_pre_g", [1, D]), ("ffn2_w_gate", [D, DFF]), ("ffn2_w_up", [D, DFF]),
                        ("ffn2_w_down", [DFF, D]), ("ffn2_post_g", [1, D])):
            self.w[nm] = self.din(nm, shp)
        self.ident_in = self.din("c_ident", [128, 128], BF16)
        self.cmat_in = self.din("c_mat", [128, 3, 128], F32)
        self.masks_in = self.din("c_masks", [128, 4, 128], BF16)
        self.rope_in = self.din("c_rope", [128, 2, NTB, 8], F32)
        self.psel_in = self.din("c_psel", [128, 2], F32)
        self.xsp1 = self.dscr("xsp1", [T, D], F32)
        self.xsp2 = self.dscr("xsp2", [T, D], F32)
        self.XKs = self.dscr("XKs", [1024, 1024], BF16); self.XKd = self.dscr("XKd", [2048, 1024], BF16)
        self.XVs = self.dscr("XVs", [128, 8192], BF16); self.XVd = self.dscr("XVd", [256, 8192], BF16)
        self.XDKs = self.dscr("XDKs", [512, 1024], BF16); self.XDKd = self.dscr("XDKd", [1024, 1024], BF16)
        self.XDVs = self.dscr("XDVs", [128, 4096], BF16); self.XDVd = self.dscr("XDVd", [256, 4096], BF16)
        self.XLs = self.dscr("XLs", [128, 64], F32); self.XLd = self.dscr("XLd", [256, 64], F32)

        o = 0
        self.offA = o
        self.A = self.sb("A", [128, NTB, D], F32, o); o += NTB * D * 4
        self.hT = self.sb("hT", [128, KD, T], BF16, o); o += KD * T * 2
        self.offB = o
        self.actT = self.sb("actT", [128, 22, T], BF16, o); o += 22 * T * 2
        self.offWA = o
        o += 2 * 16384
        self.offWB = o
        o += 4 * 2048
        self.GB = self.sb("GB", [128, D], F32, o); o += D * 4
        self.offSIL = o
        self.SIL = [self.sb("SIL%d" % i, [128, T], F32, o + i * 4096) for i in range(2)]; o += 8192
        self.ident = self.sb("ident", [128, 128], BF16, o); o += 256
        self.stat = self.sb("stat", [128, 64], F32, o); o += 256
        self.cst = self.sb("cst", [128, 8], F32, o); o += 32
        self.offX = o
        self.biasT = self.sb("biasT", [128, 16, 8, 8], F32, o); o += 4096
        self.masks = self.sb("masks", [128, 4, 128], BF16, o); o += 1024
        self.rope = self.sb("rope", [128, 2, NTB, 8], F32, o); o += 512
        self.onesb = self.sb("onesb", [128, 128], BF16, o); o += 256
        self.bmT = self.sb("bmT", [128, 48], F32, o); o += 192
        self.psel = self.sb("psel", [128, 2], F32, o); o += 32
        self.lamv = self.sb("lamv", [128, 8], F32, o); o += 32
        self.fbias = self.sb("fbias", [128, 8], F32, o); o += 32
        self.Dtab = self.sb("Dtab", [128, 128], F32, o); o += 512
        self.Dend = self.sb("Dend", [128, 64], F32, o); o += 256
        self.wff = self.sb("wff", [128, KD, 8], BF16, o); o += 256
        assert SB_BASE + o <= 229344, o
        self.hb = [self.sb("hb%d" % i, [128, D], BF16, self.offB + i * 4096) for i in range(2)]
        self.xr = [self.sb("xr%d" % i, [128, D], F32, self.offB + 8192 + i * 8192) for i in range(2)]
        self.junk = self.sb("junk", [128, D], BF16, self.offB + 8192 + 16384)

        self.ps = nc.alloc_psum_tensor("ps", [128, 8, 512], F32).ap()
        self.bank = [Buf("bank%d" % i) for i in range(8)]

        self.bA = [Buf("A%d" % i) for i in range(NTB)]
        self.bhT = Buf("hT")
        self.bact = [Buf("act%d" % i) for i in range(22)]
        self.rB = self.bact
        self.bGB = Buf("GB")
        self.bSIL = [Buf("SIL0"), Buf("SIL1")]
        self.bhb = [Buf("hb0"), Buf("hb1")]
        self.bxr = [Buf("xr0"), Buf("xr1")]
        self.bjunk = Buf("junk")
        self.bstat = [Buf("stat%d" % i) for i in range(64)]
        self.bconst = Buf("const")
        self.dsmisc = P.dsem("misc")
        self.dsx = P.dsem("x")
        self.dsxr = [P.dsem("xr0"), P.dsem("xr1")]
        self.dsgb = P.dsem("gb")
        self.dsgb2 = P.dsem("gb2")
        self.dsout = P.dsem("out")
        self.ringA = [Slot(P, "ra%d" % i, self.offWA + i * 16384, 16384) for i in range(2)]
        self.ringB = [Slot(P, "rb%d" % i, self.offWB + i * 2048, 2048) for i in range(4)]
        self.ra_i = 0
        self.rb_i = 0
        self.ws = WStream(P)
        self.chunk_ctr = 0

        P.dma("sp", self.ident, self.ident_in, self.dsmisc, writes=[self.bconst])
        P.op("dve", lambda e: e.memset(self.cst[:, 0:1], EPS), writes=[self.bconst])
        P.op("dve", lambda e: e.memset(self.cst[:, 1:2], 1.0), writes=[self.bconst])
        P.op("dve", lambda e: e.memset(self.cst[:, 2:3], 0.0), writes=[self.bconst])
        P.op("dve", lambda e: e.memset(self.cst[:, 3:4], 4.0 * EPS), writes=[self.bconst])
        P.op("dve", lambda e: e.memset(self.onesb, 1.0), writes=[self.bconst])
        self.epsb = self.cst[:, 0:1]
        self.oneb = self.cst[:, 1:2]
        if self.stage >= 2:
            P.dma("sp", self.masks, self.masks_in, self.dsmisc, writes=[self.bconst])
            P.dma("sp", self.rope, self.rope_in, self.dsmisc, writes=[self.bconst])
            P.dma("sp", self.psel, self.psel_in, self.dsmisc, writes=[self.bconst])
            P.dma("sp", self.fbias, self.w["fox_f_bias"][0:1, :].partition_broadcast(128)[:, 0, :], self.dsmisc,
                  writes=[self.bconst])
            P.dma("sp", self.bmT, self.w["b_merge_gate"].rearrange("o (c p) -> p (o c)", p=128), self.dsmisc,
                  writes=[self.bconst], allow_slow_non_contiguous=True)
            P.dma("sp", self.lamv[:, 6:7], self.w["diff_head_g"].rearrange("o p -> p o"), self.dsmisc,
                  writes=[self.bconst], allow_slow_non_contiguous=True)
            self.dswff = P.dsem("wff")
            self.bwff = Buf("wff")
            P.dma("pool", self.wff, self.w["w_in"].rearrange("(k p) n -> p k n", p=128)[:, :, C_FF:C_FF + 8],
                  self.dswff, writes=[self.bwff], allow_slow_non_contiguous=True)
        P.join(self.dsmisc, [self.bconst])

        dsxs = [P.dsem("x%d" % tb) for tb in range(NTB)]
        for tb in range(NTB):
            P.dma("sp", self.A[:, tb, :], self.x_in[tb * 128:(tb + 1) * 128, :], dsxs[tb], writes=[self.bA[tb]])
        self.prenorm_A("ffn1_pre_g")
        self.ffn_core("ffn1_")
        if self.stage >= 2:
            self.trans("ffn1_post_g", 0.5, self.x_in, None, "mix_pre_g")
            bsp1 = self.spill(self.xsp1, "sp1")
            self.mixer()
            if self.stage >= 3:
                self.trans("mix_post_g", 1.0, self.xsp1, bsp1, "ffn2_pre_g")
            else:
                self.postnorm("mix_post_g", 1.0, self.xsp1, bsp1, final=False)
        else:
            self.postnorm("ffn1_post_g", 0.5, self.x_in, None, final=False)
        if self.stage >= 3:
            bsp = self.spill(self.xsp2, "sp2")
            self.ffn_core("ffn2_")
            self.postnorm("ffn2_post_g", 0.5, self.xsp2, bsp, final=True)
        else:
            for tb in range(NTB):
                P.dma("sp", self.out[tb * 128:(tb + 1) * 128, :], self.A[:, tb, :], self.dsout, reads=[self.bA[tb]])
        dso = self.dsout
        P.q["sp"].append(lambda e: e.wait_ge(dso.h, dso.count))
        P.emit_all()
        return nc

    def next_ra(self):
        s = self.ringA[self.ra_i % 2]
        self.ra_i += 1
        return s

    def next_rb(self):
        s = self.ringB[self.rb_i % 4]
        self.rb_i += 1
        return s

    def scratch_bufs(self):
        return self.bhb + self.bxr + [self.bjunk]

    def begin_scratch(self):
        inherit(self.scratch_bufs(), list(self.rB))

    def end_scratch(self):
        inherit(list(self.rB), self.scratch_bufs())

    def set_rB(self, new_bufs):
        inherit(new_bufs, list(self.rB) + self.bhb + self.bxr + [self.bjunk])
        self.rB = new_bufs

    def spill(self, dst, name):
        P = self.P
        ds = P.dsem(name)
        bsp = Buf(name)
        for tb in range(NTB):
            P.dma("sp", dst[tb * 128:(tb + 1) * 128, :], self.A[:, tb, :], ds, reads=[self.bA[tb]], writes=[bsp])
        P.join(ds, self.bA + [bsp])
        return bsp

    def load_gain(self, gname):
        src = self.w[gname][0:1, :].partition_broadcast(128)
        self.P.dma("sp", self.GB, src[:, 0, :], self.dsgb, writes=[self.bGB])

    def rstd_col(self, col, dim, coef=1.0):
        P = self.P
        st = self.stat[:, col:col + 1]
        bias = self.epsb if coef == 1.0 else self.cst[:, 3:4]
        assert coef in (1.0, 0.5)
        P.op("act", lambda e: e.activation(out=st, in_=st, func=AF.Sqrt, bias=bias, scale=1.0 / (dim * coef * coef)),
             reads=[self.bconst, self.bstat[col]], writes=[self.bstat[col]])
        P.op("dve", lambda e: e.reciprocal(out=st, in_=st), reads=[self.bstat[col]], writes=[self.bstat[col]])

    def prenorm(self, nblk, src_fn, gname, dst_fn, dstbuf):
        P = self.P
        self.begin_scratch()
        self.load_gain(gname)
        P.op("dve", lambda e: e.memset(self.stat[:, 0:nblk], 0.0), writes=self.bstat[0:nblk])

        def stage_a(tb):
            sap, sbuf = src_fn(tb)
            P.op("act", lambda e: e.activation(out=self.junk, in_=sap, func=AF.Square,
                                               accum_out=self.stat[:, tb:tb + 1]),
                 reads=[sbuf], writes=[self.bstat[tb], self.bjunk])
            self.rstd_col(tb, D)
            hb = self.hb[tb % 2]
            bhb = self.bhb[tb % 2]
            P.op("dve", lambda e: e.scalar_tensor_tensor(
                out=hb, in0=sap, scalar=self.stat[:, tb:tb + 1], in1=self.GB,
                op0=ALU.mult, op1=ALU.mult),
                reads=[sbuf, self.bstat[tb], self.bGB], ow=[bhb])

        def stage_b(tb):
            hb = self.hb[tb % 2]
            bhb = self.bhb[tb % 2]
            for half in range(2):
                bk = (2 * tb + half) % 8
                pst = self.ps[:, bk, :].bitcast(BF16)
                for kk in range(8):
                    k = half * 8 + kk
                    P.op("pe", lambda e, pst=pst, kk=kk, k=k: e.transpose(
                        out=pst[:, kk * 128:(kk + 1) * 128], in_=hb[:, k * 128:(k + 1) * 128], identity=self.ident),
                        reads=[bhb, self.bconst], writes=[self.bank[bk]], flag=(kk == 7))
                dst = dst_fn(tb, half)
                src = pst.rearrange("p (k n) -> p k n", k=8)
                if half == 0:
                    P.op("act", lambda e, dst=dst, src=src: e.copy(out=dst, in_=src),
                         reads=[self.bank[bk]], ow=[dstbuf])
                else:
                    P.op("dve", lambda e, dst=dst, src=src: e.tensor_copy(out=dst, in_=src),
                         reads=[self.bank[bk]], ow=[dstbuf])

        stage_a(0)
        for tb in range(nblk):
            if tb + 1 < nblk:
                stage_a(tb + 1)
            stage_b(tb)
        self.end_scratch()

    def prenorm_A(self, gname):
        self.prenorm(NTB, lambda tb: (self.A[:, tb, :], self.bA[tb]), gname,
                     lambda tb, half: self.hT[:, half * 8:(half + 1) * 8, tb * 128:(tb + 1) * 128], self.bhT)

    def postnorm(self, gname, coef, src, bsrc, final):
        P = self.P
        self.begin_scratch()
        self.load_gain(gname)
        P.op("dve", lambda e: e.memset(self.stat[:, 8:16], 0.0), writes=self.bstat[8:16])
        for tb in range(NTB):
            c = 8 + tb
            xr = self.xr[tb % 2]
            bxr = self.bxr[tb % 2]
            P.dma("sp", xr, src[tb * 128:(tb + 1) * 128, :], self.dsxr[tb % 2],
                  reads=([bsrc] if bsrc is not None else []), writes=[bxr])
            P.op("act", lambda e, tb=tb, c=c: e.activation(out=self.junk, in_=self.A[:, tb, :], func=AF.Square,
                                                        accum_out=self.stat[:, c:c + 1]),
                 reads=[self.bA[tb]], writes=[self.bstat[c], self.bjunk])
            self.rstd_col(c, D, coef)
            P.op("dve", lambda e, tb=tb: e.tensor_tensor(out=self.A[:, tb, :], in0=self.A[:, tb, :], in1=self.GB,
                                                          op=ALU.mult),
                 reads=[self.bGB, self.bA[tb]], writes=[self.bA[tb]])
            P.op("dve", lambda e, tb=tb, xr=xr, c=c: e.scalar_tensor_tensor(
                out=self.A[:, tb, :], in0=self.A[:, tb, :], scalar=self.stat[:, c:c + 1], in1=xr,
                op0=ALU.mult, op1=ALU.add),
                reads=[bxr, self.bstat[c], self.bA[tb]], writes=[self.bA[tb]])
            if final:
                P.dma("sp", self.out[tb * 128:(tb + 1) * 128, :], self.A[:, tb, :], self.dsout, reads=[self.bA[tb]])
        self.end_scratch()

    def trans(self, g_post, coef, src, bsrc, g_pre):
        P = self.P
        GB2 = self.sb("GB2", [128, D], F32, self.offSIL)
        bGB2 = Buf("GB2")
        inherit([bGB2], self.bSIL)
        self.begin_scratch()
        self.load_gain(g_post)
        P.dma("sp", GB2, self.w[g_pre][0:1, :].partition_broadcast(128)[:, 0, :], self.dsgb2, writes=[bGB2])
        P.op("dve", lambda e: e.memset(self.stat[:, 0:16], 0.0), writes=self.bstat[0:16])

        def stage_p(tb):
            c = 8 + tb
            xr = self.xr[tb % 2]
            bxr = self.bxr[tb % 2]
            P.dma("sp", xr, src[tb * 128:(tb + 1) * 128, :], self.dsxr[tb % 2],
                  reads=([bsrc] if bsrc is not None else []), writes=[bxr])
            P.op("act", lambda e: e.activation(out=self.junk, in_=self.A[:, tb, :], func=AF.Square,
                                               accum_out=self.stat[:, c:c + 1]),
                 reads=[self.bA[tb]], writes=[self.bstat[c], self.bjunk])
            self.rstd_col(c, D, coef)
            P.op("dve", lambda e: e.tensor_tensor(out=self.A[:, tb, :], in0=self.A[:, tb, :], in1=self.GB, op=ALU.mult),
                 reads=[self.bGB, self.bA[tb]], writes=[self.bA[tb]])
            P.op("dve", lambda e: e.scalar_tensor_tensor(
                out=self.A[:, tb, :], in0=self.A[:, tb, :], scalar=self.stat[:, c:c + 1], in1=xr,
                op0=ALU.mult, op1=ALU.add),
                reads=[bxr, self.bstat[c], self.bA[tb]], writes=[self.bA[tb]])

        def stage_a(tb):
            P.op("act", lambda e: e.activation(out=self.junk, in_=self.A[:, tb, :], func=AF.Square,
                                               accum_out=self.stat[:, tb:tb + 1]),
                 reads=[self.bA[tb]], writes=[self.bstat[tb], self.bjunk])
            self.rstd_col(tb, D)
            hb = self.hb[tb % 2]
            bhb = self.bhb[tb % 2]
            P.op("dve", lambda e: e.scalar_tensor_tensor(
                out=hb, in0=self.A[:, tb, :], scalar=self.stat[:, tb:tb + 1], in1=GB2,
                op0=ALU.mult, op1=ALU.mult),
                reads=[self.bA[tb], self.bstat[tb], bGB2], ow=[bhb])

        def stage_b(tb):
            hb = self.hb[tb % 2]
            bhb = self.bhb[tb % 2]
            for half in range(2):
                bk = (2 * tb + half) % 8
                pst = self.ps[:, bk, :].bitcast(BF16)
                for kk in range(8):
                    k = half * 8 + kk
                    P.op("pe", lambda e, pst=pst, kk=kk, k=k: e.transpose(
                        out=pst[:, kk * 128:(kk + 1) * 128], in_=hb[:, k * 128:(k + 1) * 128], identity=self.ident),
                        reads=[bhb, self.bconst], writes=[self.bank[bk]], flag=(kk == 7))
                dst = self.hT[:, half * 8:(half + 1) * 8, tb * 128:(tb + 1) * 128]
                src_ = pst.rearrange("p (k n) -> p k n", k=8)
                P.op("act", lambda e, dst=dst, src_=src_: e.copy(out=dst, in_=src_),
                     reads=[self.bank[bk]], ow=[self.bhT])

        for t in range(NTB + 2):
            if t < NTB:
                stage_p(t)
            if 1 <= t <= NTB:
                stage_a(t - 1)
            if t >= 2:
                stage_b(t - 2)
        inherit(self.bSIL, [bGB2])
        self.end_scratch()

    def tm_steps(self, wview, c0, lhs_fn, nk, evac_fn, kbase=0):
        P = self.P
        assert nk % 2 == 0
        for kb in range(nk // 2):
            slot = self.next_rb()
            tile = self.sb("wtm", [128, 2, 512], BF16, slot.off)
            kg = kbase + kb * 2

            def dma_fn(slot=slot, tile=tile, kg=kg):
                P.dma("pool", tile, wview[:, kg:kg + 2, c0:c0 + 512], slot.ds, writes=[slot.buf])

            def comp_fn(slot=slot, tile=tile, kb=kb):
                for kk in range(2):
                    kl = kb * 2 + kk
                    for tb in range(NTB):
                        lap, lbuf = lhs_fn(kl, tb)
                        last = (kl == nk - 1)
                        P.op("pe", lambda e, kk=kk, kl=kl, tb=tb, lap=lap, last=last: e.matmul(
                            self.ps[:, tb, :], lhsT=lap, rhs=tile[:, kk, :], start=(kl == 0), stop=last),
                            reads=[slot.buf, lbuf], writes=[self.bank[tb]], flag=last or (kk == 1 and tb == NTB - 1))
                if kb == nk // 2 - 1:
                    for tb in range(NTB):
                        evac_fn(tb)

            self.ws.add(slot, dma_fn, comp_fn)

    def fm_steps(self, wview, c0, nch, rhs, rhs_buf, evac_fn, nk=KD, ntg=2, wtile_cols=None):
        P = self.P
        slot = self.next_ra()
        tile = self.sb("wfm", [128, nk, nch * 128], BF16, slot.off)

        def dma_fn(slot=slot, tile=tile):
            P.dma("pool", tile, wview[:, 0:nk, c0:c0 + nch * 128], slot.ds, writes=[slot.buf])

        def comp_fn(slot=slot, tile=tile):
            for c in range(nch):
                pair = (self.chunk_ctr % 4) * 2
                self.chunk_ctr += 1
                for k in range(nk):
                    for tg in range(ntg):
                        bk = pair + tg
                        last = (k == nk - 1 and tg == ntg - 1)
                        ncols = 512 if ntg == 2 else rhs.shape[2]
                        P.op("pe", lambda e, bk=bk, k=k, tg=tg, c=c, ncols=ncols: e.matmul(
                            self.ps[:, bk, 0:ncols],
                            lhsT=tile[:, k, c * 128:(c + 1) * 128],
                            rhs=rhs[:, k, tg * 512:(tg + 1) * 512] if ntg == 2 else rhs[:, k, :],
                            start=(k == 0), stop=(k == nk - 1)),
                            reads=[slot.buf, rhs_buf], writes=[self.bank[bk]], flag=last)
                evac_fn(c, pair)

        self.ws.add(slot, dma_fn, comp_fn)

    def ffn_core(self, pre):
        P = self.P
        wgv = self.w[pre + "w_gate"].rearrange("(k p) n -> p k n", p=128)
        wuv = self.w[pre + "w_up"].rearrange("(k p) n -> p k n", p=128)
        wdv = self.w[pre + "w_down"].rearrange("(k p) n -> p k n", p=128)
        ws = self.ws
        ctr = [0]
        for half in range(2):
            for fg in range(11):
                slot = self.next_ra()
                tile = self.sb("wgu", [128, 2, KD, 256], BF16, slot.off)
                c0 = (half * 22 + fg * 2) * 128

                def dma_fn(slot=slot, tile=tile, c0=c0):
                    P.dma("pool", tile[:, 0], wgv[:, :, c0:c0 + 256], slot.ds, writes=[slot.buf])
                    P.dma("pool", tile[:, 1], wuv[:, :, c0:c0 + 256], slot.ds, writes=[slot.buf])

                def comp_fn(slot=slot, tile=tile, fg=fg):
                    for fi in range(2):
                        fl = fg * 2 + fi
                        par = ctr[0] % 2
                        ctr[0] += 1
                        b0 = par * 4
                        for gu in range(2):
                            for k in range(KD):
                                for tg in range(2):
                                    bk = b0 + gu * 2 + tg
                                    last = (k == KD - 1 and tg == 1)
                                    P.op("pe", lambda e, bk=bk, gu=gu, k=k, tg=tg, fi=fi: e.matmul(
                                        self.ps[:, bk, :], lhsT=tile[:, gu, k, fi * 128:(fi + 1) * 128],
                                        rhs=self.hT[:, k, tg * 512:(tg + 1) * 512],
                                        start=(k == 0), stop=(k == KD - 1)),
                                        reads=[slot.buf, self.bhT], writes=[self.bank[bk]], flag=last)
                        sil = self.SIL[par]
                        bsil = self.bSIL[par]
                        pg = self.ps[:, b0:b0 + 2, :].rearrange("p a n -> p (a n)")
                        pu = self.ps[:, b0 + 2:b0 + 4, :].rearrange("p a n -> p (a n)")
                        P.op("act", lambda e, sil=sil, pg=pg: e.activation(out=sil, in_=pg, func=AF.Silu),
                             reads=[self.bank[b0], self.bank[b0 + 1]], ow=[bsil])
                        P.op("dve", lambda e, sil=sil, pu=pu, fl=fl: e.tensor_tensor(
                            out=self.actT[:, fl, :], in0=pu, in1=sil, op=ALU.mult),
                            reads=[bsil, self.bank[b0 + 2], self.bank[b0 + 3]], ow=[self.bact[fl]])

                ws.add(slot, dma_fn, comp_fn)
            for ng in range(4):
                def evac_fn(tb, ng=ng, half=half):
                    ydst = self.A[:, tb, ng * 512:(ng + 1) * 512]
                    if half == 0:
                        if tb % 2 == 0:
                            P.op("act", lambda e: e.copy(out=ydst, in_=self.ps[:, tb, :]),
                                 reads=[self.bank[tb]], ow=[self.bA[tb]])
                        else:
                            P.op("dve", lambda e: e.tensor_copy(out=ydst, in_=self.ps[:, tb, :]),
                                 reads=[self.bank[tb]], ow=[self.bA[tb]])
                    else:
                        P.op("dve", lambda e: e.tensor_tensor(out=ydst, in0=self.ps[:, tb, :], in1=ydst, op=ALU.add),
                             reads=[self.bank[tb], self.bA[tb]], writes=[self.bA[tb]])

                self.tm_steps(wdv, ng * 512, lambda kl, tb: (self.actT[:, kl, tb * 128:(tb + 1) * 128], self.bact[kl]),
                              22, evac_fn, kbase=half * 22)
        ws.run()

    def mixer(self):
        P = self.P
        nc = self.nc
        A0 = self.offA
        B0 = self.offB

        memx = self.sb("memx", [128, 2, D], F32, A0)
        fkst = [self.sb("fkst%d" % i, [128, T], BF16, A0 + 16384 + i * 2048) for i in range(2)]
        fvst = self.sb("fvst", [128, 8, NTB, 128], BF16, A0 + 20480)
        dkTst = self.sb("dkTst", [128, 4, T], BF16, A0 + 36864)
        dvst = self.sb("dvst", [128, 4, NTB, 128], BF16, A0 + 45056)
        ropeb = [self.sb("ropeb%d" % i, [128, 512], BF16, A0 + 53248 + i * 1024) for i in range(2)]
        ropet = self.sb("ropet", [128, 4, 8, 8], F32, A0 + 55296)
        spst = self.sb("spst", [128, 64], F32, A0 + 56320)
        b_memx = [Buf("memx0"), Buf("memx1")]
        b_fkst = [Buf("fkst0"), Buf("fkst1")]
        b_fvst = Buf("fvst"); b_dkTst = Buf("dkTst"); b_dvst = Buf("dvst")
        b_ropeb = [Buf("ropeb0"), Buf("ropeb1")]
        b_ropet = Buf("ropet"); b_spst = Buf("spst")
        projA = b_memx + b_fkst + [b_fvst, b_dkTst, b_dvst] + b_ropeb + [b_ropet, b_spst]
        inherit(projA, self.bA)
        ds_mem = P.dsem("mem")
        ds_fk = [P.dsem("fk0"), P.dsem("fk1")]
        ds_fv = P.dsem("fv"); ds_dk = P.dsem("dk"); ds_dv = P.dsem("dv"); ds_sp = P.dsem("spx")

        for blk in range(2):
            P.dma("sp", memx[:, blk, :], self.mem_in[blk * 128:(blk + 1) * 128, :], ds_mem, writes=[b_memx[blk]])
        P.join(ds_mem, b_memx)
        mhT = self.sb("mhT", [128, KD, 256], BF16, B0 + 36864)
        b_mhT = Buf("mhT")
        inherit([b_mhT], list(self.rB))
        self.prenorm(2, lambda tb: (memx[:, tb, :], b_memx[tb]), "mem_norm_g",
                     lambda tb, half: mhT[:, half * 8:(half + 1) * 8, tb * 128:(tb + 1) * 128], b_mhT)
        fqT = self.sb("fqT", [128, 8, T], BF16, B0)
        dqT = self.sb("dqT", [128, 4, T], BF16, B0 + 16384)
        mqT = self.sb("mqT", [128, 4, T], BF16, B0 + 24576)
        mkT = self.sb("mkT", [128, 4, 256], BF16, B0 + 32768)
        mv = self.sb("mv", [128, 2, 512], BF16, B0 + 34816)
        Y1 = self.sb("Y1", [128, T], F32, B0 + 36864)
        Y2 = self.sb("Y2", [128, T], F32, B0 + 40960)
        b_fqT = [Buf("fqT%d" % i) for i in range(8)]
        b_dqT = Buf("dqT"); b_mqT = [Buf("mqT%d" % i) for i in range(4)]
        b_mkT = Buf("mkT"); b_mv = Buf("mv"); b_Y1 = Buf("Y1"); b_Y2 = Buf("Y2")
        mixB = b_fqT + [b_dqT] + b_mqT + [b_mkT, b_mv, b_mhT]
        self.set_rB(mixB)
        inherit([b_Y1, b_Y2], [b_mhT])
        self.rB = mixB + [b_Y1, b_Y2]

        wkv = self.w["w_mem_kv"].rearrange("(k p) n -> p k n", p=128)
        win = self.w["w_in"].rearrange("(k p) n -> p k n", p=128)

        b_XKs = Buf("XKs"); b_XVs = Buf("XVs"); b_XDKs = Buf("XDKs"); b_XDVs = Buf("XDVs"); b_XLs = Buf("XLs")
        b_XKd = Buf("XKd"); b_XVd = Buf("XVd"); b_XDKd = Buf("XDKd"); b_XDVd = Buf("XDVd"); b_XLd = Buf("XLd")
        ds_cc = [P.dsem("cc%d" % i) for i in range(5)]

        def gather(i, src, bsrc, dst, bdst):
            def step():
                P.aop("pool", lambda e: e.collective_compute(
                    "AllGather", ALU.bypass, replica_groups=RG, ins=[src], outs=[dst]),
                    ds_cc[i], 1, reads=[bsrc], writes=[bdst])
            self.ws.add(None, None, step)

        for grp in range(2):
            def evac_fk(c, pair, grp=grp):
                h = grp * 4 + c
                st = fkst[h % 2]; bst = b_fkst[h % 2]
                src = self.ps[:, pair:pair + 2, :].rearrange("p a n -> p (a n)")
                if h % 2 == 0:
                    P.op("act", lambda e: e.copy(out=st, in_=src), reads=[self.bank[pair], self.bank[pair + 1]], ow=[bst])
                else:
                    P.op("dve", lambda e: e.tensor_copy(out=st, in_=src),
                         reads=[self.bank[pair], self.bank[pair + 1]], ow=[bst])
                P.dma("sp", self.XKs[h * 128:(h + 1) * 128, :], st, ds_fk[h % 2], reads=[bst], writes=[b_XKs])
            self.fm_steps(win, C_FK + grp * 512, 4, self.hT, self.bhT, evac_fk)
        gather(0, self.XKs, b_XKs, self.XKd, b_XKd)

        lhs_h = lambda kl, tb: (self.hT[:, kl, tb * 128:(tb + 1) * 128], self.bhT)
        for grp in range(2):
            def evac_fv(tb, grp=grp):
                dst = fvst[:, grp * 4:(grp + 1) * 4, tb, :]
                src = self.ps[:, tb, :].rearrange("p (h d) -> p h d", h=4)
                if tb % 2 == 0:
                    P.op("act", lambda e: e.copy(out=dst, in_=src), reads=[self.bank[tb]], ow=[b_fvst])
                else:
                    P.op("dve", lambda e: e.tensor_copy(out=dst, in_=src), reads=[self.bank[tb]], ow=[b_fvst])
            self.tm_steps(win, C_FV + grp * 512, lhs_h, KD, evac_fv)

        def st_fv():
            P.dma("sp", self.XVs, fvst.rearrange("p h t d -> p (h t d)"), ds_fv, reads=[b_fvst], writes=[b_XVs])
        self.ws.add(None, None, st_fv)
        gather(1, self.XVs, b_XVs, self.XVd, b_XVd)

        def rope_evac(tb, dstT, bdstT, scale):
            rb = ropeb[tb % 2]; brb = b_ropeb[tb % 2]
            pv = self.ps[:, tb, :].rearrange("p (g c) -> p g c", c=64)
            rbv = rb.rearrange("p (g c) -> p g c", c=64)
            cos = self.rope[:, 0, tb, :].unsqueeze(1).broadcast_to([128, 8, 8])
            sin = self.rope[:, 1, tb, :].unsqueeze(1).broadcast_to([128, 8, 8])
            x1 = pv[:, :, 0:8]; x2 = pv[:, :, 8:16]
            t = [ropet[:, i] for i in range(4)]
            rd = [self.bconst, self.bank[tb]]
            P.op("dve", lambda e: e.tensor_tensor(out=t[0], in0=x1, in1=cos, op=ALU.mult), reads=rd, writes=[b_ropet])
            P.op("dve", lambda e: e.tensor_tensor(out=t[1], in0=x2, in1=sin, op=ALU.mult), reads=rd, writes=[b_ropet])
            P.op("dve", lambda e: e.tensor_tensor(out=t[2], in0=x2, in1=cos, op=ALU.mult), reads=rd, writes=[b_ropet])
            P.op("dve", lambda e: e.tensor_tensor(out=t[3], in0=x1, in1=sin, op=ALU.mult), reads=rd, writes=[b_ropet])
            P.op("dve", lambda e: e.tensor_tensor(out=rbv[:, :, 0:8], in0=t[0], in1=t[1], op=ALU.subtract),
                 reads=[b_ropet], ow=[brb])
            P.op("dve", lambda e: e.tensor_tensor(out=rbv[:, :, 8:16], in0=t[2], in1=t[3], op=ALU.add),
                 reads=[b_ropet], ow=[brb])
            P.op("act", lambda e: e.copy(out=rbv[:, :, 16:64], in_=pv[:, :, 16:64]), reads=[self.bank[tb]], ow=[brb])
            pst = self.ps[:, tb, :].bitcast(BF16)
            for h in range(4):
                P.op("pe", lambda e, h=h: e.transpose(out=pst[:, h * 128:(h + 1) * 128],
                                                      in_=rb[:, h * 128:(h + 1) * 128], identity=self.ident),
                     reads=[brb, self.bconst], writes=[self.bank[tb]], flag=(h == 3))
            src = pst[:, 0:512].rearrange("p (h n) -> p h n", h=4)
            dst = dstT[:, :, tb * 128:(tb + 1) * 128]
            P.op("act", lambda e: e.mul(out=dst, in_=src, mul=scale), reads=[self.bank[tb]], ow=[bdstT])

        self.tm_steps(win, C_DK, lhs_h, KD, lambda tb: rope_evac(tb, dkTst, b_dkTst, 1.0))

        def st_dk():
            P.dma("sp", self.XDKs.rearrange("(h p) t -> p h t", p=128), dkTst, ds_dk, reads=[b_dkTst], writes=[b_XDKs])
        self.ws.add(None, None, st_dk)
        gather(2, self.XDKs, b_XDKs, self.XDKd, b_XDKd)

        def evac_dv(tb):
            dst = dvst[:, :, tb, :]
            src = self.ps[:, tb, :].rearrange("p (h d) -> p h d", h=4)
            if tb % 2 == 0:
                P.op("act", lambda e: e.copy(out=dst, in_=src), reads=[self.bank[tb]], ow=[b_dvst])
            else:
                P.op("dve", lambda e: e.tensor_copy(out=dst, in_=src), reads=[self.bank[tb]], ow=[b_dvst])
        self.tm_steps(win, C_DV, lhs_h, KD, evac_dv)

        def st_dv():
            P.dma("sp", self.XDVs, dvst.rearrange("p h t d -> p (h t d)"), ds_dv, reads=[b_dvst], writes=[b_XDVs])
        self.ws.add(None, None, st_dv)
        gather(3, self.XDVs, b_XDVs, self.XDVd, b_XDVd)

        def ff_step():
            for tb in range(NTB):
                for k in range(KD):
                    P.op("pe", lambda e, tb=tb, k=k: e.matmul(
                        self.ps[:, 0, tb * 8:(tb + 1) * 8], lhsT=self.hT[:, k, tb * 128:(tb + 1) * 128],
                        rhs=self.wff[:, k, :], start=(k == 0), stop=(k == KD - 1)),
                        reads=[self.bhT, self.bwff], writes=[self.bank[0]], flag=(k == KD - 1 and tb == NTB - 1))
            fb = self.fbias.unsqueeze(1).broadcast_to([128, NTB, 8])
            P.op("dve", lambda e: e.tensor_tensor(out=spst.rearrange("p (t h) -> p t h", h=8),
                                                  in0=self.ps[:, 0, 0:64].rearrange("p (t h) -> p t h", h=8),
                                                  in1=fb, op=ALU.add),
                 reads=[self.bconst, self.bank[0]], writes=[b_spst])
            P.op("act", lambda e: e.activation(out=spst, in_=spst, func=AF.Exp, scale=-1.0), reads=[b_spst], writes=[b_spst])
            P.op("act", lambda e: e.activation(out=spst, in_=spst, func=AF.Ln, bias=self.oneb, scale=1.0),
                 reads=[self.bconst, b_spst], writes=[b_spst])
            P.dma("sp", self.XLs, spst, ds_sp, reads=[b_spst], writes=[b_XLs])
        self.ws.add(None, None, ff_step)
        gather(4, self.XLs, b_XLs, self.XLd, b_XLd)

        def evac_mk(c, pair):
            P.op("act", lambda e: e.copy(out=mkT[:, c, :], in_=self.ps[:, pair, 0:256]),
                 reads=[self.bank[pair]], ow=[b_mkT])
        self.fm_steps(wkv, 0, 4, mhT, b_mhT, evac_mk, ntg=1)
        for kb in range(KD // 2):
            slot = self.next_rb()
            tile = self.sb("wmv", [128, 2, 512], BF16, slot.off)

            def dma_fn(slot=slot, tile=tile, kb=kb):
                P.dma("pool", tile, wkv[:, kb * 2:kb * 2 + 2, 512:1024], slot.ds, writes=[slot.buf])

            def comp_fn(slot=slot, tile=tile, kb=kb):
                for kk in range(2):
                    kl = kb * 2 + kk
                    for blk in range(2):
                        last = (kl == KD - 1)
                        P.op("pe", lambda e, kk=kk, kl=kl, blk=blk, last=last: e.matmul(
                            self.ps[:, blk, :], lhsT=mhT[:, kl, blk * 128:(blk + 1) * 128], rhs=tile[:, kk, :],
                            start=(kl == 0), stop=last),
                            reads=[slot.buf, b_mhT], writes=[self.bank[blk]], flag=last or (kk == 1 and blk == 1))
                if kb == KD // 2 - 1:
                    for blk in range(2):
                        P.op("dve", lambda e, blk=blk: e.tensor_copy(out=mv[:, blk, :], in_=self.ps[:, blk, :]),
                             reads=[self.bank[blk]], ow=[b_mv])
            self.ws.add(slot, dma_fn, comp_fn)

        sc_f = 128.0 ** -0.5
        for grp in range(2):
            def evac_fq(c, pair, grp=grp):
                h = grp * 4 + c
                src = self.ps[:, pair:pair + 2, :].rearrange("p a n -> p (a n)")
                P.op("act", lambda e: e.mul(out=fqT[:, h, :], in_=src, mul=sc_f),
                     reads=[self.bank[pair], self.bank[pair + 1]], ow=[b_fqT[h]])
            self.fm_steps(win, C_FQ + grp * 512, 4, self.hT, self.bhT, evac_fq)

        def evac_mq(c, pair):
            src = self.ps[:, pair:pair + 2, :].rearrange("p a n -> p (a n)")
            P.op("dve", lambda e: e.tensor_scalar(out=mqT[:, c, :], in0=src, scalar1=sc_f, scalar2=None, op0=ALU.mult),
                 reads=[self.bank[pair], self.bank[pair + 1]], ow=[b_mqT[c]])
        self.fm_steps(win, C_MQ, 4, self.hT, self.bhT, evac_mq)
        self.tm_steps(win, C_DQ, lhs_h, KD, lambda tb: rope_evac(tb, dqT, b_dqT, 64.0 ** -0.5))
        self.ws.run()
        self.dump("fqT", fqT, b_fqT, [128, 8, T], BF16)
        self.dump("dqT", dqT, [b_dqT], [128, 4, T], BF16)
        self.dump("mkT", mkT, [b_mkT], [128, 4, 256], BF16)
        self.dump("mv", mv, [b_mv], [128, 2, 512], BF16)

        S0 = self.offSIL
        lamt = self.sb("lamt", [128, 4, 64], F32, S0)
        Lt = self.sb("Lt", [128, 2, 64], F32, S0 + 1024)
        Lpre = self.sb("Lpre", [128, 2, 64], F32, S0 + 1536)
        cmat = self.sb("cmat", [128, 3, 128], F32, S0 + 2048)
        b_S = Buf("silscratch")
        inherit([b_S], self.bSIL)
        ds_l = P.dsem("lam")
        for i, nm in enumerate(("diff_lambda_q1", "diff_lambda_k1", "diff_lambda_q2", "diff_lambda_k2")):
            P.dma("sp", lamt[:, i, :], self.w[nm][0:1, :].partition_broadcast(128)[:, 0, :], ds_l, writes=[b_S])
        P.dma("sp", cmat, self.cmat_in, ds_l, writes=[b_S])
        P.join(ds_l, [b_S])
        lv = self.lamv
        b_lam = Buf("lam")
        inherit([b_lam], [self.bconst])
        P.op("dve", lambda e: e.tensor_tensor(out=lamt[:, 0, :], in0=lamt[:, 0, :], in1=lamt[:, 1, :], op=ALU.mult),
             writes=[b_S])
        P.op("dve", lambda e: e.tensor_tensor(out=lamt[:, 2, :], in0=lamt[:, 2, :], in1=lamt[:, 3, :], op=ALU.mult),
             writes=[b_S])
        P.op("dve", lambda e: e.tensor_reduce(out=lv[:, 0:1], in_=lamt[:, 0, :], axis=AX.X, op=ALU.add),
             reads=[b_S], writes=[b_lam])
        P.op("dve", lambda e: e.tensor_reduce(out=lv[:, 1:2], in_=lamt[:, 2, :], axis=AX.X, op=ALU.add),
             reads=[b_S], writes=[b_lam])
        P.op("act", lambda e: e.activation(out=lv[:, 2:4], in_=lv[:, 0:2], func=AF.Exp), writes=[b_lam])
        P.op("dve", lambda e: e.tensor_tensor(out=lv[:, 4:5], in0=lv[:, 2:3], in1=lv[:, 3:4], op=ALU.subtract),
             writes=[b_lam])
        P.op("dve", lambda e: e.tensor_scalar(out=lv[:, 5:6], in0=lv[:, 4:5], scalar1=-1.0, scalar2=-LAM_INIT,
                                              op0=ALU.mult, op1=ALU.add), writes=[b_lam])
        P.op("dve", lambda e: e.tensor_scalar(out=lv[:, 6:7], in0=lv[:, 6:7], scalar1=1.0 - LAM_INIT, scalar2=None,
                                              op0=ALU.mult), writes=[b_lam])
        neglam = lv[:, 5:6]
        gsc = lv[:, 6:7]

        ds_lt = P.dsem("lt")
        b_Lt = Buf("Lt")
        inherit([b_Lt], [b_S])
        P.dma("sp", Lt, self.XLd.rearrange("(r s) c -> s r c", r=2), ds_lt, reads=[b_XLd], writes=[b_Lt])
        Ltv = Lt.rearrange("p r (j h) -> p r j h", h=8)
        Lpv = Lpre.rearrange("p r (j h) -> p r j h", h=8)
        P.op("dve", lambda e: e.memset(Lpv[:, 0, 0, :], 0.0), writes=[b_S])
        for g in range(1, 16):
            r, j = g % 2, g // 2
            rp, jp = (g - 1) % 2, (g - 1) // 2
            P.op("dve", lambda e, r=r, j=j, rp=rp, jp=jp: e.tensor_tensor(
                out=Lpv[:, r, j, :], in0=Lpv[:, rp, jp, :], in1=Ltv[:, rp, jp, :], op=ALU.add), reads=[b_Lt], writes=[b_S])
        b_D = Buf("Dtab")
        inherit([b_D], [self.bconst])
        P.op("pe", lambda e: e.matmul(self.ps[:, 0, 0:128], lhsT=cmat[:, 0, :], rhs=Lt.rearrange("p r c -> p (r c)"),
                                      start=True, stop=False), reads=[b_S, b_Lt], writes=[self.bank[0]], flag=False)
        P.op("pe", lambda e: e.matmul(self.ps[:, 0, 0:128], lhsT=cmat[:, 1, :], rhs=Lpre.rearrange("p r c -> p (r c)"),
                                      start=False, stop=True), reads=[b_S], writes=[self.bank[0]])
        P.op("dve", lambda e: e.tensor_copy(out=self.Dtab, in_=self.ps[:, 0, 0:128]), reads=[self.bank[0]], writes=[b_D])
        P.op("pe", lambda e: e.matmul(self.ps[:, 1, 0:128], lhsT=cmat[:, 2, :], rhs=self.Dtab, start=True, stop=True),
             reads=[b_S, b_D], writes=[self.bank[1]])
        P.op("dve", lambda e: e.tensor_scalar(out=self.Dend, in0=self.ps[:, 1, 0:64], scalar1=self.psel[:, 0:1],
                                              scalar2=None, op0=ALU.mult),
             reads=[self.bconst, self.bank[1]], writes=[b_D])
        P.op("dve", lambda e: e.scalar_tensor_tensor(out=self.Dend, in0=self.ps[:, 1, 64:128], scalar=self.psel[:, 1:2],
                                                     in1=self.Dend, op0=ALU.mult, op1=ALU.add),
             reads=[self.bconst, self.bank[1], b_D], writes=[b_D])
        in0 = self.Dtab.rearrange("p (g h) -> p g h", h=8).unsqueeze(3).broadcast_to([128, 16, 8, 8])
        in1 = self.Dend.rearrange("p (j h) -> p h j", h=8).unsqueeze(1).broadcast_to([128, 16, 8, 8])
        b_bias = Buf("biasT")
        inherit([b_bias], [self.bconst])
        P.op("dve", lambda e: e.tensor_tensor(out=self.biasT, in0=in0, in1=in1, op=ALU.subtract),
             reads=[b_D], writes=[b_bias])
        bflat = self.biasT.rearrange("p a b c -> p (a b c)")
        P.op("dve", lambda e: e.tensor_scalar_min(out=bflat, in0=bflat, scalar1=0.0), writes=[b_bias])
        self.dump("Dtab", self.Dtab, [b_D], [128, 128], F32)
        self.dump("biasT", bflat, [b_bias], [128, 1024], F32)

        yfoxT = self.sb("yfoxT", [128, 8, T], BF16, A0)
        ydiffT = self.sb("ydiffT", [128, 4, T], BF16, A0 + 16384)
        ymemT = self.sb("ymemT", [128, 4, T], BF16, A0 + 24576)
        KTb = [self.sb("KT%d" % i, [128, 2, T], BF16, A0 + 32768 + i * 4096) for i in range(2)]
        Vb = [self.sb("V%d" % i, [128, 2, NTB, 128], BF16, A0 + 40960 + i * 4096) for i in range(2)]
        PT = [self.sb("PT%d" % i, [128, T], BF16, A0 + 49152 + i * 2048) for i in range(3)]
        recip = self.sb("recip", [128, T], F32, A0 + 55296)
        sq = self.sb("sq", [128, T], BF16, A0 + 59392)
        accS = self.sb("accS", [128, T], F32, A0 + 61440)
        b_yfox = [Buf("yfox%d" % i) for i in range(8)]
        b_ydiff = [Buf("ydiff%d" % i) for i in range(4)]
        b_ymem = [Buf("ymem%d" % i) for i in range(4)]
        b_KT = [Buf("KT0"), Buf("KT1")]; b_V = [Buf("V0"), Buf("V1")]
        b_PT = [[Buf("PT%d_%d" % (i, j)) for j in range(8)] for i in range(3)]
        b_recip = Buf("recip"); b_sq = Buf("sq"); b_accS = Buf("accS")
        b_PTall = b_PT[0] + b_PT[1] + b_PT[2]
        attnA = b_yfox + b_ydiff + b_ymem + b_KT + b_V + b_PTall + [b_recip, b_sq, b_accS]
        inherit(attnA, projA)
        ds_kv = [P.dsem("kv0"), P.dsem("kv1")]
        self.pt_i = 0
        self.sp_i = 0
        accv = self.ps[:, 4:6, :].rearrange("p a n -> p (a n)")
        rsv = self.ps[:, 6:8, :].rearrange("p a n -> p (a n)")

        passes = []
        deferred = []

        def add_pass(nkb, q_ap, q_bufs, k_fn, v_fn, kv_bufs, causal, bias_fn, mask_fn, pre_fn, post_fn):
            passes.append(dict(nkb=nkb, q_ap=q_ap, q_bufs=q_bufs, k_fn=k_fn, v_fn=v_fn, kv_bufs=kv_bufs,
                               causal=causal, bias_fn=bias_fn, mask_fn=mask_fn, pre_fn=pre_fn, post_fn=post_fn))

        def item_geom(p, g):
            jmin = g // 2 if p["causal"] else 0
            c0 = jmin * 128
            parts = []
            if c0 < 512:
                parts.append((0, c0, 512))
            parts.append((1, max(c0, 512), 1024))
            return jmin, c0, parts

        pre_done = set()

        def pre(k):
            if 0 <= k < len(passes) and k not in pre_done:
                pre_done.add(k)
                if passes[k]["pre_fn"] is not None:
                    passes[k]["pre_fn"]()

        def emit_qk(p, g, st):
            jmin, c0, parts = item_geom(p, g)
            sp = self.sp_i % 2; self.sp_i += 1
            pi = self.pt_i % 3; self.pt_i += 1
            st["sp"] = sp; st["pi"] = pi
            k_fn = p["k_fn"]; q_ap = p["q_ap"]
            for ip, (hb_, lo, hi) in enumerate(parts):
                bk = 2 * sp + hb_
                l0 = lo - hb_ * 512
                P.op("pe", lambda e, bk=bk, lo=lo, hi=hi, g=g, l0=l0: e.matmul(
                    self.ps[:, bk, l0:l0 + (hi - lo)],
                    lhsT=k_fn(g), rhs=q_ap[:, lo:hi], start=True, stop=True),
                    reads=list(p["kv_bufs"]) + list(p["q_bufs"]), writes=[self.bank[bk]], flag=(ip == len(parts) - 1))

        def emit_rest(p, g, st):
            jmin, c0, parts = item_geom(p, g)
            sp = st["sp"]; pi = st["pi"]
            nkb = p["nkb"]
            pt = PT[pi]; bpt = b_PT[pi]
            bias_fn = p["bias_fn"]; mask_fn = p["mask_fn"]; v_fn = p["v_fn"]
            sbanks = [self.bank[2 * sp + hb_] for (hb_, _, _) in parts]
            if bias_fn is None:
                sv = self.ps[:, 2 * sp:2 * sp + 2, :].rearrange("p a n -> p (a n)")
                P.op("act", lambda e: e.activation(out=pt[:, c0:T], in_=sv[:, c0:T], func=AF.Exp),
                     reads=sbanks, ow=bpt[jmin:8])
            else:
                for j in range(jmin, 8):
                    bk = 2 * sp + j // 4
                    P.op("act", lambda e, bk=bk, j=j: e.activation(
                        out=pt[:, j * 128:(j + 1) * 128], in_=self.ps[:, bk, (j % 4) * 128:(j % 4 + 1) * 128],
                        func=AF.Exp, bias=bias_fn(g, j), scale=1.0),
                        reads=[b_bias, self.bank[bk]], ow=[bpt[j]])
            if mask_fn is not None:
                P.op("pool", lambda e: e.tensor_tensor(
                    out=pt[:, c0:c0 + 128], in0=pt[:, c0:c0 + 128], in1=mask_fn(g), op=ALU.mult),
                    reads=[self.bconst, bpt[jmin]], writes=[bpt[jmin]])
            for ip, (hb_, lo, hi) in enumerate(parts):
                l0 = lo - hb_ * 512
                P.op("pe", lambda e, hb_=hb_, l0=l0, lo=lo, hi=hi: e.matmul(
                    self.ps[:, 4 + hb_, l0:l0 + (hi - lo)], lhsT=v_fn(g), rhs=pt[:, lo:hi],
                    start=(g == 0), stop=(g == nkb - 1), skip_group_check=True),
                    reads=list(p["kv_bufs"]) + bpt[lo // 128:hi // 128], writes=[self.bank[4 + hb_]], flag=False)
                P.op("pe", lambda e, hb_=hb_, l0=l0, lo=lo, hi=hi: e.matmul(
                    self.ps[:, 6 + hb_, l0:l0 + (hi - lo)], lhsT=self.onesb, rhs=pt[:, lo:hi],
                    start=(g == 0), stop=(g == nkb - 1), skip_group_check=True),
                    reads=bpt[lo // 128:hi // 128] + [self.bconst], writes=[self.bank[6 + hb_]],
                    flag=(ip == len(parts) - 1))
            if g == nkb - 1:
                d = p["post_fn"](sp)
                if d is not None:
                    deferred.append([4, d])
            for dd in list(deferred):
                dd[0] -= 1
                if dd[0] <= 0:
                    deferred.remove(dd)
                    dd[1](sp)

        def run_passes():
            items = [(p, g, {}) for p in passes for g in range(p["nkb"])]
            pre(0); pre(1)
            emit_qk(*items[0])
            for i, it in enumerate(items):
                if i + 1 < len(items):
                    emit_qk(*items[i + 1])
                emit_rest(*it)
                if it[1] == it[0]["nkb"] - 1:
                    pre(passes.index(it[0]) + 2)
            for dd in deferred:
                dd[1](it[2]["sp"])

        def normalize(dst, bdst):
            P.op("act", lambda e: e.copy(out=accS, in_=accv), reads=[self.bank[4], self.bank[5]], writes=[b_accS])
            P.op("dve", lambda e: e.tensor_copy(out=recip, in_=rsv), reads=[self.bank[6], self.bank[7]], writes=[b_recip])
            P.op("dve", lambda e: e.reciprocal(out=recip, in_=recip), reads=[b_recip], writes=[b_recip])
            P.op("dve", lambda e: e.tensor_tensor(out=dst, in0=accS, in1=recip, op=ALU.mult),
                 reads=[b_recip, b_accS], ow=[bdst])

        for h in range(4):
            add_pass(2, mqT[:, h, :], [b_mqT[h]],
                     lambda g, h=h: mkT[:, h, g * 128:(g + 1) * 128],
                     lambda g, h=h: mv[:, g, h * 128:(h + 1) * 128],
                     [b_mkT, b_mv], False, None, None, None,
                     lambda sp, h=h: normalize(ymemT[:, h, :], b_ymem[h]))

        XKv = self.XKd.rearrange("(r h d) t -> h d r t", r=2, h=8)
        XVv = self.XVd.rearrange("(r s) (h x) -> h s r x", r=2, h=8)
        kvi = 0
        for h in range(8):
            i = kvi % 2; kvi += 1
            kt = KTb[i]; vv = Vb[i]

            def pre_fox(h=h, i=i):
                P.dma("sp", KTb[i], XKv[h], ds_kv[i], reads=[b_XKd], writes=[b_KT[i]])
                P.dma("sp", Vb[i].rearrange("p r t d -> p r (t d)"), XVv[h], ds_kv[i], reads=[b_XVd], writes=[b_V[i]])
                P.join(ds_kv[i], [b_KT[i], b_V[i]])
            add_pass(16, fqT[:, h, :], [b_fqT[h]],
                     lambda g, kt=kt: kt[:, g % 2, (g // 2) * 128:(g // 2 + 1) * 128],
                     lambda g, vv=vv: vv[:, g % 2, g // 2, :],
                     [b_KT[i], b_V[i]], True,
                     lambda g, j, h=h: self.biasT[:, (g % 2) * 8 + g // 2, h, j:j + 1],
                     lambda g: self.masks[:, g % 2, :], pre_fox,
                     lambda sp, h=h: normalize(yfoxT[:, h, :], b_yfox[h]))

        XDKv = self.XDKd.rearrange("(r h d) t -> h d r t", r=2, h=4)
        XDVv = self.XDVd.rearrange("(r s) (h x) -> h s r x", r=2, h=4)

        def diff_tail(sp, h):
            P.op("dve", lambda e: e.scalar_tensor_tensor(out=Y1, in0=Y2, scalar=neglam, in1=Y1, op0=ALU.mult, op1=ALU.add),
                 reads=[b_lam, b_Y2, b_Y1], writes=[b_Y1])
            P.op("act", lambda e: e.activation(out=sq, in_=Y1, func=AF.Square), reads=[b_Y1], writes=[b_sq])
            for tg in range(2):
                P.op("pe", lambda e, tg=tg: e.matmul(self.ps[:, 2 * sp + tg, :], lhsT=self.onesb,
                                                    rhs=sq[:, tg * 512:(tg + 1) * 512], start=True, stop=True),
                     reads=[b_sq, self.bconst], writes=[self.bank[2 * sp + tg]], flag=(tg == 1))
            ssv = self.ps[:, 2 * sp:2 * sp + 2, :].rearrange("p a n -> p (a n)")
            P.op("act", lambda e: e.activation(out=recip, in_=ssv, func=AF.Sqrt, bias=self.epsb, scale=1.0 / 128.0),
                 reads=[self.bconst, self.bank[2 * sp], self.bank[2 * sp + 1]], writes=[b_recip])
            P.op("dve", lambda e: e.reciprocal(out=recip, in_=recip), reads=[b_recip], writes=[b_recip])
            P.op("dve", lambda e: e.scalar_tensor_tensor(out=ydiffT[:, h, :], in0=Y1, scalar=gsc, in1=recip,
                                                         op0=ALU.mult, op1=ALU.mult),
                 reads=[b_Y1, b_recip, b_lam], ow=[b_ydiff[h]])

        for h in range(4):
            i = kvi % 2; kvi += 1
            kt = KTb[i]; vv = Vb[i]

            def pre_diff(h=h, i=i):
                P.dma("sp", KTb[i], XDKv[h], ds_kv[i], reads=[b_XDKd], writes=[b_KT[i]])
                P.dma("sp", Vb[i].rearrange("p r t d -> p r (t d)"), XDVv[h], ds_kv[i], reads=[b_XDVd], writes=[b_V[i]])
                P.join(ds_kv[i], [b_KT[i], b_V[i]])
            for m in range(2):
                def post_diff(sp, h=h, m=m):
                    if m == 0:
                        normalize(Y1, b_Y1)
                        return None
                    normalize(Y2, b_Y2)
                    return lambda sp2, h=h: diff_tail(sp2, h)
                add_pass(16, dqT[m * 64:(m + 1) * 64, h, :], [b_dqT],
                         lambda g, kt=kt, m=m: kt[m * 64:(m + 1) * 64, g % 2, (g // 2) * 128:(g // 2 + 1) * 128],
                         lambda g, vv=vv: vv[:, g % 2, g // 2, :],
                         [b_KT[i], b_V[i]], True, None,
                         lambda g: self.masks[:, 2 + g % 2, :],
                         pre_diff if m == 0 else None, post_diff)
        run_passes()
        self.dump("ymemT", ymemT, b_ymem, [128, 4, T], BF16)
        self.dump("yfoxT", yfoxT, b_yfox, [128, 8, T], BF16)
        self.dump("ydiffT", ydiffT, b_ydiff, [128, 4, T], BF16)

        mrgT = self.sb("mrgT", [128, KD, T], BF16, B0)
        b_mrg = [Buf("mrg%d" % i) for i in range(KD)]
        self.set_rB(b_mrg + [Buf("Btail")])
        sig = [self.sb("sig%d" % i, [128, T], BF16, A0 + 32768 + i * 2048) for i in range(3)]
        acc = self.sb("macc", [128, T], F32, A0 + 32768 + 6144)
        tmp = [self.sb("mtmp%d" % i, [128, T], F32, A0 + 32768 + 10240 + i * 4096) for i in range(2)]
        b_sig = [Buf("sig%d" % i) for i in range(3)]
        b_acc = Buf("macc"); b_tmp = [Buf("mtmp0"), Buf("mtmp1")]
        inherit(b_sig + [b_acc] + b_tmp, b_KT + b_V + b_PTall + [b_recip, b_sq, b_accS])
        wmg = self.w["w_merge_gate"].rearrange("(k p) (g c) -> p k g c", p=128, g=3)
        wbr = [self.w["w_branch_fox"].rearrange("(k p) n -> p k n", p=128),
               self.w["w_branch_diff"].rearrange("(k p) n -> p k n", p=128),
               self.w["w_branch_mem"].rearrange("(k p) n -> p k n", p=128)]
        ybr = [(yfoxT, b_yfox, 8), (ydiffT, b_ydiff, 4), (ymemT, b_ymem, 4)]
        for m in range(KD):
            slot = self.next_ra()
            gt = self.sb("wmg", [128, KD, 3, 128], BF16, slot.off)
            bt = self.sb("wbr", [128, KD, 128], BF16, slot.off + 12288)

            def dma_fn(slot=slot, gt=gt, bt=bt, m=m):
                for b in range(3):
                    P.dma("pool", gt[:, :, b, :], wmg[:, :, b, m * 128:(m + 1) * 128], slot.ds, writes=[slot.buf])
                P.dma("pool", bt[:, 0:8, :], wbr[0][:, :, m * 128:(m + 1) * 128], slot.ds, writes=[slot.buf])
                P.dma("pool", bt[:, 8:12, :], wbr[1][:, :, m * 128:(m + 1) * 128], slot.ds, writes=[slot.buf])
                P.dma("pool", bt[:, 12:16, :], wbr[2][:, :, m * 128:(m + 1) * 128], slot.ds, writes=[slot.buf])

            def comp_fn(slot=slot, gt=gt, bt=bt, m=m):
                kofs = 0
                for b in range(3):
                    yT, byT, nkb = ybr[b]
                    pg = (self.chunk_ctr % 4) * 2; self.chunk_ctr += 1
                    for k in range(KD):
                        for tg in range(2):
                            P.op("pe", lambda e, k=k, tg=tg, b=b, pg=pg: e.matmul(
                                self.ps[:, pg + tg, :], lhsT=gt[:, k, b, :], rhs=self.hT[:, k, tg * 512:(tg + 1) * 512],
                                start=(k == 0), stop=(k == KD - 1)),
                                reads=[slot.buf, self.bhT], writes=[self.bank[pg + tg]], flag=(k == KD - 1 and tg == 1))
                    gv = self.ps[:, pg:pg + 2, :].rearrange("p a n -> p (a n)")
                    P.op("act", lambda e, b=b, gv=gv: e.activation(out=sig[b], in_=gv, func=AF.Sigmoid,
                                                                 bias=self.bmT[:, b * 16 + m:b * 16 + m + 1], scale=1.0),
                         reads=[self.bconst, self.bank[pg], self.bank[pg + 1]], ow=[b_sig[b]])
                    py = (self.chunk_ctr % 4) * 2; self.chunk_ctr += 1
                    for k in range(nkb):
                        for tg in range(2):
                            P.op("pe", lambda e, k=k, tg=tg, py=py, kofs=kofs, yT=yT, nkb=nkb: e.matmul(
                                self.ps[:, py + tg, :], lhsT=bt[:, kofs + k, :], rhs=yT[:, k, tg * 512:(tg + 1) * 512],
                                start=(k == 0), stop=(k == nkb - 1)),
                                reads=[slot.buf] + list(byT), writes=[self.bank[py + tg]],
                                flag=(k == nkb - 1 and tg == 1))
                    yv = self.ps[:, py:py + 2, :].rearrange("p a n -> p (a n)")
                    ybanks = [self.bank[py], self.bank[py + 1]]
                    if b == 0:
                        P.op("dve", lambda e, yv=yv: e.tensor_tensor(out=acc, in0=yv, in1=sig[0], op=ALU.mult),
                             reads=[b_sig[0]] + ybanks, ow=[b_acc])
                    elif b == 1:
                        P.op("dve", lambda e, yv=yv: e.tensor_tensor(out=tmp[0], in0=yv, in1=sig[1], op=ALU.mult),
                             reads=[b_sig[1]] + ybanks, ow=[b_tmp[0]])
                        P.op("dve", lambda e: e.tensor_tensor(out=acc, in0=acc, in1=tmp[0], op=ALU.add),
                             reads=[b_tmp[0], b_acc], writes=[b_acc])
                    else:
                        P.op("dve", lambda e, yv=yv: e.tensor_tensor(out=tmp[1], in0=yv, in1=sig[2], op=ALU.mult),
                             reads=[b_sig[2]] + ybanks, ow=[b_tmp[1]])
                        P.op("dve", lambda e, m=m: e.tensor_tensor(out=mrgT[:, m, :], in0=acc, in1=tmp[1], op=ALU.add),
                             reads=[b_tmp[1], b_acc], ow=[b_mrg[m]])
                    kofs += nkb

            self.ws.add(slot, dma_fn, comp_fn)
        wov = self.w["w_out"].rearrange("(k p) n -> p k n", p=128)
        self.ws.add(None, None, lambda: inherit(self.bA, attnA + b_sig + [b_acc] + b_tmp))
        for ng in range(4):
            def evac_o(tb, ng=ng):
                ydst = self.A[:, tb, ng * 512:(ng + 1) * 512]
                if tb % 2 == 0:
                    P.op("act", lambda e: e.copy(out=ydst, in_=self.ps[:, tb, :]), reads=[self.bank[tb]], ow=[self.bA[tb]])
                else:
                    P.op("dve", lambda e: e.tensor_copy(out=ydst, in_=self.ps[:, tb, :]),
                         reads=[self.bank[tb]], ow=[self.bA[tb]])
            self.tm_steps(wov, ng * 512, lambda kl, tb: (mrgT[:, kl, tb * 128:(tb + 1) * 128], b_mrg[kl]), KD, evac_o)
        self.ws.run()
        self.dump("mrgT", mrgT, b_mrg, [128, KD, T], BF16)
        inherit(self.bSIL, [b_S])
        self.set_rB(self.bact)


_NC_CACHE = {}


def _get_builder(stage, dbg=()):
    key = (stage, tuple(dbg))
    if key not in _NC_CACHE:
        b = Builder(stage, dbg)
        b.build()
        _NC_CACHE[key] = b
    return _NC_CACHE[key]


def _shard_x(x, c):
    b, p = c // 2, c % 2
    return np.ascontiguousarray(x[b].reshape(8, 2, 128, D)[:, p].reshape(T, D))


def _consts(p):
    s = np.arange(128)[:, None]
    t = np.arange(128)[None, :]
    tri = (s <= t).astype(np.float32)
    ones = np.ones((128, 128), np.float32)
    zeros = np.zeros((128, 128), np.float32)
    sel = np.zeros((128, 128), np.float32); sel[127, :] = 1.0
    cmat = np.stack([tri, ones, sel], axis=1)
    chunk = ((s // 64) <= (t // 64)).astype(np.float32)
    if p == 0:
        masks = [tri, zeros, chunk, zeros]
    else:
        masks = [ones, tri, ones, chunk]
    masks = np.stack(masks, axis=1).astype(ml_dtypes.bfloat16)
    pos = ((2 * np.arange(8)[None, :] + p) * 128 + np.arange(128)[:, None]).astype(np.float32)
    inv = (500000.0 ** (-np.arange(0, 16, 2, dtype=np.float32) / 16)).astype(np.float32)
    ang = pos[:, :, None] * inv[None, None, :]
    rope = np.stack([np.cos(ang), np.sin(ang)], axis=1).astype(np.float32)
    psel = np.zeros((128, 2), np.float32); psel[:, p] = 1.0
    return {"c_ident": np.eye(128, dtype=ml_dtypes.bfloat16), "c_mat": cmat, "c_masks": masks,
            "c_rope": rope, "c_psel": psel}


_W2D = ("ffn1_pre_g", "ffn1_w_gate", "ffn1_w_up", "ffn1_w_down", "ffn1_post_g", "mix_pre_g", "w_in", "fox_f_bias",
        "diff_lambda_q1", "diff_lambda_k1", "diff_lambda_q2", "diff_lambda_k2", "diff_head_g", "mem_norm_g",
        "w_mem_kv", "w_branch_fox", "w_branch_diff", "w_branch_mem", "w_merge_gate", "b_merge_gate", "w_out",
        "mix_post_g", "ffn2_pre_g", "ffn2_w_gate", "ffn2_w_up", "ffn2_w_down", "ffn2_post_g")


def kernel(stage=3, dbg=(), **inputs):
    bld = _get_builder(stage, dbg)
    nc = bld.nc
    x = np.asarray(inputs["x"], dtype=np.float32)
    mem = np.asarray(inputs["mem"], dtype=np.float32)
    shared = {}
    for nm in _W2D:
        a = np.asarray(inputs[nm], dtype=np.float32)[0]
        shared[nm] = np.ascontiguousarray(a.reshape(1, -1) if a.ndim == 1 else a)
    cst = [_consts(0), _consts(1)]
    in_maps = []
    for c in range(8):
        m = dict(shared)
        m.update(cst[c % 2])
        m["x"] = _shard_x(x, c)
        m["mem"] = np.ascontiguousarray(mem[c // 2])
        in_maps.append(m)
    res = run_bass_kernel_spmd(nc, in_maps, core_ids=list(range(8)))
    out = np.empty((4, 2048, D), np.float32)
    for c in range(8):
        b, p = c // 2, c % 2
        out[b].reshape(8, 2, 128, D)[:, p] = res.results[c]["out"].reshape(8, 128, D)
    if dbg:
        kernel.last_dbg = [{k: res.results[c]["dbg_" + k] for k in bld.dbg_out} for c in range(8)]
    return out
```

```python
import numpy as np
import ml_dtypes
from contextlib import ExitStack
import concourse.bass as bass
import concourse.mybir as mybir
from concourse.bass_utils import run_bass_kernel_spmd

F32 = mybir.dt.float32
BF16 = mybir.dt.bfloat16
AF = mybir.ActivationFunctionType
ALU = mybir.AluOpType
AX = mybir.AxisListType

D = 2048
DFF = 5632
T = 1024
NTB = 8
KD = 16
NF = 44
EPS = 1e-6
SAME_ENG_SYNC = True

SB_BASE = 16512


class Buf:
    __slots__ = ("name", "w", "r")

    def __init__(self, name):
        self.name = name
        self.w = {}
        self.r = {}


class DSem:
    def __init__(self, handle, key):
        self.h = handle
        self.key = key
        self.count = 0


def inherit(new_bufs, old_bufs):
    acc = {}
    for b in old_bufs:
        _merge(acc, b.w)
        _merge(acc, b.r)
    for nb in new_bufs:
        _merge(nb.w, acc)


def _merge(dst, src):
    for k, v in src.items():
        if dst.get(k, 0) < v:
            dst[k] = v


class Prog:
    ENGS = ("pe", "act", "dve", "pool", "sp")

    def __init__(self, nc, stack):
        self.nc = nc
        self.stack = stack
        self.q = {e: [] for e in self.ENGS}
        self.cnt = {e: 0 for e in self.ENGS}
        self.handles = {}
        for e in ("pe", "act", "dve", "pool"):
            self.handles[e] = stack.enter_context(nc.semaphore("s_" + e))
        self.seen = {e: {} for e in self.ENGS}
        self.ndsem = 0
        self.pe_pending = False
        self.nwaits = 0

    def dsem(self, name):
        h = self.stack.enter_context(self.nc.semaphore("d_" + name))
        key = "d_%s_%d" % (name, self.ndsem)
        self.ndsem += 1
        self.handles[key] = h
        return DSem(h, key)

    def _deps(self, eng, reads, writes, ow=()):
        deps = {}
        for b in reads:
            _merge(deps, b.w)
        for b in writes:
            _merge(deps, b.w)
            _merge(deps, b.r)
        if ow:
            d2 = {}
            for b in ow:
                _merge(d2, b.w)
                _merge(d2, b.r)
            d2.pop(eng, None)
            _merge(deps, d2)
        if eng == "pe" or not SAME_ENG_SYNC:
            deps.pop(eng, None)
        waits = []
        seen = self.seen[eng]
        for k, v in deps.items():
            if seen.get(k, 0) < v:
                seen[k] = v
                waits.append((self.handles[k], v))
                if k == "pe" and eng != "pe":
                    assert v <= self.cnt["pe"], "wait on unflagged PE op"
        return waits

    def op(self, eng, fn, reads=(), writes=(), ow=(), flag=True):
        waits = self._deps(eng, reads, writes, ow)
        if ow:
            writes = list(writes) + list(ow)
        if eng == "pe" and not flag:
            tokv = self.cnt[eng] + 1
            self.pe_pending = True
        else:
            self.cnt[eng] += 1
            tokv = self.cnt[eng]
            if eng == "pe":
                self.pe_pending = False
        sem = self.handles[eng]
        self.nwaits += len(waits)

        def run(e, waits=waits, fn=fn, flag=flag, sem=sem):
            for h, v in waits:
                e.wait_ge(h, v)
            ins = fn(e)
            if flag:
                ins.then_inc(sem, 1)

        self.q[eng].append(run)
        for b in reads:
            if b.r.get(eng, 0) < tokv:
                b.r[eng] = tokv
        for b in writes:
            if b.w.get(eng, 0) < tokv:
                b.w[eng] = tokv
            b.r = {}

    def dma(self, eng, out, in_, ds, reads=(), writes=(), **kw):
        waits = self._deps(eng, reads, writes)
        ds.count += 16
        tokv = ds.count
        self.nwaits += len(waits)

        def run(e, waits=waits, out=out, in_=in_, h=ds.h, kw=kw):
            for hh, v in waits:
                e.wait_ge(hh, v)
            e.dma_start(out=out, in_=in_, **kw).then_inc(h, 16)

        self.q[eng].append(run)
        for b in reads:
            if b.r.get(ds.key, 0) < tokv:
                b.r[ds.key] = tokv
        for b in writes:
            if b.w.get(ds.key, 0) < tokv:
                b.w[ds.key] = tokv
            b.r = {}

    def aop(self, eng, fn, ds, inc, reads=(), writes=()):
        waits = self._deps(eng, reads, writes)
        ds.count += inc
        tokv = ds.count

        def run(e, waits=waits, fn=fn, h=ds.h, inc=inc):
            for hh, v in waits:
                e.wait_ge(hh, v)
            fn(e).then_inc(h, inc)

        self.q[eng].append(run)
        for b in reads:
            if b.r.get(ds.key, 0) < tokv:
                b.r[ds.key] = tokv
        for b in writes:
            if b.w.get(ds.key, 0) < tokv:
                b.w[ds.key] = tokv
            b.r = {}

    def join(self, ds, bufs):
        for b in bufs:
            for dd in (b.r, b.w):
                if ds.key in dd:
                    dd[ds.key] = ds.count

    def raw(self, eng, fn, reads=(), writes=()):
        waits = self._deps(eng, reads, writes)

        def run(e, waits=waits, fn=fn):
            for h, v in waits:
                e.wait_ge(h, v)
            if fn is not None:
                fn(e)

        self.q[eng].append(run)

    def emit_all(self):
        nc = self.nc
        q = self.q
        with nc.Block() as block:
            @block.tensor
            def _(e):
                for f in q["pe"]:
                    f(e)

            @block.scalar
            def _(e):
                for f in q["act"]:
                    f(e)

            @block.vector
            def _(e):
                for f in q["dve"]:
                    f(e)

            @block.gpsimd
            def _(e):
                for f in q["pool"]:
                    f(e)

            @block.sync
            def _(e):
                for f in q["sp"]:
                    f(e)


class Slot:
    def __init__(self, P, name, off, nbytes):
        self.buf = Buf(name)
        self.ds = P.dsem(name)
        self.off = off
        self.nbytes = nbytes
        self.name = name


class WStream:
    def __init__(self, P):
        self.P = P
        self.steps = []

    def add(self, slot, dma_fn, compute_fn):
        self.steps.append((slot, dma_fn, compute_fn))

    def run(self):
        steps = self.steps
        n = len(steps)
        prev_user = [-1] * n
        last = {}
        for i, (slot, _, _) in enumerate(steps):
            if slot is not None:
                prev_user[i] = last.get(id(slot), -1)
                last[id(slot)] = i
        nd = 0
        for i in range(n):
            while nd < n and prev_user[nd] < i:
                slot, dma_fn, _ = steps[nd]
                if dma_fn is not None:
                    dma_fn()
                nd += 1
            assert nd > i
            steps[i][2]()
        self.steps = []


C_FQ, C_FK, C_FV, C_FF, C_DQ, C_DK, C_DV, C_MQ = 0, 1024, 2048, 3072, 3080, 3592, 4104, 4616
D_IN = 5128
LAM_INIT = 0.8 - 0.6
RG = [[0, 1], [2, 3], [4, 5], [6, 7]]


class Builder:
    def __init__(self, stage, dbg=()):
        self.stage = stage
        self.dbg = dbg
        self.nc = bass.Bass("TRN2", target_bir_lowering=False)
        self.stack = ExitStack()
        self.P = Prog(self.nc, self.stack)
        self.nsb = 0
        self.dbg_out = {}

    def sb(self, name, shape, dtype, off):
        self.nsb += 1
        h = self.nc.alloc_sbuf_tensor_at("%s_%d" % (name, self.nsb), list(shape), dtype, offset=SB_BASE + off)
        return h.ap()

    def din(self, name, shape, dtype=F32):
        return self.nc.dram_tensor(name, list(shape), dtype, kind="ExternalInput").ap()

    def dscr(self, name, shape, dtype):
        return self.nc.dram_tensor(name, list(shape), dtype).ap()

    def dump(self, name, ap, bufs, shape, dtype):
        if name not in self.dbg:
            return
        o = self.nc.dram_tensor("dbg_" + name, list(shape), dtype, kind="ExternalOutput").ap()
        self.dbg_out[name] = (list(shape), dtype)
        self.P.dma("sp", o, ap, self.dsout, reads=bufs)

    def build(self):
        nc = self.nc
        P = self.P
        self.x_in = self.din("x", [T, D])
        self.mem_in = self.din("mem", [256, D])
        self.out = nc.dram_tensor("out", [T, D], F32, kind="ExternalOutput").ap()
        self.w = {}
        for nm, shp in (("ffn1_pre_g", [1, D]), ("ffn1_w_gate", [D, DFF]), ("ffn1_w_up", [D, DFF]),
                        ("ffn1_w_down", [DFF, D]), ("ffn1_post_g", [1, D]),
                        ("mix_pre_g", [1, D]), ("w_in", [D, D_IN]), ("fox_f_bias", [1, 8]),
                        ("diff_lambda_q1", [1, 64]), ("diff_lambda_k1", [1, 64]),
                        ("diff_lambda_q2", [1, 64]), ("diff_lambda_k2", [1, 64]),
                        ("diff_head_g", [1, 128]), ("mem_norm_g", [1, D]), ("w_mem_kv", [D, 1024]),
                        ("w_branch_fox", [1024, D]), ("w_branch_diff", [512, D]), ("w_branch_mem", [512, D]),
                        ("w_merge_gate", [D, 3 * D]), ("b_merge_gate", [1, 3 * D]), ("w_out", [D, D]),
                        ("mix_post_g", [1, D]),
                        ("ffn2_pre_g", [1, D]), ("ffn2_w_gate", [D, DFF]), ("ffn2_w_up", [D, DFF]),
                        ("ffn2_w_down", [DFF, D]), ("ffn2_post_g", [1, D])):
            self.w[nm] = self.din(nm, shp)
        self.ident_in = self.din("c_ident", [128, 128], BF16)
        self.cmat_in = self.din("c_mat", [128, 3, 128], F32)
        self.masks_in = self.din("c_masks", [128, 4, 128], BF16)
        self.rope_in = self.din("c_rope", [128, 2, NTB, 8], F32)
        self.psel_in = self.din("c_psel", [128, 2], F32)
        self.xsp1 = self.dscr("xsp1", [T, D], F32)
        self.xsp2 = self.dscr("xsp2", [T, D], F32)
        self.XKs = self.dscr("XKs", [1024, 1024], BF16); self.XKd = self.dscr("XKd", [2048, 1024], BF16)
        self.XVs = self.dscr("XVs", [128, 8192], BF16); self.XVd = self.dscr("XVd", [256, 8192], BF16)
        self.XDKs = self.dscr("XDKs", [512, 1024], BF16); self.XDKd = self.dscr("XDKd", [1024, 1024], BF16)
        self.XDVs = self.dscr("XDVs", [128, 4096], BF16); self.XDVd = self.dscr("XDVd", [256, 4096], BF16)
        self.XLs = self.dscr("XLs", [128, 64], F32); self.XLd = self.dscr("XLd", [256, 64], F32)

        o = 0
        self.offA = o
        self.A = self.sb("A", [128, NTB, D], F32, o); o += NTB * D * 4
        self.hT = self.sb("hT", [128, KD, T], BF16, o); o += KD * T * 2
        self.offB = o
        self.actT = self.sb("actT", [128, 22, T], BF16, o); o += 22 * T * 2
        self.offWA = o
        o += 2 * 16384
        self.offWB = o
        o += 4 * 2048
        self.GB = self.sb("GB", [128, D], F32, o); o += D * 4
        self.offSIL = o
        self.SIL = [self.sb("SIL%d" % i, [128, T], F32, o + i * 4096) for i in range(2)]; o += 8192
        self.ident = self.sb("ident", [128, 128], BF16, o); o += 256
        self.stat = self.sb("stat", [128, 64], F32, o); o += 256
        self.cst = self.sb("cst", [128, 8], F32, o); o += 32
        self.offX = o
        self.biasT = self.sb("biasT", [128, 16, 8, 8], F32, o); o += 4096
        self.masks = self.sb("masks", [128, 4, 128], BF16, o); o += 1024
        self.rope = self.sb("rope", [128, 2, NTB, 8], F32, o); o += 512
        self.onesb = self.sb("onesb", [128, 128], BF16, o); o += 256
        self.bmT = self.sb("bmT", [128, 48], F32, o); o += 192
        self.psel = self.sb("psel", [128, 2], F32, o); o += 32
        self.lamv = self.sb("lamv", [128, 8], F32, o); o += 32
        self.fbias = self.sb("fbias", [128, 8], F32, o); o += 32
        self.Dtab = self.sb("Dtab", [128, 128], F32, o); o += 512
        self.Dend = self.sb("Dend", [128, 64], F32, o); o += 256
        self.wff = self.sb("wff", [128, KD, 8], BF16, o); o += 256
        assert SB_BASE + o <= 229344, o
        self.hb = [self.sb("hb%d" % i, [128, D], BF16, self.offB + i * 4096) for i in range(2)]
        self.xr = [self.sb("xr%d" % i, [128, D], F32, self.offB + 8192 + i * 8192) for i in range(2)]
        self.junk = self.sb("junk", [128, D], BF16, self.offB + 8192 + 16384)

        self.ps = nc.alloc_psum_tensor("ps", [128, 8, 512], F32).ap()
        self.bank = [Buf("bank%d" % i) for i in range(8)]

        self.bA = [Buf("A%d" % i) for i in range(NTB)]
        self.bhT = Buf("hT")
        self.bact = [Buf("act%d" % i) for i in range(22)]
        self.rB = self.bact
        self.bGB = Buf("GB")
        self.bSIL = [Buf("SIL0"), Buf("SIL1")]
        self.bhb = [Buf("hb0"), Buf("hb1")]
        self.bxr = [Buf("xr0"), Buf("xr1")]
        self.bjunk = Buf("junk")
        self.bstat = [Buf("stat%d" % i) for i in range(64)]
        self.bconst = Buf("const")
        self.dsmisc = P.dsem("misc")
        self.dsx = P.dsem("x")
        self.dsxr = [P.dsem("xr0"), P.dsem("xr1")]
        self.dsgb = P.dsem("gb")
        self.dsgb2 = P.dsem("gb2")
        self.dsout = P.dsem("out")
        self.ringA = [Slot(P, "ra%d" % i, self.offWA + i * 16384, 16384) for i in range(2)]
        self.ringB = [Slot(P, "rb%d" % i, self.offWB + i * 2048, 2048) for i in range(4)]
        self.ra_i = 0
        self.rb_i = 0
        self.ws = WStream(P)
        self.chunk_ctr = 0

        P.dma("sp", self.ident, self.ident_in, self.dsmisc, writes=[self.bconst])
        P.op("dve", lambda e: e.memset(self.cst[:, 0:1], EPS), writes=[self.bconst])
        P.op("dve", lambda e: e.memset(self.cst[:, 1:2], 1.0), writes=[self.bconst])
        P.op("dve", lambda e: e.memset(self.cst[:, 2:3], 0.0), writes=[self.bconst])
        P.op("dve", lambda e: e.memset(self.cst[:, 3:4], 4.0 * EPS), writes=[self.bconst])
        P.op("dve", lambda e: e.memset(self.onesb, 1.0), writes=[self.bconst])
        self.epsb = self.cst[:, 0:1]
        self.oneb = self.cst[:, 1:2]
        if self.stage >= 2:
            P.dma("sp", self.masks, self.masks_in, self.dsmisc, writes=[self.bconst])
            P.dma("sp", self.rope, self.rope_in, self.dsmisc, writes=[self.bconst])
            P.dma("sp", self.psel, self.psel_in, self.dsmisc, writes=[self.bconst])
            P.dma("sp", self.fbias, self.w["fox_f_bias"][0:1, :].partition_broadcast(128)[:, 0, :], self.dsmisc,
                  writes=[self.bconst])
            P.dma("sp", self.bmT, self.w["b_merge_gate"].rearrange("o (c p) -> p (o c)", p=128), self.dsmisc,
                  writes=[self.bconst], allow_slow_non_contiguous=True)
            P.dma("sp", self.lamv[:, 6:7], self.w["diff_head_g"].rearrange("o p -> p o"), self.dsmisc,
                  writes=[self.bconst], allow_slow_non_contiguous=True)
            self.dswff = P.dsem("wff")
            self.bwff = Buf("wff")
            P.dma("pool", self.wff, self.w["w_in"].rearrange("(k p) n -> p k n", p=128)[:, :, C_FF:C_FF + 8],
                  self.dswff, writes=[self.bwff], allow_slow_non_contiguous=True)
        P.join(self.dsmisc, [self.bconst])

        dsxs = [P.dsem("x%d" % tb) for tb in range(NTB)]
        for tb in range(NTB):
            P.dma("sp", self.A[:, tb, :], self.x_in[tb * 128:(tb + 1) * 128, :], dsxs[tb], writes=[self.bA[tb]])
        self.prenorm_A("ffn1_pre_g")
        self.ffn_core("ffn1_")
        if self.stage >= 2:
            self.trans("ffn1_post_g", 0.5, self.x_in, None, "mix_pre_g")
            bsp1 = self.spill(self.xsp1, "sp1")
            self.mixer()
            if self.stage >= 3:
                self.trans("mix_post_g", 1.0, self.xsp1, bsp1, "ffn2_pre_g")
            else:
                self.postnorm("mix_post_g", 1.0, self.xsp1, bsp1, final=False)
        else:
            self.postnorm("ffn1_post_g", 0.5, self.x_in, None, final=False)
        if self.stage >= 3:
            bsp = self.spill(self.xsp2, "sp2")
            self.ffn_core("ffn2_")
            self.postnorm("ffn2_post_g", 0.5, self.xsp2, bsp, final=True)
        else:
            for tb in range(NTB):
                P.dma("sp", self.out[tb * 128:(tb + 1) * 128, :], self.A[:, tb, :], self.dsout, reads=[self.bA[tb]])
        dso = self.dsout
        P.q["sp"].append(lambda e: e.wait_ge(dso.h, dso.count))
        P.emit_all()
        return nc

    def next_ra(self):
        s = self.ringA[self.ra_i % 2]
        self.ra_i += 1
        return s

    def next_rb(self):
        s = self.ringB[self.rb_i % 4]
        self.rb_i += 1
        return s

    def scratch_bufs(self):
        return self.bhb + self.bxr + [self.bjunk]

    def begin_scratch(self):
        inherit(self.scratch_bufs(), list(self.rB))

    def end_scratch(self):
        inherit(list(self.rB), self.scratch_bufs())

    def set_rB(self, new_bufs):
        inherit(new_bufs, list(self.rB) + self.bhb + self.bxr + [self.bjunk])
        self.rB = new_bufs

    def spill(self, dst, name):
        P = self.P
        ds = P.dsem(name)
        bsp = Buf(name)
        for tb in range(NTB):
            P.dma("sp", dst[tb * 128:(tb + 1) * 128, :], self.A[:, tb, :], ds, reads=[self.bA[tb]], writes=[bsp])
        P.join(ds, self.bA + [bsp])
        return bsp

    def load_gain(self, gname):
        src = self.w[gname][0:1, :].partition_broadcast(128)
        self.P.dma("sp", self.GB, src[:, 0, :], self.dsgb, writes=[self.bGB])

    def rstd_col(self, col, dim, coef=1.0):
        P = self.P
        st = self.stat[:, col:col + 1]
        bias = self.epsb if coef == 1.0 else self.cst[:, 3:4]
        assert coef in (1.0, 0.5)
        P.op("act", lambda e: e.activation(out=st, in_=st, func=AF.Sqrt, bias=bias, scale=1.0 / (dim * coef * coef)),
             reads=[self.bconst, self.bstat[col]], writes=[self.bstat[col]])
        P.op("dve", lambda e: e.reciprocal(out=st, in_=st), reads=[self.bstat[col]], writes=[self.bstat[col]])

    def prenorm(self, nblk, src_fn, gname, dst_fn, dstbuf):
        P = self.P
        self.begin_scratch()
        self.load_gain(gname)
        P.op("dve", lambda e: e.memset(self.stat[:, 0:nblk], 0.0), writes=self.bstat[0:nblk])

        def stage_a(tb):
            sap, sbuf = src_fn(tb)
            P.op("act", lambda e: e.activation(out=self.junk, in_=sap, func=AF.Square,
                                               accum_out=self.stat[:, tb:tb + 1]),
                 reads=[sbuf], writes=[self.bstat[tb], self.bjunk])
            self.rstd_col(tb, D)
            hb = self.hb[tb % 2]
            bhb = self.bhb[tb % 2]
            P.op("dve", lambda e: e.scalar_tensor_tensor(
                out=hb, in0=sap, scalar=self.stat[:, tb:tb + 1], in1=self.GB,
                op0=ALU.mult, op1=ALU.mult),
                reads=[sbuf, self.bstat[tb], self.bGB], ow=[bhb])

        def stage_b(tb):
            hb = self.hb[tb % 2]
            bhb = self.bhb[tb % 2]
            for half in range(2):
                bk = (2 * tb + half) % 8
                pst = self.ps[:, bk, :].bitcast(BF16)
                for kk in range(8):
                    k = half * 8 + kk
                    P.op("pe", lambda e, pst=pst, kk=kk, k=k: e.transpose(
                        out=pst[:, kk * 128:(kk + 1) * 128], in_=hb[:, k * 128:(k + 1) * 128], identity=self.ident),
                        reads=[bhb, self.bconst], writes=[self.bank[bk]], flag=(kk == 7))
                dst = dst_fn(tb, half)
                src = pst.rearrange("p (k n) -> p k n", k=8)
                if half == 0:
                    P.op("act", lambda e, dst=dst, src=src: e.copy(out=dst, in_=src),
                         reads=[self.bank[bk]], ow=[dstbuf])
                else:
                    P.op("dve", lambda e, dst=dst, src=src: e.tensor_copy(out=dst, in_=src),
                         reads=[self.bank[bk]], ow=[dstbuf])

        stage_a(0)
        for tb in range(nblk):
            if tb + 1 < nblk:
                stage_a(tb + 1)
            stage_b(tb)
        self.end_scratch()

    def prenorm_A(self, gname):
        self.prenorm(NTB, lambda tb: (self.A[:, tb, :], self.bA[tb]), gname,
                     lambda tb, half: self.hT[:, half * 8:(half + 1) * 8, tb * 128:(tb + 1) * 128], self.bhT)

    def postnorm(self, gname, coef, src, bsrc, final):
        P = self.P
        self.begin_scratch()
        self.load_gain(gname)
        P.op("dve", lambda e: e.memset(self.stat[:, 8:16], 0.0), writes=self.bstat[8:16])
        for tb in range(NTB):
            c = 8 + tb
            xr = self.xr[tb % 2]
            bxr = self.bxr[tb % 2]
            P.dma("sp", xr, src[tb * 128:(tb + 1) * 128, :], self.dsxr[tb % 2],
                  reads=([bsrc] if bsrc is not None else []), writes=[bxr])
            P.op("act", lambda e, tb=tb, c=c: e.activation(out=self.junk, in_=self.A[:, tb, :], func=AF.Square,
                                                        accum_out=self.stat[:, c:c + 1]),
                 reads=[self.bA[tb]], writes=[self.bstat[c], self.bjunk])
            self.rstd_col(c, D, coef)
            P.op("dve", lambda e, tb=tb: e.tensor_tensor(out=self.A[:, tb, :], in0=self.A[:, tb, :], in1=self.GB,
                                                          op=ALU.mult),
                 reads=[self.bGB, self.bA[tb]], writes=[self.bA[tb]])
            P.op("dve", lambda e, tb=tb, xr=xr, c=c: e.scalar_tensor_tensor(
                out=self.A[:, tb, :], in0=self.A[:, tb, :], scalar=self.stat[:, c:c + 1], in1=xr,
                op0=ALU.mult, op1=ALU.add),
                reads=[bxr, self.bstat[c], self.bA[tb]], writes=[self.bA[tb]])
            if final:
                P.dma("sp", self.out[tb * 128:(tb + 1) * 128, :], self.A[:, tb, :], self.dsout, reads=[self.bA[tb]])
        self.end_scratch()

    def trans(self, g_post, coef, src, bsrc, g_pre):
        P = self.P
        GB2 = self.sb("GB2", [128, D], F32, self.offSIL)
        bGB2 = Buf("GB2")
        inherit([bGB2], self.bSIL)
        self.begin_scratch()
        self.load_gain(g_post)
        P.dma("sp", GB2, self.w[g_pre][0:1, :].partition_broadcast(128)[:, 0, :], self.dsgb2, writes=[bGB2])
        P.op("dve", lambda e: e.memset(self.stat[:, 0:16], 0.0), writes=self.bstat[0:16])

        def stage_p(tb):
            c = 8 + tb
            xr = self.xr[tb % 2]
            bxr = self.bxr[tb % 2]
            P.dma("sp", xr, src[tb * 128:(tb + 1) * 128, :], self.dsxr[tb % 2],
                  reads=([bsrc] if bsrc is not None else []), writes=[bxr])
            P.op("act", lambda e: e.activation(out=self.junk, in_=self.A[:, tb, :], func=AF.Square,
                                               accum_out=self.stat[:, c:c + 1]),
                 reads=[self.bA[tb]], writes=[self.bstat[c], self.bjunk])
            self.rstd_col(c, D, coef)
            P.op("dve", lambda e: e.tensor_tensor(out=self.A[:, tb, :], in0=self.A[:, tb, :], in1=self.GB, op=ALU.mult),
                 reads=[self.bGB, self.bA[tb]], writes=[self.bA[tb]])
            P.op("dve", lambda e: e.scalar_tensor_tensor(
                out=self.A[:, tb, :], in0=self.A[:, tb, :], scalar=self.stat[:, c:c + 1], in1=xr,
                op0=ALU.mult, op1=ALU.add),
                reads=[bxr, self.bstat[c], self.bA[tb]], writes=[self.bA[tb]])

        def stage_a(tb):
            P.op("act", lambda e: e.activation(out=self.junk, in_=self.A[:, tb, :], func=AF.Square,
                                               accum_out=self.stat[:, tb:tb + 1]),
                 reads=[self.bA[tb]], writes=[self.bstat[tb], self.bjunk])
            self.rstd_col(tb, D)
            hb = self.hb[tb % 2]
            bhb = self.bhb[tb % 2]
            P.op("dve", lambda e: e.scalar_tensor_tensor(
                out=hb, in0=self.A[:, tb, :], scalar=self.stat[:, tb:tb + 1], in1=GB2,
                op0=ALU.mult, op1=ALU.mult),
                reads=[self.bA[tb], self.bstat[tb], bGB2], ow=[bhb])

        def stage_b(tb):
            hb = self.hb[tb % 2]
            bhb = self.bhb[tb % 2]
            for half in range(2):
                bk = (2 * tb + half) % 8
                pst = self.ps[:, bk, :].bitcast(BF16)
                for kk in range(8):
                    k = half * 8 + kk
                    P.op("pe", lambda e, pst=pst, kk=kk, k=k: e.transpose(
                        out=pst[:, kk * 128:(kk + 1) * 128], in_=hb[:, k * 128:(k + 1) * 128], identity=self.ident),
                        reads=[bhb, self.bconst], writes=[self.bank[bk]], flag=(kk == 7))
                dst = self.hT[:, half * 8:(half + 1) * 8, tb * 128:(tb + 1) * 128]
                src_ = pst.rearrange("p (k n) -> p k n", k=8)
                P.op("act", lambda e, dst=dst, src_=src_: e.copy(out=dst, in_=src_),
                     reads=[self.bank[bk]], ow=[self.bhT])

        for t in range(NTB + 2):
            if t < NTB:
                stage_p(t)
            if 1 <= t <= NTB:
                stage_a(t - 1)
            if t >= 2:
                stage_b(t - 2)
        inherit(self.bSIL, [bGB2])
        self.end_scratch()

    def tm_steps(self, wview, c0, lhs_fn, nk, evac_fn, kbase=0):
        P = self.P
        assert nk % 2 == 0
        for kb in range(nk // 2):
            slot = self.next_rb()
            tile = self.sb("wtm", [128, 2, 512], BF16, slot.off)
            kg = kbase + kb * 2

            def dma_fn(slot=slot, tile=tile, kg=kg):
                P.dma("pool", tile, wview[:, kg:kg + 2, c0:c0 + 512], slot.ds, writes=[slot.buf])

            def comp_fn(slot=slot, tile=tile, kb=kb):
                for kk in range(2):
                    kl = kb * 2 + kk
                    for tb in range(NTB):
                        lap, lbuf = lhs_fn(kl, tb)
                        last = (kl == nk - 1)
                        P.op("pe", lambda e, kk=kk, kl=kl, tb=tb, lap=lap, last=last: e.matmul(
                            self.ps[:, tb, :], lhsT=lap, rhs=tile[:, kk, :], start=(kl == 0), stop=last),
                            reads=[slot.buf, lbuf], writes=[self.bank[tb]], flag=last or (kk == 1 and tb == NTB - 1))
                if kb == nk // 2 - 1:
                    for tb in range(NTB):
                        evac_fn(tb)

            self.ws.add(slot, dma_fn, comp_fn)

    def fm_steps(self, wview, c0, nch, rhs, rhs_buf, evac_fn, nk=KD, ntg=2, wtile_cols=None):
        P = self.P
        slot = self.next_ra()
        tile = self.sb("wfm", [128, nk, nch * 128], BF16, slot.off)

        def dma_fn(slot=slot, tile=tile):
            P.dma("pool", tile, wview[:, 0:nk, c0:c0 + nch * 128], slot.ds, writes=[slot.buf])

        def comp_fn(slot=slot, tile=tile):
            for c in range(nch):
                pair = (self.chunk_ctr % 4) * 2
                self.chunk_ctr += 1
                for k in range(nk):
                    for tg in range(ntg):
                        bk = pair + tg
                        last = (k == nk - 1 and tg == ntg - 1)
                        ncols = 512 if ntg == 2 else rhs.shape[2]
                        P.op("pe", lambda e, bk=bk, k=k, tg=tg, c=c, ncols=ncols: e.matmul(
                            self.ps[:, bk, 0:ncols],
                            lhsT=tile[:, k, c * 128:(c + 1) * 128],
                            rhs=rhs[:, k, tg * 512:(tg + 1) * 512] if ntg == 2 else rhs[:, k, :],
                            start=(k == 0), stop=(k == nk - 1)),
                            reads=[slot.buf, rhs_buf], writes=[self.bank[bk]], flag=last)
                evac_fn(c, pair)

        self.ws.add(slot, dma_fn, comp_fn)

    def ffn_core(self, pre):
        P = self.P
        wgv = self.w[pre + "w_gate"].rearrange("(k p) n -> p k n", p=128)
        wuv = self.w[pre + "w_up"].rearrange("(k p) n -> p k n", p=128)
        wdv = self.w[pre + "w_down"].rearrange("(k p) n -> p k n", p=128)
        ws = self.ws
        ctr = [0]
        for half in range(2):
            for fg in range(11):
                slot = self.next_ra()
                tile = self.sb("wgu", [128, 2, KD, 256], BF16, slot.off)
                c0 = (half * 22 + fg * 2) * 128

                def dma_fn(slot=slot, tile=tile, c0=c0):
                    P.dma("pool", tile[:, 0], wgv[:, :, c0:c0 + 256], slot.ds, writes=[slot.buf])
                    P.dma("pool", tile[:, 1], wuv[:, :, c0:c0 + 256], slot.ds, writes=[slot.buf])

                def comp_fn(slot=slot, tile=tile, fg=fg):
                    for fi in range(2):
                        fl = fg * 2 + fi
                        par = ctr[0] % 2
                        ctr[0] += 1
                        b0 = par * 4
                        for gu in range(2):
                            for k in range(KD):
                                for tg in range(2):
                                    bk = b0 + gu * 2 + tg
                                    last = (k == KD - 1 and tg == 1)
                                    P.op("pe", lambda e, bk=bk, gu=gu, k=k, tg=tg, fi=fi: e.matmul(
                                        self.ps[:, bk, :], lhsT=tile[:, gu, k, fi * 128:(fi + 1) * 128],
                                        rhs=self.hT[:, k, tg * 512:(tg + 1) * 512],
                                        start=(k == 0), stop=(k == KD - 1)),
                                        reads=[slot.buf, self.bhT], writes=[self.bank[bk]], flag=last)
                        sil = self.SIL[par]
                        bsil = self.bSIL[par]
                        pg = self.ps[:, b0:b0 + 2, :].rearrange("p a n -> p (a n)")
                        pu = self.ps[:, b0 + 2:b0 + 4, :].rearrange("p a n -> p (a n)")
                        P.op("act", lambda e, sil=sil, pg=pg: e.activation(out=sil, in_=pg, func=AF.Silu),
                             reads=[self.bank[b0], self.bank[b0 + 1]], ow=[bsil])
                        P.op("dve", lambda e, sil=sil, pu=pu, fl=fl: e.tensor_tensor(
                            out=self.actT[:, fl, :], in0=pu, in1=sil, op=ALU.mult),
                            reads=[bsil, self.bank[b0 + 2], self.bank[b0 + 3]], ow=[self.bact[fl]])

                ws.add(slot, dma_fn, comp_fn)
            for ng in range(4):
                def evac_fn(tb, ng=ng, half=half):
                    ydst = self.A[:, tb, ng * 512:(ng + 1) * 512]
                    if half == 0:
                        if tb % 2 == 0:
                            P.op("act", lambda e: e.copy(out=ydst, in_=self.ps[:, tb, :]),
                                 reads=[self.bank[tb]], ow=[self.bA[tb]])
                        else:
                            P.op("dve", lambda e: e.tensor_copy(out=ydst, in_=self.ps[:, tb, :]),
                                 reads=[self.bank[tb]], ow=[self.bA[tb]])
                    else:
                        P.op("dve", lambda e: e.tensor_tensor(out=ydst, in0=self.ps[:, tb, :], in1=ydst, op=ALU.add),
                             reads=[self.bank[tb], self.bA[tb]], writes=[self.bA[tb]])

                self.tm_steps(wdv, ng * 512, lambda kl, tb: (self.actT[:, kl, tb * 128:(tb + 1) * 128], self.bact[kl]),
                              22, evac_fn, kbase=half * 22)
        ws.run()

    def mixer(self):
        P = self.P
        nc = self.nc
        A0 = self.offA
        B0 = self.offB

        memx = self.sb("memx", [128, 2, D], F32, A0)
        fkst = [self.sb("fkst%d" % i, [128, T], BF16, A0 + 16384 + i * 2048) for i in range(2)]
        fvst = self.sb("fvst", [128, 8, NTB, 128], BF16, A0 + 20480)
        dkTst = self.sb("dkTst", [128, 4, T], BF16, A0 + 36864)
        dvst = self.sb("dvst", [128, 4, NTB, 128], BF16, A0 + 45056)
        ropeb = [self.sb("ropeb%d" % i, [128, 512], BF16, A0 + 53248 + i * 1024) for i in range(2)]
        ropet = self.sb("ropet", [128, 4, 8, 8], F32, A0 + 55296)
        spst = self.sb("spst", [128, 64], F32, A0 + 56320)
        b_memx = [Buf("memx0"), Buf("memx1")]
        b_fkst = [Buf("fkst0"), Buf("fkst1")]
        b_fvst = Buf("fvst"); b_dkTst = Buf("dkTst"); b_dvst = Buf("dvst")
        b_ropeb = [Buf("ropeb0"), Buf("ropeb1")]
        b_ropet = Buf("ropet"); b_spst = Buf("spst")
        projA = b_memx + b_fkst + [b_fvst, b_dkTst, b_dvst] + b_ropeb + [b_ropet, b_spst]
        inherit(projA, self.bA)
        ds_mem = P.dsem("mem")
        ds_fk = [P.dsem("fk0"), P.dsem("fk1")]
        ds_fv = P.dsem("fv"); ds_dk = P.dsem("dk"); ds_dv = P.dsem("dv"); ds_sp = P.dsem("spx")

        for blk in range(2):
            P.dma("sp", memx[:, blk, :], self.mem_in[blk * 128:(blk + 1) * 128, :], ds_mem, writes=[b_memx[blk]])
        P.join(ds_mem, b_memx)
        mhT = self.sb("mhT", [128, KD, 256], BF16, B0 + 36864)
        b_mhT = Buf("mhT")
        inherit([b_mhT], list(self.rB))
        self.prenorm(2, lambda tb: (memx[:, tb, :], b_memx[tb]), "mem_norm_g",
                     lambda tb, half: mhT[:, half * 8:(half + 1) * 8, tb * 128:(tb + 1) * 128], b_mhT)
        fqT = self.sb("fqT", [128, 8, T], BF16, B0)
        dqT = self.sb("dqT", [128, 4, T], BF16, B0 + 16384)
        mqT = self.sb("mqT", [128, 4, T], BF16, B0 + 24576)
        mkT = self.sb("mkT", [128, 4, 256], BF16, B0 + 32768)
        mv = self.sb("mv", [128, 2, 512], BF16, B0 + 34816)
        Y1 = self.sb("Y1", [128, T], F32, B0 + 36864)
        Y2 = self.sb("Y2", [128, T], F32, B0 + 40960)
        b_fqT = [Buf("fqT%d" % i) for i in range(8)]
        b_dqT = Buf("dqT"); b_mqT = [Buf("mqT%d" % i) for i in range(4)]
        b_mkT = Buf("mkT"); b_mv = Buf("mv"); b_Y1 = Buf("Y1"); b_Y2 = Buf("Y2")
        mixB = b_fqT + [b_dqT] + b_mqT + [b_mkT, b_mv, b_mhT]
        self.set_rB(mixB)
        inherit([b_Y1, b_Y2], [b_mhT])
        self.rB = mixB + [b_Y1, b_Y2]

        wkv = self.w["w_mem_kv"].rearrange("(k p) n -> p k n", p=128)
        win = self.w["w_in"].rearrange("(k p) n -> p k n", p=128)

        b_XKs = Buf("XKs"); b_XVs = Buf("XVs"); b_XDKs = Buf("XDKs"); b_XDVs = Buf("XDVs"); b_XLs = Buf("XLs")
        b_XKd = Buf("XKd"); b_XVd = Buf("XVd"); b_XDKd = Buf("XDKd"); b_XDVd = Buf("XDVd"); b_XLd = Buf("XLd")
        ds_cc = [P.dsem("cc%d" % i) for i in range(5)]

        def gather(i, src, bsrc, dst, bdst):
            def step():
                P.aop("pool", lambda e: e.collective_compute(
                    "AllGather", ALU.bypass, replica_groups=RG, ins=[src], outs=[dst]),
                    ds_cc[i], 1, reads=[bsrc], writes=[bdst])
            self.ws.add(None, None, step)

        for grp in range(2):
            def evac_fk(c, pair, grp=grp):
                h = grp * 4 + c
                st = fkst[h % 2]; bst = b_fkst[h % 2]
                src = self.ps[:, pair:pair + 2, :].rearrange("p a n -> p (a n)")
                if h % 2 == 0:
                    P.op("act", lambda e: e.copy(out=st, in_=src), reads=[self.bank[pair], self.bank[pair + 1]], ow=[bst])
                else:
                    P.op("dve", lambda e: e.tensor_copy(out=st, in_=src),
                         reads=[self.bank[pair], self.bank[pair + 1]], ow=[bst])
                P.dma("sp", self.XKs[h * 128:(h + 1) * 128, :], st, ds_fk[h % 2], reads=[bst], writes=[b_XKs])
            self.fm_steps(win, C_FK + grp * 512, 4, self.hT, self.bhT, evac_fk)
        gather(0, self.XKs, b_XKs, self.XKd, b_XKd)

        lhs_h = lambda kl, tb: (self.hT[:, kl, tb * 128:(tb + 1) * 128], self.bhT)
        for grp in range(2):
            def evac_fv(tb, grp=grp):
                dst = fvst[:, grp * 4:(grp + 1) * 4, tb, :]
                src = self.ps[:, tb, :].rearrange("p (h d) -> p h d", h=4)
                if tb % 2 == 0:
                    P.op("act", lambda e: e.copy(out=dst, in_=src), reads=[self.bank[tb]], ow=[b_fvst])
                else:
                    P.op("dve", lambda e: e.tensor_copy(out=dst, in_=src), reads=[self.bank[tb]], ow=[b_fvst])
            self.tm_steps(win, C_FV + grp * 512, lhs_h, KD, evac_fv)

        def st_fv():
            P.dma("sp", self.XVs, fvst.rearrange("p h t d -> p (h t d)"), ds_fv, reads=[b_fvst], writes=[b_XVs])
        self.ws.add(None, None, st_fv)
        gather(1, self.XVs, b_XVs, self.XVd, b_XVd)

        def rope_evac(tb, dstT, bdstT, scale):
            rb = ropeb[tb % 2]; brb = b_ropeb[tb % 2]
            pv = self.ps[:, tb, :].rearrange("p (g c) -> p g c", c=64)
            rbv = rb.rearrange("p (g c) -> p g c", c=64)
            cos = self.rope[:, 0, tb, :].unsqueeze(1).broadcast_to([128, 8, 8])
            sin = self.rope[:, 1, tb, :].unsqueeze(1).broadcast_to([128, 8, 8])
            x1 = pv[:, :, 0:8]; x2 = pv[:, :, 8:16]
            t = [ropet[:, i] for i in range(4)]
            rd = [self.bconst, self.bank[tb]]
            P.op("dve", lambda e: e.tensor_tensor(out=t[0], in0=x1, in1=cos, op=ALU.mult), reads=rd, writes=[b_ropet])
            P.op("dve", lambda e: e.tensor_tensor(out=t[1], in0=x2, in1=sin, op=ALU.mult), reads=rd, writes=[b_ropet])
            P.op("dve", lambda e: e.tensor_tensor(out=t[2], in0=x2, in1=cos, op=ALU.mult), reads=rd, writes=[b_ropet])
            P.op("dve", lambda e: e.tensor_tensor(out=t[3], in0=x1, in1=sin, op=ALU.mult), reads=rd, writes=[b_ropet])
            P.op("dve", lambda e: e.tensor_tensor(out=rbv[:, :, 0:8], in0=t[0], in1=t[1], op=ALU.subtract),
                 reads=[b_ropet], ow=[brb])
            P.op("dve", lambda e: e.tensor_tensor(out=rbv[:, :, 8:16], in0=t[2], in1=t[3], op=ALU.add),
                 reads=[b_ropet], ow=[brb])
            P.op("act", lambda e: e.copy(out=rbv[:, :, 16:64], in_=pv[:, :, 16:64]), reads=[self.bank[tb]], ow=[brb])
            pst = self.ps[:, tb, :].bitcast(BF16)
            for h in range(4):
                P.op("pe", lambda e, h=h: e.transpose(out=pst[:, h * 128:(h + 1) * 128],
                                                      in_=rb[:, h * 128:(h + 1) * 128], identity=self.ident),
                     reads=[brb, self.bconst], writes=[self.bank[tb]], flag=(h == 3))
            src = pst[:, 0:512].rearrange("p (h n) -> p h n", h=4)
            dst = dstT[:, :, tb * 128:(tb + 1) * 128]
            P.op("act", lambda e: e.mul(out=dst, in_=src, mul=scale), reads=[self.bank[tb]], ow=[bdstT])

        self.tm_steps(win, C_DK, lhs_h, KD, lambda tb: rope_evac(tb, dkTst, b_dkTst, 1.0))

        def st_dk():
            P.dma("sp", self.XDKs.rearrange("(h p) t -> p h t", p=128), dkTst, ds_dk, reads=[b_dkTst], writes=[b_XDKs])
        self.ws.add(None, None, st_dk)
        gather(2, self.XDKs, b_XDKs, self.XDKd, b_XDKd)

        def evac_dv(tb):
            dst = dvst[:, :, tb, :]
            src = self.ps[:, tb, :].rearrange("p (h d) -> p h d", h=4)
            if tb % 2 == 0:
                P.op("act", lambda e: e.copy(out=dst, in_=src), reads=[self.bank[tb]], ow=[b_dvst])
            else:
                P.op("dve", lambda e: e.tensor_copy(out=dst, in_=src), reads=[self.bank[tb]], ow=[b_dvst])
        self.tm_steps(win, C_DV, lhs_h, KD, evac_dv)

        def st_dv():
            P.dma("sp", self.XDVs, dvst.rearrange("p h t d -> p (h t d)"), ds_dv, reads=[b_dvst], writes=[b_XDVs])
        self.ws.add(None, None, st_dv)
        gather(3, self.XDVs, b_XDVs, self.XDVd, b_XDVd)

        def ff_step():
            for tb in range(NTB):
                for k in range(KD):
                    P.op("pe", lambda e, tb=tb, k=k: e.matmul(
                        self.ps[:, 0, tb * 8:(tb + 1) * 8], lhsT=self.hT[:, k, tb * 128:(tb + 1) * 128],
                        rhs=self.wff[:, k, :], start=(k == 0), stop=(k == KD - 1)),
                        reads=[self.bhT, self.bwff], writes=[self.bank[0]], flag=(k == KD - 1 and tb == NTB - 1))
            fb = self.fbias.unsqueeze(1).broadcast_to([128, NTB, 8])
            P.op("dve", lambda e: e.tensor_tensor(out=spst.rearrange("p (t h) -> p t h", h=8),
                                                  in0=self.ps[:, 0, 0:64].rearrange("p (t h) -> p t h", h=8),
                                                  in1=fb, op=ALU.add),
                 reads=[self.bconst, self.bank[0]], writes=[b_spst])
            P.op("act", lambda e: e.activation(out=spst, in_=spst, func=AF.Exp, scale=-1.0), reads=[b_spst], writes=[b_spst])
            P.op("act", lambda e: e.activation(out=spst, in_=spst, func=AF.Ln, bias=self.oneb, scale=1.0),
                 reads=[self.bconst, b_spst], writes=[b_spst])
            P.dma("sp", self.XLs, spst, ds_sp, reads=[b_spst], writes=[b_XLs])
        self.ws.add(None, None, ff_step)
        gather(4, self.XLs, b_XLs, self.XLd, b_XLd)

        def evac_mk(c, pair):
            P.op("act", lambda e: e.copy(out=mkT[:, c, :], in_=self.ps[:, pair, 0:256]),
                 reads=[self.bank[pair]], ow=[b_mkT])
        self.fm_steps(wkv, 0, 4, mhT, b_mhT, evac_mk, ntg=1)
        for kb in range(KD // 2):
            slot = self.next_rb()
            tile = self.sb("wmv", [128, 2, 512], BF16, slot.off)

            def dma_fn(slot=slot, tile=tile, kb=kb):
                P.dma("pool", tile, wkv[:, kb * 2:kb * 2 + 2, 512:1024], slot.ds, writes=[slot.buf])

            def comp_fn(slot=slot, tile=tile, kb=kb):
                for kk in range(2):
                    kl = kb * 2 + kk
                    for blk in range(2):
                        last = (kl == KD - 1)
                        P.op("pe", lambda e, kk=kk, kl=kl, blk=blk, last=last: e.matmul(
                            self.ps[:, blk, :], lhsT=mhT[:, kl, blk * 128:(blk + 1) * 128], rhs=tile[:, kk, :],
                            start=(kl == 0), stop=last),
                            reads=[slot.buf, b_mhT], writes=[self.bank[blk]], flag=last or (kk == 1 and blk == 1))
                if kb == KD // 2 - 1:
                    for blk in range(2):
                        P.op("dve", lambda e, blk=blk: e.tensor_copy(out=mv[:, blk, :], in_=self.ps[:, blk, :]),
                             reads=[self.bank[blk]], ow=[b_mv])
            self.ws.add(slot, dma_fn, comp_fn)

        sc_f = 128.0 ** -0.5
        for grp in range(2):
            def evac_fq(c, pair, grp=grp):
                h = grp * 4 + c
                src = self.ps[:, pair:pair + 2, :].rearrange("p a n -> p (a n)")
                P.op("act", lambda e: e.mul(out=fqT[:, h, :], in_=src, mul=sc_f),
                     reads=[self.bank[pair], self.bank[pair + 1]], ow=[b_fqT[h]])
            self.fm_steps(win, C_FQ + grp * 512, 4, self.hT, self.bhT, evac_fq)

        def evac_mq(c, pair):
            src = self.ps[:, pair:pair + 2, :].rearrange("p a n -> p (a n)")
            P.op("dve", lambda e: e.tensor_scalar(out=mqT[:, c, :], in0=src, scalar1=sc_f, scalar2=None, op0=ALU.mult),
                 reads=[self.bank[pair], self.bank[pair + 1]], ow=[b_mqT[c]])
        self.fm_steps(win, C_MQ, 4, self.hT, self.bhT, evac_mq)
        self.tm_steps(win, C_DQ, lhs_h, KD, lambda tb: rope_evac(tb, dqT, b_dqT, 64.0 ** -0.5))
        self.ws.run()
        self.dump("fqT", fqT, b_fqT, [128, 8, T], BF16)
        self.dump("dqT", dqT, [b_dqT], [128, 4, T], BF16)
        self.dump("mkT", mkT, [b_mkT], [128, 4, 256], BF16)
        self.dump("mv", mv, [b_mv], [128, 2, 512], BF16)

        S0 = self.offSIL
        lamt = self.sb("lamt", [128, 4, 64], F32, S0)
        Lt = self.sb("Lt", [128, 2, 64], F32, S0 + 1024)
        Lpre = self.sb("Lpre", [128, 2, 64], F32, S0 + 1536)
        cmat = self.sb("cmat", [128, 3, 128], F32, S0 + 2048)
        b_S = Buf("silscratch")
        inherit([b_S], self.bSIL)
        ds_l = P.dsem("lam")
        for i, nm in enumerate(("diff_lambda_q1", "diff_lambda_k1", "diff_lambda_q2", "diff_lambda_k2")):
            P.dma("sp", lamt[:, i, :], self.w[nm][0:1, :].partition_broadcast(128)[:, 0, :], ds_l, writes=[b_S])
        P.dma("sp", cmat, self.cmat_in, ds_l, writes=[b_S])
        P.join(ds_l, [b_S])
        lv = self.lamv
        b_lam = Buf("lam")
        inherit([b_lam], [self.bconst])
        P.op("dve", lambda e: e.tensor_tensor(out=lamt[:, 0, :], in0=lamt[:, 0, :], in1=lamt[:, 1, :], op=ALU.mult),
             writes=[b_S])
        P.op("dve", lambda e: e.tensor_tensor(out=lamt[:, 2, :], in0=lamt[:, 2, :], in1=lamt[:, 3, :], op=ALU.mult),
             writes=[b_S])
        P.op("dve", lambda e: e.tensor_reduce(out=lv[:, 0:1], in_=lamt[:, 0, :], axis=AX.X, op=ALU.add),
             reads=[b_S], writes=[b_lam])
        P.op("dve", lambda e: e.tensor_reduce(out=lv[:, 1:2], in_=lamt[:, 2, :], axis=AX.X, op=ALU.add),
             reads=[b_S], writes=[b_lam])
        P.op("act", lambda e: e.activation(out=lv[:, 2:4], in_=lv[:, 0:2], func=AF.Exp), writes=[b_lam])
        P.op("dve", lambda e: e.tensor_tensor(out=lv[:, 4:5], in0=lv[:, 2:3], in1=lv[:, 3:4], op=ALU.subtract),
             writes=[b_lam])
        P.op("dve", lambda e: e.tensor_scalar(out=lv[:, 5:6], in0=lv[:, 4:5], scalar1=-1.0, scalar2=-LAM_INIT,
                                              op0=ALU.mult, op1=ALU.add), writes=[b_lam])
        P.op("dve", lambda e: e.tensor_scalar(out=lv[:, 6:7], in0=lv[:, 6:7], scalar1=1.0 - LAM_INIT, scalar2=None,
                                              op0=ALU.mult), writes=[b_lam])
        neglam = lv[:, 5:6]
        gsc = lv[:, 6:7]

        ds_lt = P.dsem("lt")
        b_Lt = Buf("Lt")
        inherit([b_Lt], [b_S])
        P.dma("sp", Lt, self.XLd.rearrange("(r s) c -> s r c", r=2), ds_lt, reads=[b_XLd], writes=[b_Lt])
        Ltv = Lt.rearrange("p r (j h) -> p r j h", h=8)
        Lpv = Lpre.rearrange("p r (j h) -> p r j h", h=8)
        P.op("dve", lambda e: e.memset(Lpv[:, 0, 0, :], 0.0), writes=[b_S])
        for g in range(1, 16):
            r, j = g % 2, g // 2
            rp, jp = (g - 1) % 2, (g - 1) // 2
            P.op("dve", lambda e, r=r, j=j, rp=rp, jp=jp: e.tensor_tensor(
                out=Lpv[:, r, j, :], in0=Lpv[:, rp, jp, :], in1=Ltv[:, rp, jp, :], op=ALU.add), reads=[b_Lt], writes=[b_S])
        b_D = Buf("Dtab")
        inherit([b_D], [self.bconst])
        P.op("pe", lambda e: e.matmul(self.ps[:, 0, 0:128], lhsT=cmat[:, 0, :], rhs=Lt.rearrange("p r c -> p (r c)"),
                                      start=True, stop=False), reads=[b_S, b_Lt], writes=[self.bank[0]], flag=False)
        P.op("pe", lambda e: e.matmul(self.ps[:, 0, 0:128], lhsT=cmat[:, 1, :], rhs=Lpre.rearrange("p r c -> p (r c)"),
                                      start=False, stop=True), reads=[b_S], writes=[self.bank[0]])
        P.op("dve", lambda e: e.tensor_copy(out=self.Dtab, in_=self.ps[:, 0, 0:128]), reads=[self.bank[0]], writes=[b_D])
        P.op("pe", lambda e: e.matmul(self.ps[:, 1, 0:128], lhsT=cmat[:, 2, :], rhs=self.Dtab, start=True, stop=True),
             reads=[b_S, b_D], writes=[self.bank[1]])
        P.op("dve", lambda e: e.tensor_scalar(out=self.Dend, in0=self.ps[:, 1, 0:64], scalar1=self.psel[:, 0:1],
                                              scalar2=None, op0=ALU.mult),
             reads=[self.bconst, self.bank[1]], writes=[b_D])
        P.op("dve", lambda e: e.scalar_tensor_tensor(out=self.Dend, in0=self.ps[:, 1, 64:128], scalar=self.psel[:, 1:2],
                                                     in1=self.Dend, op0=ALU.mult, op1=ALU.add),
             reads=[self.bconst, self.bank[1], b_D], writes=[b_D])
        in0 = self.Dtab.rearrange("p (g h) -> p g h", h=8).unsqueeze(3).broadcast_to([128, 16, 8, 8])
        in1 = self.Dend.rearrange("p (j h) -> p h j", h=8).unsqueeze(1).broadcast_to([128, 16, 8, 8])
        b_bias = Buf("biasT")
        inherit([b_bias], [self.bconst])
        P.op("dve", lambda e: e.tensor_tensor(out=self.biasT, in0=in0, in1=in1, op=ALU.subtract),
             reads=[b_D], writes=[b_bias])
        bflat = self.biasT.rearrange("p a b c -> p (a b c)")
        P.op("dve", lambda e: e.tensor_scalar_min(out=bflat, in0=bflat, scalar1=0.0), writes=[b_bias])
        self.dump("Dtab", self.Dtab, [b_D], [128, 128], F32)
        self.dump("biasT", bflat, [b_bias], [128, 1024], F32)

        yfoxT = self.sb("yfoxT", [128, 8, T], BF16, A0)
        ydiffT = self.sb("ydiffT", [128, 4, T], BF16, A0 + 16384)
        ymemT = self.sb("ymemT", [128, 4, T], BF16, A0 + 24576)
        KTb = [self.sb("KT%d" % i, [128, 2, T], BF16, A0 + 32768 + i * 4096) for i in range(2)]
        Vb = [self.sb("V%d" % i, [128, 2, NTB, 128], BF16, A0 + 40960 + i * 4096) for i in range(2)]
        PT = [self.sb("PT%d" % i, [128, T], BF16, A0 + 49152 + i * 2048) for i in range(3)]
        recip = self.sb("recip", [128, T], F32, A0 + 55296)
        sq = self.sb("sq", [128, T], BF16, A0 + 59392)
        accS = self.sb("accS", [128, T], F32, A0 + 61440)
        b_yfox = [Buf("yfox%d" % i) for i in range(8)]
        b_ydiff = [Buf("ydiff%d" % i) for i in range(4)]
        b_ymem = [Buf("ymem%d" % i) for i in range(4)]
        b_KT = [Buf("KT0"), Buf("KT1")]; b_V = [Buf("V0"), Buf("V1")]
        b_PT = [[Buf("PT%d_%d" % (i, j)) for j in range(8)] for i in range(3)]
        b_recip = Buf("recip"); b_sq = Buf("sq"); b_accS = Buf("accS")
        b_PTall = b_PT[0] + b_PT[1] + b_PT[2]
        attnA = b_yfox + b_ydiff + b_ymem + b_KT + b_V + b_PTall + [b_recip, b_sq, b_accS]
        inherit(attnA, projA)
        ds_kv = [P.dsem("kv0"), P.dsem("kv1")]
        self.pt_i = 0
        self.sp_i = 0
        accv = self.ps[:, 4:6, :].rearrange("p a n -> p (a n)")
        rsv = self.ps[:, 6:8, :].rearrange("p a n -> p (a n)")

        passes = []
        deferred = []

        def add_pass(nkb, q_ap, q_bufs, k_fn, v_fn, kv_bufs, causal, bias_fn, mask_fn, pre_fn, post_fn):
            passes.append(dict(nkb=nkb, q_ap=q_ap, q_bufs=q_bufs, k_fn=k_fn, v_fn=v_fn, kv_bufs=kv_bufs,
                               causal=causal, bias_fn=bias_fn, mask_fn=mask_fn, pre_fn=pre_fn, post_fn=post_fn))

        def item_geom(p, g):
            jmin = g // 2 if p["causal"] else 0
            c0 = jmin * 128
            parts = []
            if c0 < 512:
                parts.append((0, c0, 512))
            parts.append((1, max(c0, 512), 1024))
            return jmin, c0, parts

        pre_done = set()

        def pre(k):
            if 0 <= k < len(passes) and k not in pre_done:
                pre_done.add(k)
                if passes[k]["pre_fn"] is not None:
                    passes[k]["pre_fn"]()

        def emit_qk(p, g, st):
            jmin, c0, parts = item_geom(p, g)
            sp = self.sp_i % 2; self.sp_i += 1
            pi = self.pt_i % 3; self.pt_i += 1
            st["sp"] = sp; st["pi"] = pi
            k_fn = p["k_fn"]; q_ap = p["q_ap"]
            for ip, (hb_, lo, hi) in enumerate(parts):
                bk = 2 * sp + hb_
                l0 = lo - hb_ * 512
                P.op("pe", lambda e, bk=bk, lo=lo, hi=hi, g=g, l0=l0: e.matmul(
                    self.ps[:, bk, l0:l0 + (hi - lo)],
                    lhsT=k_fn(g), rhs=q_ap[:, lo:hi], start=True, stop=True),
                    reads=list(p["kv_bufs"]) + list(p["q_bufs"]), writes=[self.bank[bk]], flag=(ip == len(parts) - 1))

        def emit_rest(p, g, st):
            jmin, c0, parts = item_geom(p, g)
            sp = st["sp"]; pi = st["pi"]
            nkb = p["nkb"]
            pt = PT[pi]; bpt = b_PT[pi]
            bias_fn = p["bias_fn"]; mask_fn = p["mask_fn"]; v_fn = p["v_fn"]
            sbanks = [self.bank[2 * sp + hb_] for (hb_, _, _) in parts]
            if bias_fn is None:
                sv = self.ps[:, 2 * sp:2 * sp + 2, :].rearrange("p a n -> p (a n)")
                P.op("act", lambda e: e.activation(out=pt[:, c0:T], in_=sv[:, c0:T], func=AF.Exp),
                     reads=sbanks, ow=bpt[jmin:8])
            else:
                for j in range(jmin, 8):
                    bk = 2 * sp + j // 4
                    P.op("act", lambda e, bk=bk, j=j: e.activation(
                        out=pt[:, j * 128:(j + 1) * 128], in_=self.ps[:, bk, (j % 4) * 128:(j % 4 + 1) * 128],
                        func=AF.Exp, bias=bias_fn(g, j), scale=1.0),
                        reads=[b_bias, self.bank[bk]], ow=[bpt[j]])
            if mask_fn is not None:
                P.op("pool", lambda e: e.tensor_tensor(
                    out=pt[:, c0:c0 + 128], in0=pt[:, c0:c0 + 128], in1=mask_fn(g), op=ALU.mult),
                    reads=[self.bconst, bpt[jmin]], writes=[bpt[jmin]])
            for ip, (hb_, lo, hi) in enumerate(parts):
                l0 = lo - hb_ * 512
                P.op("pe", lambda e, hb_=hb_, l0=l0, lo=lo, hi=hi: e.matmul(
                    self.ps[:, 4 + hb_, l0:l0 + (hi - lo)], lhsT=v_fn(g), rhs=pt[:, lo:hi],
                    start=(g == 0), stop=(g == nkb - 1), skip_group_check=True),
                    reads=list(p["kv_bufs"]) + bpt[lo // 128:hi // 128], writes=[self.bank[4 + hb_]], flag=False)
                P.op("pe", lambda e, hb_=hb_, l0=l0, lo=lo, hi=hi: e.matmul(
                    self.ps[:, 6 + hb_, l0:l0 + (hi - lo)], lhsT=self.onesb, rhs=pt[:, lo:hi],
                    start=(g == 0), stop=(g == nkb - 1), skip_group_check=True),
                    reads=bpt[lo // 128:hi // 128] + [self.bconst], writes=[self.bank[6 + hb_]],
                    flag=(ip == len(parts) - 1))
            if g == nkb - 1:
                d = p["post_fn"](sp)
                if d is not None:
                    deferred.append([4, d])
            for dd in list(deferred):
                dd[0] -= 1
                if dd[0] <= 0:
                    deferred.remove(dd)
                    dd[1](sp)

        def run_passes():
            items = [(p, g, {}) for p in passes for g in range(p["nkb"])]
            pre(0); pre(1)
            emit_qk(*items[0])
            for i, it in enumerate(items):
                if i + 1 < len(items):
                    emit_qk(*items[i + 1])
                emit_rest(*it)
                if it[1] == it[0]["nkb"] - 1:
                    pre(passes.index(it[0]) + 2)
            for dd in deferred:
                dd[1](it[2]["sp"])

        def normalize(dst, bdst):
            P.op("dve", lambda e: e.tensor_copy(out=accS, in_=accv), reads=[self.bank[4], self.bank[5]], writes=[b_accS])
            P.op("dve", lambda e: e.tensor_copy(out=recip, in_=rsv), reads=[self.bank[6], self.bank[7]], writes=[b_recip])
            P.op("dve", lambda e: e.reciprocal(out=recip, in_=recip), reads=[b_recip], writes=[b_recip])
            P.op("dve", lambda e: e.tensor_tensor(out=dst, in0=accS, in1=recip, op=ALU.mult),
                 reads=[b_recip, b_accS], ow=[bdst])

        for h in range(4):
            add_pass(2, mqT[:, h, :], [b_mqT[h]],
                     lambda g, h=h: mkT[:, h, g * 128:(g + 1) * 128],
                     lambda g, h=h: mv[:, g, h * 128:(h + 1) * 128],
                     [b_mkT, b_mv], False, None, None, None,
                     lambda sp, h=h: normalize(ymemT[:, h, :], b_ymem[h]))

        XKv = self.XKd.rearrange("(r h d) t -> h d r t", r=2, h=8)
        XVv = self.XVd.rearrange("(r s) (h x) -> h s r x", r=2, h=8)
        kvi = 0
        for h in range(8):
            i = kvi % 2; kvi += 1
            kt = KTb[i]; vv = Vb[i]

            def pre_fox(h=h, i=i):
                P.dma("sp", KTb[i], XKv[h], ds_kv[i], reads=[b_XKd], writes=[b_KT[i]])
                P.dma("sp", Vb[i].rearrange("p r t d -> p r (t d)"), XVv[h], ds_kv[i], reads=[b_XVd], writes=[b_V[i]])
                P.join(ds_kv[i], [b_KT[i], b_V[i]])
            add_pass(16, fqT[:, h, :], [b_fqT[h]],
                     lambda g, kt=kt: kt[:, g % 2, (g // 2) * 128:(g // 2 + 1) * 128],
                     lambda g, vv=vv: vv[:, g % 2, g // 2, :],
                     [b_KT[i], b_V[i]], True,
                     lambda g, j, h=h: self.biasT[:, (g % 2) * 8 + g // 2, h, j:j + 1],
                     lambda g: self.masks[:, g % 2, :], pre_fox,
                     lambda sp, h=h: normalize(yfoxT[:, h, :], b_yfox[h]))

        XDKv = self.XDKd.rearrange("(r h d) t -> h d r t", r=2, h=4)
        XDVv = self.XDVd.rearrange("(r s) (h x) -> h s r x", r=2, h=4)

        def diff_tail(sp, h):
            P.op("dve", lambda e: e.scalar_tensor_tensor(out=Y1, in0=Y2, scalar=neglam, in1=Y1, op0=ALU.mult, op1=ALU.add),
                 reads=[b_lam, b_Y2, b_Y1], writes=[b_Y1])
            P.op("act", lambda e: e.activation(out=sq, in_=Y1, func=AF.Square), reads=[b_Y1], writes=[b_sq])
            for tg in range(2):
                P.op("pe", lambda e, tg=tg: e.matmul(self.ps[:, 2 * sp + tg, :], lhsT=self.onesb,
                                                    rhs=sq[:, tg * 512:(tg + 1) * 512], start=True, stop=True),
                     reads=[b_sq, self.bconst], writes=[self.bank[2 * sp + tg]], flag=(tg == 1))
            ssv = self.ps[:, 2 * sp:2 * sp + 2, :].rearrange("p a n -> p (a n)")
            P.op("act", lambda e: e.activation(out=recip, in_=ssv, func=AF.Sqrt, bias=self.epsb, scale=1.0 / 128.0),
                 reads=[self.bconst, self.bank[2 * sp], self.bank[2 * sp + 1]], writes=[b_recip])
            P.op("dve", lambda e: e.reciprocal(out=recip, in_=recip), reads=[b_recip], writes=[b_recip])
            P.op("dve", lambda e: e.scalar_tensor_tensor(out=ydiffT[:, h, :], in0=Y1, scalar=gsc, in1=recip,
                                                         op0=ALU.mult, op1=ALU.mult),
                 reads=[b_Y1, b_recip, b_lam], ow=[b_ydiff[h]])

        for h in range(4):
            i = kvi % 2; kvi += 1
            kt = KTb[i]; vv = Vb[i]

            def pre_diff(h=h, i=i):
                P.dma("sp", KTb[i], XDKv[h], ds_kv[i], reads=[b_XDKd], writes=[b_KT[i]])
                P.dma("sp", Vb[i].rearrange("p r t d -> p r (t d)"), XDVv[h], ds_kv[i], reads=[b_XDVd], writes=[b_V[i]])
                P.join(ds_kv[i], [b_KT[i], b_V[i]])
            for m in range(2):
                def post_diff(sp, h=h, m=m):
                    if m == 0:
                        normalize(Y1, b_Y1)
                        return None
                    normalize(Y2, b_Y2)
                    return lambda sp2, h=h: diff_tail(sp2, h)
                add_pass(16, dqT[m * 64:(m + 1) * 64, h, :], [b_dqT],
                         lambda g, kt=kt, m=m: kt[m * 64:(m + 1) * 64, g % 2, (g // 2) * 128:(g // 2 + 1) * 128],
                         lambda g, vv=vv: vv[:, g % 2, g // 2, :],
                         [b_KT[i], b_V[i]], True, None,
                         lambda g: self.masks[:, 2 + g % 2, :],
                         pre_diff if m == 0 else None, post_diff)
        run_passes()
        self.dump("ymemT", ymemT, b_ymem, [128, 4, T], BF16)
        self.dump("yfoxT", yfoxT, b_yfox, [128, 8, T], BF16)
        self.dump("ydiffT", ydiffT, b_ydiff, [128, 4, T], BF16)

        mrgT = self.sb("mrgT", [128, KD, T], BF16, B0)
        b_mrg = [Buf("mrg%d" % i) for i in range(KD)]
        self.set_rB(b_mrg + [Buf("Btail")])
        sig = [self.sb("sig%d" % i, [128, T], BF16, A0 + 32768 + i * 2048) for i in range(3)]
        acc = self.sb("macc", [128, T], F32, A0 + 32768 + 6144)
        tmp = [self.sb("mtmp%d" % i, [128, T], F32, A0 + 32768 + 10240 + i * 4096) for i in range(2)]
        b_sig = [Buf("sig%d" % i) for i in range(3)]
        b_acc = Buf("macc"); b_tmp = [Buf("mtmp0"), Buf("mtmp1")]
        inherit(b_sig + [b_acc] + b_tmp, b_KT + b_V + b_PTall + [b_recip, b_sq, b_accS])
        wmg = self.w["w_merge_gate"].rearrange("(k p) (g c) -> p k g c", p=128, g=3)
        wbr = [self.w["w_branch_fox"].rearrange("(k p) n -> p k n", p=128),
               self.w["w_branch_diff"].rearrange("(k p) n -> p k n", p=128),
               self.w["w_branch_mem"].rearrange("(k p) n -> p k n", p=128)]
        ybr = [(yfoxT, b_yfox, 8), (ydiffT, b_ydiff, 4), (ymemT, b_ymem, 4)]
        for m in range(KD):
            slot = self.next_ra()
            gt = self.sb("wmg", [128, KD, 3, 128], BF16, slot.off)
            bt = self.sb("wbr", [128, KD, 128], BF16, slot.off + 12288)

            def dma_fn(slot=slot, gt=gt, bt=bt, m=m):
                for b in range(3):
                    P.dma("pool", gt[:, :, b, :], wmg[:, :, b, m * 128:(m + 1) * 128], slot.ds, writes=[slot.buf])
                P.dma("pool", bt[:, 0:8, :], wbr[0][:, :, m * 128:(m + 1) * 128], slot.ds, writes=[slot.buf])
                P.dma("pool", bt[:, 8:12, :], wbr[1][:, :, m * 128:(m + 1) * 128], slot.ds, writes=[slot.buf])
                P.dma("pool", bt[:, 12:16, :], wbr[2][:, :, m * 128:(m + 1) * 128], slot.ds, writes=[slot.buf])

            def comp_fn(slot=slot, gt=gt, bt=bt, m=m):
                kofs = 0
                for b in range(3):
                    yT, byT, nkb = ybr[b]
                    pg = (self.chunk_ctr % 4) * 2; self.chunk_ctr += 1
                    for k in range(KD):
                        for tg in range(2):
                            P.op("pe", lambda e, k=k, tg=tg, b=b, pg=pg: e.matmul(
                                self.ps[:, pg + tg, :], lhsT=gt[:, k, b, :], rhs=self.hT[:, k, tg * 512:(tg + 1) * 512],
                                start=(k == 0), stop=(k == KD - 1)),
                                reads=[slot.buf, self.bhT], writes=[self.bank[pg + tg]], flag=(k == KD - 1 and tg == 1))
                    gv = self.ps[:, pg:pg + 2, :].rearrange("p a n -> p (a n)")
                    P.op("act", lambda e, b=b, gv=gv: e.activation(out=sig[b], in_=gv, func=AF.Sigmoid,
                                                                 bias=self.bmT[:, b * 16 + m:b * 16 + m + 1], scale=1.0),
                         reads=[self.bconst, self.bank[pg], self.bank[pg + 1]], ow=[b_sig[b]])
                    py = (self.chunk_ctr % 4) * 2; self.chunk_ctr += 1
                    for k in range(nkb):
                        for tg in range(2):
                            P.op("pe", lambda e, k=k, tg=tg, py=py, kofs=kofs, yT=yT, nkb=nkb: e.matmul(
                                self.ps[:, py + tg, :], lhsT=bt[:, kofs + k, :], rhs=yT[:, k, tg * 512:(tg + 1) * 512],
                                start=(k == 0), stop=(k == nkb - 1)),
                                reads=[slot.buf] + list(byT), writes=[self.bank[py + tg]],
                                flag=(k == nkb - 1 and tg == 1))
                    yv = self.ps[:, py:py + 2, :].rearrange("p a n -> p (a n)")
                    ybanks = [self.bank[py], self.bank[py + 1]]
                    if b == 0:
                        P.op("dve", lambda e, yv=yv: e.tensor_tensor(out=acc, in0=yv, in1=sig[0], op=ALU.mult),
                             reads=[b_sig[0]] + ybanks, ow=[b_acc])
                    elif b == 1:
                        P.op("dve", lambda e, yv=yv: e.tensor_tensor(out=tmp[0], in0=yv, in1=sig[1], op=ALU.mult),
                             reads=[b_sig[1]] + ybanks, ow=[b_tmp[0]])
                        P.op("dve", lambda e: e.tensor_tensor(out=acc, in0=acc, in1=tmp[0], op=ALU.add),
                             reads=[b_tmp[0], b_acc], writes=[b_acc])
                    else:
                        P.op("dve", lambda e, yv=yv: e.tensor_tensor(out=tmp[1], in0=yv, in1=sig[2], op=ALU.mult),
                             reads=[b_sig[2]] + ybanks, ow=[b_tmp[1]])
                        P.op("dve", lambda e, m=m: e.tensor_tensor(out=mrgT[:, m, :], in0=acc, in1=tmp[1], op=ALU.add),
                             reads=[b_tmp[1], b_acc], ow=[b_mrg[m]])
                    kofs += nkb

            self.ws.add(slot, dma_fn, comp_fn)
        wov = self.w["w_out"].rearrange("(k p) n -> p k n", p=128)
        self.ws.add(None, None, lambda: inherit(self.bA, attnA + b_sig + [b_acc] + b_tmp))
        for ng in range(4):
            def evac_o(tb, ng=ng):
                ydst = self.A[:, tb, ng * 512:(ng + 1) * 512]
                if tb % 2 == 0:
                    P.op("act", lambda e: e.copy(out=ydst, in_=self.ps[:, tb, :]), reads=[self.bank[tb]], ow=[self.bA[tb]])
                else:
                    P.op("dve", lambda e: e.tensor_copy(out=ydst, in_=self.ps[:, tb, :]),
                         reads=[self.bank[tb]], ow=[self.bA[tb]])
            self.tm_steps(wov, ng * 512, lambda kl, tb: (mrgT[:, kl, tb * 128:(tb + 1) * 128], b_mrg[kl]), KD, evac_o)
        self.ws.run()
        self.dump("mrgT", mrgT, b_mrg, [128, KD, T], BF16)
        inherit(self.bSIL, [b_S])
        self.set_rB(self.bact)


_NC_CACHE = {}


def _get_builder(stage, dbg=()):
    key = (stage, tuple(dbg))
    if key not in _NC_CACHE:
        b = Builder(stage, dbg)
        b.build()
        _NC_CACHE[key] = b
    return _NC_CACHE[key]


def _shard_x(x, c):
    b, p = c // 2, c % 2
    return np.ascontiguousarray(x[b].reshape(8, 2, 128, D)[:, p].reshape(T, D))


def _consts(p):
    s = np.arange(128)[:, None]
    t = np.arange(128)[None, :]
    tri = (s <= t).astype(np.float32)
    ones = np.ones((128, 128), np.float32)
    zeros = np.zeros((128, 128), np.float32)
    sel = np.zeros((128, 128), np.float32); sel[127, :] = 1.0
    cmat = np.stack([tri, ones, sel], axis=1)
    chunk = ((s // 64) <= (t // 64)).astype(np.float32)
    if p == 0:
        masks = [tri, zeros, chunk, zeros]
    else:
        masks = [ones, tri, ones, chunk]
    masks = np.stack(masks, axis=1).astype(ml_dtypes.bfloat16)
    pos = ((2 * np.arange(8)[None, :] + p) * 128 + np.arange(128)[:, None]).astype(np.float32)
    inv = (500000.0 ** (-np.arange(0, 16, 2, dtype=np.float32) / 16)).astype(np.float32)
    ang = pos[:, :, None] * inv[None, None, :]
    rope = np.stack([np.cos(ang), np.sin(ang)], axis=1).astype(np.float32)
    psel = np.zeros((128, 2), np.float32); psel[:, p] = 1.0
    return {"c_ident": np.eye(128, dtype=ml_dtypes.bfloat16), "c_mat": cmat, "c_masks": masks,
            "c_rope": rope, "c_psel": psel}


_W2D = ("ffn1_pre_g", "ffn1_w_gate", "ffn1_w_up", "ffn1_w_down", "ffn1_post_g", "mix_pre_g", "w_in", "fox_f_bias",
        "diff_lambda_q1", "diff_lambda_k1", "diff_lambda_q2", "diff_lambda_k2", "diff_head_g", "mem_norm_g",
        "w_mem_kv", "w_branch_fox", "w_branch_diff", "w_branch_mem", "w_merge_gate", "b_merge_gate", "w_out",
        "mix_post_g", "ffn2_pre_g", "ffn2_w_gate", "ffn2_w_up", "ffn2_w_down", "ffn2_post_g")


def kernel(stage=3, dbg=(), **inputs):
    bld = _get_builder(stage, dbg)
    nc = bld.nc
    x = np.asarray(inputs["x"], dtype=np.float32)
    mem = np.asarray(inputs["mem"], dtype=np.float32)
    shared = {}
    for nm in _W2D:
        a = np.asarray(inputs[nm], dtype=np.float32)[0]
        shared[nm] = np.ascontiguousarray(a.reshape(1, -1) if a.ndim == 1 else a)
    cst = [_consts(0), _consts(1)]
    in_maps = []
    for c in range(8):
        m = dict(shared)
        m.update(cst[c % 2])
        m["x"] = _shard_x(x, c)
        m["mem"] = np.ascontiguousarray(mem[c // 2])
        in_maps.append(m)
    res = run_bass_kernel_spmd(nc, in_maps, core_ids=list(range(8)))
    out = np.empty((4, 2048, D), np.float32)
    for c in range(8):
        b, p = c // 2, c % 2
        out[b].reshape(8, 2, 128, D)[:, p] = res.results[c]["out"].reshape(8, 128, D)
    if dbg:
        kernel.last_dbg = [{k: res.results[c]["dbg_" + k] for k in bld.dbg_out} for c in range(8)]
    return out
```
